# Optimizing a Trainium2 kernel written in Bass

```python
import math
import jax, jax.numpy as jnp
from jax import lax
import numpy as np

D_MODEL = 2048
BATCH = 4
SEQ = 4096
DEPTH = 2

S5_WIDTH = D_MODEL // 2
S5_GROUP = 16
S5_GROUPS = S5_WIDTH // S5_GROUP
S5_STATE = 64
LRU_WIDTH = D_MODEL // 2
LRU_BLOCKS = 16
LRU_BLOCK = LRU_WIDTH // LRU_BLOCKS
CONV_WIDTH = 4
LRU_C = 8.0
HEAD_DIM = 128
N_HEADS = 8
ATTN_WIDTH = N_HEADS * HEAD_DIM
Q_BLOCK = 128
N_BRANCH = 3
D_FF = 4 * D_MODEL
IN_COLS = S5_WIDTH + LRU_WIDTH + 3 * ATTN_WIDTH + N_BRANCH * D_MODEL
EPS = 1e-6

kernel_name = "hybrid_s5_rglru_stickbreak_gated"


def rms_norm(x, g):
    x32 = x.astype(jnp.float32)
    y = x32 * lax.rsqrt(jnp.mean(x32 * x32, axis=-1, keepdims=True) + EPS)
    return (y * g.astype(jnp.float32)).astype(x.dtype)


def linear_scan(a, b):
    def combine(e1, e2):
        a1, b1 = e1
        a2, b2 = e2
        return (a1 * a2, a2 * b1 + b2)
    _, h = lax.associative_scan(combine, (a, b), axis=0)
    return h


def s5_branch(u, lam_re, lam_im, log_dt, b_re, b_im, c_re, c_im, d_skip, w_glu, b_glu):
    bsz, L, _ = u.shape
    u32 = u.astype(jnp.float32)
    dt = jnp.exp(log_dt.astype(jnp.float32))[:, None]
    lam = lax.complex(lam_re.astype(jnp.float32), lam_im.astype(jnp.float32))
    lam_bar = jnp.exp(lam * dt)
    b_mat = lax.complex(b_re.astype(jnp.float32), b_im.astype(jnp.float32))
    b_bar = ((lam_bar - 1.0) / lam)[..., None] * b_mat
    ug = u32.transpose(1, 0, 2).reshape(L, bsz, S5_GROUPS, S5_GROUP)
    bu = lax.complex(jnp.einsum('lbgc,gpc->lbgp', ug, jnp.real(b_bar)),
                     jnp.einsum('lbgc,gpc->lbgp', ug, jnp.imag(b_bar)))
    a = jnp.broadcast_to(lam_bar, (L, 1) + lam_bar.shape)
    h = linear_scan(a, bu)
    y = (jnp.einsum('lbgp,gcp->lbgc', jnp.real(h), c_re.astype(jnp.float32))
         - jnp.einsum('lbgp,gcp->lbgc', jnp.imag(h), c_im.astype(jnp.float32)))
    y = y.reshape(L, bsz, S5_WIDTH).transpose(1, 0, 2) + d_skip.astype(jnp.float32) * u32
    y = jax.nn.gelu(y)
    return y * jax.nn.sigmoid(y @ w_glu.astype(jnp.float32) + b_glu.astype(jnp.float32))


def causal_depthwise_conv(x, w, b):
    K = w.shape[0]
    L = x.shape[1]
    xp = jnp.pad(x, ((0, 0), (K - 1, 0), (0, 0)))
    out = b + xp[:, 0:L] * w[0]
    for kk in range(1, K):
        out = out + xp[:, kk:kk + L] * w[kk]
    return out


def rglru_branch(x, conv_w, conv_b, w_r, b_r, w_i, b_i, lam):
    bsz, L, _ = x.shape
    xc = causal_depthwise_conv(x, conv_w, conv_b).astype(jnp.float32)
    xb = xc.reshape(bsz, L, LRU_BLOCKS, LRU_BLOCK)
    r = jax.nn.sigmoid(jnp.einsum('blnj,njk->blnk', xb, w_r.astype(jnp.float32)) + b_r).reshape(bsz, L, LRU_WIDTH)
    i = jax.nn.sigmoid(jnp.einsum('blnj,njk->blnk', xb, w_i.astype(jnp.float32)) + b_i).reshape(bsz, L, LRU_WIDTH)
    log_a = -LRU_C * r * jax.nn.softplus(-lam.astype(jnp.float32))
    a = jnp.exp(log_a)
    gated = jnp.sqrt(-jnp.expm1(2.0 * log_a)) * (i * xc)
    h = linear_scan(a.transpose(1, 0, 2), gated.transpose(1, 0, 2))
    return h.transpose(1, 0, 2)


def stick_breaking_attention(q, k, v):
    bsz, L, H, Dh = q.shape
    nb = L // Q_BLOCK
    scale = Dh ** -0.5
    q_blocks = q.astype(jnp.float32).reshape(bsz, nb, Q_BLOCK, H, Dh).transpose(1, 0, 3, 2, 4)
    kh = k.astype(jnp.float32).transpose(0, 2, 1, 3)
    vh = v.astype(jnp.float32).transpose(0, 2, 1, 3)
    key_pos = jnp.arange(L)

    def one_block(args):
        qb, bi = args
        q_pos = bi * Q_BLOCK + jnp.arange(Q_BLOCK)
        z = jnp.einsum('bhqd,bhkd->bhqk', qb, kh) * scale
        causal = key_pos[None, :] < q_pos[:, None]
        log_keep = jnp.where(causal, jax.nn.log_sigmoid(-z), 0.0)
        log_stick = lax.cumsum(log_keep, axis=3, reverse=True) - log_keep
        w = jnp.where(causal, jnp.exp(jax.nn.log_sigmoid(z) + log_stick), 0.0)
        return jnp.einsum('bhqk,bhkd->bhqd', w, vh)

    out = lax.map(one_block, (q_blocks, jnp.arange(nb)))
    return out.transpose(1, 0, 3, 2, 4).reshape(bsz, L, H * Dh)


def hybrid_mixer(xn, w_in, b_gate,
                 s5_lam_re, s5_lam_im, s5_log_dt, s5_b_re, s5_b_im, s5_c_re, s5_c_im, s5_d, s5_w_glu, s5_b_glu,
                 lru_conv_w, lru_conv_b, lru_w_r, lru_b_r, lru_w_i, lru_b_i, lru_lambda,
                 w_br_s5, w_br_lru, w_br_attn, w_out):
    bsz, L, _ = xn.shape
    proj = xn @ w_in
    offs = [S5_WIDTH, S5_WIDTH + LRU_WIDTH, S5_WIDTH + LRU_WIDTH + ATTN_WIDTH,
            S5_WIDTH + LRU_WIDTH + 2 * ATTN_WIDTH, S5_WIDTH + LRU_WIDTH + 3 * ATTN_WIDTH]
    u_s5, u_lru, q, k, v, gate_logits = jnp.split(proj, offs, axis=-1)
    gates = jax.nn.sigmoid(gate_logits + b_gate)
    g_s5, g_lru, g_attn = jnp.split(gates, [D_MODEL, 2 * D_MODEL], axis=-1)

    y_s5 = s5_branch(u_s5, s5_lam_re, s5_lam_im, s5_log_dt, s5_b_re, s5_b_im, s5_c_re, s5_c_im,
                     s5_d, s5_w_glu, s5_b_glu)
    y_lru = rglru_branch(u_lru, lru_conv_w, lru_conv_b, lru_w_r, lru_b_r, lru_w_i, lru_b_i, lru_lambda)
    y_attn = stick_breaking_attention(q.reshape(bsz, L, N_HEADS, HEAD_DIM),
                                      k.reshape(bsz, L, N_HEADS, HEAD_DIM),
                                      v.reshape(bsz, L, N_HEADS, HEAD_DIM))
    merged = (g_s5 * (y_s5.astype(xn.dtype) @ w_br_s5)
              + g_lru * (y_lru.astype(xn.dtype) @ w_br_lru)
              + g_attn * (y_attn.astype(xn.dtype) @ w_br_attn))
    return merged @ w_out


def squared_relu_mlp(x, w_up, w_down):
    hdn = jax.nn.relu(x @ w_up)
    return (hdn * hdn) @ w_down


def setup_inputs(seed: int = 0) -> dict:
    key = jax.random.key(seed)
    ks = jax.random.split(key, 32)
    f32 = jnp.float32

    def nrm(k, shape, scale):
        return jax.random.normal(k, shape, f32) * scale

    x = jax.random.normal(ks[0], (BATCH, SEQ, D_MODEL), f32)
    norm_mix_g = 1.0 + nrm(ks[1], (DEPTH, D_MODEL), 0.01)
    w_in = nrm(ks[2], (DEPTH, D_MODEL, IN_COLS), D_MODEL ** -0.5)
    b_gate = nrm(ks[3], (DEPTH, N_BRANCH * D_MODEL), 0.01)
    n_idx = jnp.arange(S5_STATE, dtype=f32)
    s5_lam_re = -0.5 + nrm(ks[4], (DEPTH, S5_GROUPS, S5_STATE), 0.01)
    s5_lam_im = math.pi * n_idx[None, None, :] + nrm(ks[5], (DEPTH, S5_GROUPS, S5_STATE), 0.01)
    s5_log_dt = jax.random.uniform(ks[6], (DEPTH, S5_GROUPS), f32, math.log(1e-3), math.log(1e-1))
    s5_b_re = nrm(ks[7], (DEPTH, S5_GROUPS, S5_STATE, S5_GROUP), (2 * S5_GROUP) ** -0.5)
    s5_b_im = nrm(ks[8], (DEPTH, S5_GROUPS, S5_STATE, S5_GROUP), (2 * S5_GROUP) ** -0.5)
    s5_c_re = nrm(ks[9], (DEPTH, S5_GROUPS, S5_GROUP, S5_STATE), S5_STATE ** -0.5)
    s5_c_im = nrm(ks[10], (DEPTH, S5_GROUPS, S5_GROUP, S5_STATE), S5_STATE ** -0.5)
    s5_d = nrm(ks[11], (DEPTH, S5_WIDTH), 1.0)
    s5_w_glu = nrm(ks[12], (DEPTH, S5_WIDTH, S5_WIDTH), S5_WIDTH ** -0.5)
    s5_b_glu = nrm(ks[13], (DEPTH, S5_WIDTH), 0.01)
    lru_conv_w = nrm(ks[14], (DEPTH, CONV_WIDTH, LRU_WIDTH), CONV_WIDTH ** -0.5)
    lru_conv_b = nrm(ks[15], (DEPTH, LRU_WIDTH), 0.01)
    lru_w_r = nrm(ks[16], (DEPTH, LRU_BLOCKS, LRU_BLOCK, LRU_BLOCK), LRU_BLOCK ** -0.5)
    lru_b_r = nrm(ks[17], (DEPTH, LRU_BLOCKS, LRU_BLOCK), 0.01)
    lru_w_i = nrm(ks[18], (DEPTH, LRU_BLOCKS, LRU_BLOCK, LRU_BLOCK), LRU_BLOCK ** -0.5)
    lru_b_i = nrm(ks[19], (DEPTH, LRU_BLOCKS, LRU_BLOCK), 0.01)
    a_pow = jax.random.uniform(ks[20], (DEPTH, LRU_WIDTH), f32, 0.9, 0.999)
    a0 = a_pow ** (1.0 / LRU_C)
    lru_lambda = jnp.log(a0) - jnp.log1p(-a0)
    w_br_s5 = nrm(ks[21], (DEPTH, S5_WIDTH, D_MODEL), S5_WIDTH ** -0.5)
    w_br_lru = nrm(ks[22], (DEPTH, LRU_WIDTH, D_MODEL), LRU_WIDTH ** -0.5)
    w_br_attn = nrm(ks[23], (DEPTH, ATTN_WIDTH, D_MODEL), ATTN_WIDTH ** -0.5)
    w_out = nrm(ks[24], (DEPTH, D_MODEL, D_MODEL), D_MODEL ** -0.5)
    norm_mlp_g = 1.0 + nrm(ks[25], (DEPTH, D_MODEL), 0.01)
    w_up = nrm(ks[26], (DEPTH, D_MODEL, D_FF), D_MODEL ** -0.5)
    w_down = nrm(ks[27], (DEPTH, D_FF, D_MODEL), D_FF ** -0.5)
    final_norm_g = 1.0 + nrm(ks[28], (D_MODEL,), 0.01)
    return {"x": x, "norm_mix_g": norm_mix_g, "w_in": w_in, "b_gate": b_gate,
            "s5_lam_re": s5_lam_re, "s5_lam_im": s5_lam_im, "s5_log_dt": s5_log_dt,
            "s5_b_re": s5_b_re, "s5_b_im": s5_b_im, "s5_c_re": s5_c_re, "s5_c_im": s5_c_im,
            "s5_d": s5_d, "s5_w_glu": s5_w_glu, "s5_b_glu": s5_b_glu,
            "lru_conv_w": lru_conv_w, "lru_conv_b": lru_conv_b, "lru_w_r": lru_w_r, "lru_b_r": lru_b_r,
            "lru_w_i": lru_w_i, "lru_b_i": lru_b_i, "lru_lambda": lru_lambda,
            "w_br_s5": w_br_s5, "w_br_lru": w_br_lru, "w_br_attn": w_br_attn, "w_out": w_out,
            "norm_mlp_g": norm_mlp_g, "w_up": w_up, "w_down": w_down, "final_norm_g": final_norm_g}


def reference(x, norm_mix_g, w_in, b_gate,
              s5_lam_re, s5_lam_im, s5_log_dt, s5_b_re, s5_b_im, s5_c_re, s5_c_im,
              s5_d, s5_w_glu, s5_b_glu,
              lru_conv_w, lru_conv_b, lru_w_r, lru_b_r, lru_w_i, lru_b_i, lru_lambda,
              w_br_s5, w_br_lru, w_br_attn, w_out,
              norm_mlp_g, w_up, w_down, final_norm_g):
    h = x
    for l in range(DEPTH):
        xn = rms_norm(h, norm_mix_g[l])
        mixed = hybrid_mixer(xn, w_in[l], b_gate[l],
                             s5_lam_re[l], s5_lam_im[l], s5_log_dt[l], s5_b_re[l], s5_b_im[l],
                             s5_c_re[l], s5_c_im[l], s5_d[l], s5_w_glu[l], s5_b_glu[l],
                             lru_conv_w[l], lru_conv_b[l], lru_w_r[l], lru_b_r[l], lru_w_i[l],
                             lru_b_i[l], lru_lambda[l],
                             w_br_s5[l], w_br_lru[l], w_br_attn[l], w_out[l])
        h = h + mixed.astype(h.dtype)
        hn = rms_norm(h, norm_mlp_g[l])
        h = h + squared_relu_mlp(hn, w_up[l], w_down[l]).astype(h.dtype)
    return rms_norm(h, final_norm_g)
```

```python
import numpy as np
import ml_dtypes
import concourse.bass as bass
import concourse.mybir as mybir
from concourse.bass_utils import run_bass_kernel_spmd

F32 = mybir.dt.float32
BF16 = mybir.dt.bfloat16
AF = mybir.ActivationFunctionType
ALU = mybir.AluOpType

D = 2048
L = 4096
NBATCH = 4
TOK = 2048
DFF = 8192
EPS = 1e-6
NCORES = 8
QSCALE = 128.0 ** -0.5


class _Op:
    __slots__ = ("eng", "fn", "deps", "idx", "sig", "stream", "ndma", "slot", "cnt", "prev_cnt")


class Sched:
    ENG_NAMES = ("pe", "act", "dve", "pool", "sp")

    def __init__(self, nc, dma_k=4):
        self.nc = nc
        self.ops = []
        self.lastw = {}
        self.readers = {}
        self.dma_k = dma_k
        self.streams = {}
        self.barrier_idx = None

    def op(self, eng, fn, reads=(), writes=(), stream=None, ndma=1):
        o = _Op()
        o.eng = eng
        o.fn = fn
        o.idx = len(self.ops)
        o.stream = stream
        o.ndma = ndma
        o.sig = False
        o.cnt = 0
        o.slot = 0
        o.prev_cnt = 0
        deps = set()
        fresh = False
        for r in reads:
            w = self.lastw.get(r)
            if w is not None:
                deps.add(w)
            else:
                fresh = True
        for w_ in writes:
            w = self.lastw.get(w_)
            if w is not None:
                deps.add(w)
            else:
                fresh = True
            rd = self.readers.get(w_)
            if rd:
                deps.update(rd.values())
        if fresh and self.barrier_idx is not None:
            deps.add(self.barrier_idx)
        deps.discard(o.idx)
        o.deps = deps
        rk = eng if stream is None else ("dma", o.idx)
        for r in reads:
            self.readers.setdefault(r, {})[rk] = o.idx
        for w_ in writes:
            self.lastw[w_] = o.idx
            self.readers[w_] = {}
        self.ops.append(o)
        if stream is not None:
            self.streams.setdefault(stream, []).append(o.idx)
        return o

    def barrier(self):
        keys = list(self.lastw.keys())
        o = self.op("sp", lambda e: e.nop(), reads=keys, writes=keys)
        self.lastw = {}
        self.readers = {}
        self.barrier_idx = o.idx
        return o

    def emit(self):
        nc = self.nc
        ops = self.ops
        eseq = {e: 0 for e in self.ENG_NAMES}
        seq = {}
        for o in ops:
            if o.stream is None:
                seq[o.idx] = eseq[o.eng]
                eseq[o.eng] += 1
        for o in ops:
            nd = set()
            for d in o.deps:
                do = ops[d]
                if do.stream is None and o.stream is None and do.eng == o.eng:
                    if o.eng == "pe" or seq[o.idx] - seq[d] >= 2:
                        continue
                nd.add(d)
            o.deps = nd
            for d in nd:
                ops[d].sig = True
        esem = {e: nc.alloc_semaphore("s_" + e) for e in self.ENG_NAMES}
        ssem = {}
        for s in self.streams:
            ssem[s] = [nc.alloc_semaphore("d_%s_%d" % (s, k)) for k in range(self.dma_k)]
        ecnt = {e: 0 for e in self.ENG_NAMES}
        scnt = {s: [0] * self.dma_k for s in self.streams}
        spos = {s: 0 for s in self.streams}
        for o in ops:
            if o.stream is None:
                if o.sig:
                    ecnt[o.eng] += 1
                    o.cnt = ecnt[o.eng]
            else:
                j = spos[o.stream]
                spos[o.stream] += 1
                o.slot = j % self.dma_k
                o.prev_cnt = scnt[o.stream][o.slot]
                scnt[o.stream][o.slot] += 16 * o.ndma
                o.cnt = scnt[o.stream][o.slot]
        per_eng = {e: [] for e in self.ENG_NAMES}
        for o in ops:
            per_eng[o.eng].append(o)
        final_streams = {e: set() for e in self.ENG_NAMES}
        for o in ops:
            if o.stream is not None:
                final_streams[o.eng].add(o.stream)
        self.stats = {e: len(per_eng[e]) for e in self.ENG_NAMES}
        self.stats["sem_max"] = dict(ecnt)

        def run_engine(ename, eng):
            waited = {}
            for o in per_eng[ename]:
                need = {}
                for d in o.deps:
                    do = ops[d]
                    if do.stream is None:
                        k = ("e", do.eng)
                        sem = esem[do.eng]
                    else:
                        k = ("s", do.stream, do.slot)
                        sem = ssem[do.stream][do.slot]
                    if need.get(k, (None, 0))[1] < do.cnt:
                        need[k] = (sem, do.cnt)
                if o.stream is not None:
                    k = ("s", o.stream, o.slot)
                    if o.prev_cnt > 0 and need.get(k, (None, 0))[1] < o.prev_cnt:
                        need[k] = (ssem[o.stream][o.slot], o.prev_cnt)
                for k, (sem, val) in need.items():
                    if val > 0 and waited.get(k, 0) < val:
                        waited[k] = val
                        eng.wait_ge(sem, val)
                if o.stream is not None:
                    insts = o.fn(eng)
                    if not isinstance(insts, (list, tuple)):
                        insts = [insts]
                    assert len(insts) == o.ndma, (len(insts), o.ndma)
                    for ins in insts:
                        ins.then_inc(ssem[o.stream][o.slot], 16)
                else:
                    ins = o.fn(eng)
                    if o.sig:
                        ins.then_inc(esem[ename], 1)
            for s in sorted(final_streams[ename]):
                for k in range(self.dma_k):
                    key = ("s", s, k)
                    if scnt[s][k] > 0 and waited.get(key, 0) < scnt[s][k]:
                        waited[key] = scnt[s][k]
                        eng.wait_ge(ssem[s][k], scnt[s][k])

        with nc.Block() as block:
            @block.tensor
            def _(e):
                run_engine("pe", e)

            @block.scalar
            def _(e):
                run_engine("act", e)

            @block.vector
            def _(e):
                run_engine("dve", e)

            @block.gpsimd
            def _(e):
                run_engine("pool", e)

            @block.sync
            def _(e):
                run_engine("sp", e)


class Rot:
    def __init__(self, items):
        self.items = list(items)
        self.i = 0

    def next(self):
        it = self.items[self.i % len(self.items)]
        self.i += 1
        return it


class KB:
    def __init__(self):
        if KB.NUM_DEVICES is None:
            self.nc = bass.Bass("TRN2", target_bir_lowering=False)
        else:
            self.nc = bass.Bass("TRN2", target_bir_lowering=False, num_devices=KB.NUM_DEVICES)
        self.S = Sched(self.nc)
        self.ps = [(self.nc.alloc_psum_tensor("ps%d" % i, [128, 512], F32), "ps%d" % i) for i in range(8)]
        self.psrot = Rot(self.ps)
        self._n = 0
        self.consts = {}
        self.ones_f32()
        self.one_col()
        self.const_col("c_ln2", float(np.log(2.0)))

    ARENA_WORDS = 52000
    NUM_DEVICES = None

    def sb(self, name, shape, dt):
        if not hasattr(self, "arena"):
            self.arena = self.nc.alloc_sbuf_tensor("arena", [128, self.ARENA_WORDS], F32)
            self.aoff = 0
        assert shape[0] == 128
        elems = int(np.prod(shape[1:]))
        words = (elems * (4 if dt == F32 else 2) + 3) // 4
        words = (words + 7) // 8 * 8
        assert self.aoff + words <= self.ARENA_WORDS, ("SBUF arena overflow", name, self.aoff, words)
        v = self.arena[:, self.aoff:self.aoff + words]
        self.aoff += words
        if dt != F32:
            v = v.bitcast(dt)
        v = v[:, 0:elems]
        if len(shape) > 2:
            names = "abcd"[:len(shape) - 1]
            pat = "p (%s) -> p %s" % (" ".join(names), " ".join(names))
            v = v.rearrange(pat, **{n: int(sz) for n, sz in zip(names[:-1], shape[1:-1])})
        return v

    def mark(self):
        return getattr(self, "aoff", 0)

    def release(self, mark):
        self.aoff = mark

    def din(self, name, shape, dt):
        return self.nc.dram_tensor(name, list(shape), dt, kind="ExternalInput").ap()

    def dout(self, name, shape, dt):
        return self.nc.dram_tensor(name, list(shape), dt, kind="ExternalOutput").ap()

    def dint(self, name, shape, dt):
        return self.nc.dram_tensor(name, list(shape), dt, kind="Internal").ap()

    def const_col(self, key, val):
        if key not in self.consts:
            t = self.sb(key, [128, 8], F32)
            self.S.op("pool", lambda e: e.memset(t[:], val), writes=[key])
            self.consts[key] = t
        return self.consts[key][:, 0:1]

    def one_col(self):
        if "one_col" not in self.consts:
            t = self.sb("c_onecol", [128, 1], F32)
            self.S.op("pool", lambda e: e.memset(t[:], 1.0), writes=["c_onecol"])
            self.consts["one_col"] = t
        return self.consts["one_col"][:, 0:1]

    def ones_f32(self):
        if "ones" not in self.consts:
            t = self.sb("c_ones", [128, 128], F32)
            self.S.op("pool", lambda e: e.memset(t[:], 1.0), writes=["c_ones"])
            self.consts["ones"] = t
        return self.consts["ones"]


def load_cast(Kb, stream, out_ap, w_ap, kt_n, wkey, extra_reads=()):
    S = Kb.S
    pieces = []
    step = 4
    for k0 in range(0, kt_n, step):
        k1 = min(kt_n, k0 + step)
        pieces.append((k0, k1))

    def fn(e):
        r = []
        for (k0, k1) in pieces:
            r.append(e.dma_start(out=out_ap[:, k0:k1, :],
                                 in_=w_ap[k0 * 128:k1 * 128, :].rearrange("(kt p) c -> p kt c", p=128)))
        return r

    return S.op("pool", fn, reads=list(extra_reads), writes=[wkey], stream=stream, ndma=len(pieces))


class WStream:
    def __init__(self, Kb, name, nbuf, kt, cols):
        self.Kb = Kb
        self.name = name
        self.bufs = [(Kb.sb("%s_%d" % (name, i), [128, kt, cols], BF16), "%s_%d" % (name, i)) for i in range(nbuf)]
        self.rot = Rot(self.bufs)

    def load(self, w_ap, kt_n, ncols):
        buf, key = self.rot.next()
        load_cast(self.Kb, self.name, buf[:, 0:kt_n, 0:ncols], w_ap, kt_n, key)
        return buf, key


def mm_group(S, ps_ap, ps_key, pairs, first=True, last=True):
    n = len(pairs)
    for i, (l, r, keys) in enumerate(pairs):
        S.op("pe", lambda e, l=l, r=r, st=(first and i == 0), sp=(last and i == n - 1):
             e.matmul(ps_ap, lhsT=l, rhs=r, start=st, stop=sp), reads=keys, writes=[ps_key])


def make_nrm(Kb, tag):
    S = Kb.S
    c = dict(
        sq=Rot([(Kb.sb("n%s_sq%d" % (tag, i), [128, 512], F32), "n%s_sq%d" % (tag, i)) for i in range(3)]),
        ln=(Kb.sb("n%s_ln" % tag, [128, 512], F32), "n%s_ln" % tag),
        rstd=(Kb.sb("n%s_rstd" % tag, [128, 512], F32), "n%s_rstd" % tag),
        eps=(Kb.sb("n%s_eps" % tag, [128, 8], F32), "n%s_eps" % tag),
    )
    et = c["eps"][0]
    S.op("pool", lambda e: e.memset(et[:], EPS), writes=[c["eps"][1]])
    return c


def rmsnorm_tile(Kb, hb, hkey, g_sb, gkey, out_fn, out_keys_fn, c):
    S = Kb.S
    ones = Kb.ones_f32()
    ps, pkey = Kb.psrot.next()
    for ft in range(16):
        s, skey = c["sq"].next()
        S.op("act", lambda e, s=s, ft=ft: e.activation(out=s[:], in_=hb[:, ft, :], func=AF.Square),
             reads=[hkey], writes=[skey])
        S.op("pe", lambda e, s=s, ft=ft: e.matmul(ps[:], lhsT=ones[:], rhs=s[:], start=(ft == 0), stop=(ft == 15)),
             reads=[skey, "c_ones"], writes=[pkey])
    ln, lkey = c["ln"]
    rstd, rkey = c["rstd"]
    et = c["eps"][0]
    S.op("act", lambda e: e.activation(out=ln[:], in_=ps[:], func=AF.Ln, bias=et[:, 0:1], scale=1.0 / D),
         reads=[pkey, c["eps"][1]], writes=[lkey])
    S.op("act", lambda e: e.activation(out=rstd[:], in_=ln[:], func=AF.Exp, scale=-0.5),
         reads=[lkey], writes=[rkey])
    for ft in range(16):
        S.op("dve", lambda e, ft=ft: e.scalar_tensor_tensor(out=out_fn(ft), in0=hb[:, ft, :], scalar=g_sb[:, ft:ft + 1],
                                                            in1=rstd[:], op0=ALU.mult, op1=ALU.mult),
             reads=[hkey, rkey, gkey], writes=out_keys_fn(ft))


def stage_A(Kb, io, h_dep_keys=()):
    S = Kb.S
    xn = Kb.sb("A_xn", [128, 16, TOK], BF16)
    hbuf = [(Kb.sb("A_h%d" % i, [128, 16, 512], F32), "A_h%d" % i) for i in range(1)]
    g_sb = Kb.sb("A_g", [128, 16], F32)
    nrmA = make_nrm(Kb, "A")
    S.op("sp", lambda e: e.dma_start(out=g_sb[:], in_=io["g1"]), writes=["A_g"], stream="ld")
    hT = io["hT"]
    for tt in range(4):
        hb, hkey = hbuf[0]

        def ld(e, hb=hb, tt=tt):
            r = []
            for f0 in range(0, 16, 4):
                r.append(e.dma_start(out=hb[:, f0:f0 + 4, :],
                                     in_=hT[f0 * 128:(f0 + 4) * 128, tt * 512:(tt + 1) * 512].rearrange("(ft p) t -> p ft t", p=128)))
            return r
        S.op("sp", ld, reads=list(h_dep_keys), writes=[hkey], stream="ld", ndma=4)
        rmsnorm_tile(Kb, hb, hkey, g_sb, "A_g",
                     lambda ft, tt=tt: xn[:, ft, tt * 512:(tt + 1) * 512],
                     lambda ft, tt=tt: [("A_xn", tt)], nrmA)
    xkeys = [("A_xn", tt) for tt in range(4)]

    def st_xn(e):
        r = []
        for f0 in range(0, 16, 4):
            r.append(e.dma_start(out=io["xnT"][f0 * 128:(f0 + 4) * 128, :].rearrange("(ft p) t -> p ft t", p=128),
                                 in_=xn[:, f0:f0 + 4, :]))
        return r
    S.op("sp", st_xn, reads=xkeys, writes=["d_xnT"], stream="st", ndma=4)

    ws = WStream(Kb, "wA", 3, 16, 512)
    stF = Rot([(Kb.sb("A_stF%d" % i, [128, TOK], F32), "A_stF%d" % i) for i in range(2)])
    stB = Rot([(Kb.sb("A_stB%d" % i, [128, TOK], BF16), "A_stB%d" % i) for i in range(2)])
    w5 = io["w_in5"]
    ev = 0
    for s in range(8):
        wb, wkey = ws.load(w5[:, s * 512:(s + 1) * 512], 16, 512)
        for cbl in range(4):
            cb = s * 4 + cbl
            if cb < 16:
                st, stkey = stF.next()
            else:
                st, stkey = stB.next()
            scale = QSCALE if 16 <= cb < 24 else 1.0
            for tt in range(4):
                ps, pkey = Kb.psrot.next()
                mm_group(S, ps[:], pkey, [(wb[:, kt, cbl * 128:(cbl + 1) * 128], xn[:, kt, tt * 512:(tt + 1) * 512],
                                           [wkey, ("A_xn", tt)]) for kt in range(16)])
                o_ap = st[:, tt * 512:(tt + 1) * 512]
                if ev % 2 == 0:
                    S.op("act", lambda e, o=o_ap, ps=ps, sc=scale: e.activation(out=o, in_=ps[:], func=AF.Copy, scale=sc),
                         reads=[pkey], writes=[stkey])
                else:
                    S.op("dve", lambda e, o=o_ap, ps=ps, sc=scale: e.tensor_scalar(out=o, in0=ps[:], scalar1=sc, scalar2=None, op0=ALU.mult),
                         reads=[pkey], writes=[stkey])
                ev += 1
            if cb < 8:
                dst = io["uS5T"][cb * 128:(cb + 1) * 128, :]
            elif cb < 16:
                dst = io["uLruT"][(cb - 8) * 128:(cb - 7) * 128, :]
            elif cb < 24:
                dst = io["qT"][(cb - 16) * 128:(cb - 15) * 128, :]
            else:
                dst = io["kT"][(cb - 24) * 128:(cb - 23) * 128, :]
            S.op("sp", lambda e, dst=dst, st=st: e.dma_start(out=dst, in_=st[:]), reads=[stkey], writes=[("d_A", cb)], stream="st")
    vs = []
    for s in range(2):
        vs.append(ws.load(w5[:, 4096 + s * 512:4096 + (s + 1) * 512], 16, 512))
    stV = Rot([(Kb.sb("A_stV%d" % i, [128, 1024], BF16), "A_stV%d" % i) for i in range(2)])
    for tb in range(16):
        st, stkey = stV.next()
        for s in range(2):
            wb, wkey = vs[s]
            ps, pkey = Kb.psrot.next()
            mm_group(S, ps[:], pkey, [(xn[:, kt, tb * 128:(tb + 1) * 128], wb[:, kt, :], [wkey, ("A_xn", tb // 4)])
                                      for kt in range(16)])
            o_ap = st[:, s * 512:(s + 1) * 512]
            if ev % 2 == 0:
                S.op("act", lambda e, o=o_ap, ps=ps: e.activation(out=o, in_=ps[:], func=AF.Copy), reads=[pkey], writes=[stkey])
            else:
                S.op("dve", lambda e, o=o_ap, ps=ps: e.tensor_copy(out=o, in_=ps[:]), reads=[pkey], writes=[stkey])
            ev += 1
        S.op("sp", lambda e, st=st, tb=tb: e.dma_start(out=io["v"][tb * 128:(tb + 1) * 128, :], in_=st[:]),
             reads=[stkey], writes=[("d_v", tb)], stream="st")


def build_A():
    Kb = KB()
    io = dict(
        hT=Kb.din("hT", [D, TOK], F32), g1=Kb.din("g1", [128, 16], F32), w_in5=Kb.din("w_in5", [D, 5120], F32),
        xnT=Kb.dout("xnT", [D, TOK], BF16), uS5T=Kb.dout("uS5T", [1024, TOK], F32), uLruT=Kb.dout("uLruT", [1024, TOK], F32),
        qT=Kb.dout("qT", [1024, TOK], BF16), kT=Kb.dout("kT", [1024, TOK], BF16), v=Kb.dout("v", [TOK, 1024], BF16),
    )
    stage_A(Kb, io)
    Kb.S.emit()
    return Kb


TC = 256
NCH = L // TC
PI = float(np.pi)


def _ts(S, eng, out, in0, s1, s2, op0, op1, reads, writes):
    if s2 is None:
        S.op(eng, lambda e: e.tensor_scalar(out=out, in0=in0, scalar1=s1, scalar2=None, op0=op0), reads=reads, writes=writes)
    else:
        S.op(eng, lambda e: e.tensor_scalar(out=out, in0=in0, scalar1=s1, scalar2=s2, op0=op0, op1=op1), reads=reads, writes=writes)


def _tt(S, eng, out, in0, in1, op, reads, writes):
    S.op(eng, lambda e: e.tensor_tensor(out=out, in0=in0, in1=in1, op=op), reads=reads, writes=writes)


def _stt(S, eng, out, in0, scalar, in1, op0, op1, reads, writes):
    S.op(eng, lambda e: e.scalar_tensor_tensor(out=out, in0=in0, scalar=scalar, in1=in1, op0=op0, op1=op1),
         reads=reads, writes=writes)


def _act(S, out, in_, func, reads, writes, scale=1.0, bias=None):
    if bias is None:
        S.op("act", lambda e: e.activation(out=out, in_=in_, func=func, scale=scale), reads=reads, writes=writes)
    else:
        S.op("act", lambda e: e.activation(out=out, in_=in_, func=func, scale=scale, bias=bias), reads=reads, writes=writes)


MAGIC = 12582912.0


def _sincos(S, eng, ang, cs, sn, tmp, key_ang, key_cs, key_sn, key_tmp):
    _ts(S, eng, tmp, ang, 1.0 / (2 * PI), MAGIC, ALU.mult, ALU.add, [key_ang], [key_tmp])
    _ts(S, eng, tmp, tmp, -MAGIC, -2 * PI, ALU.add, ALU.mult, [key_tmp], [key_tmp])
    _tt(S, eng, tmp, tmp, ang, ALU.add, [key_tmp, key_ang], [key_tmp])
    _act(S, sn, tmp, AF.Sin, [key_tmp], [key_sn])
    _ts(S, eng, cs, ang, 0.5 * PI, None, ALU.add, None, [key_ang], [key_cs])
    _ts(S, eng, tmp, cs, 1.0 / (2 * PI), MAGIC, ALU.mult, ALU.add, [key_cs, key_sn], [key_tmp])
    _ts(S, eng, tmp, tmp, -MAGIC, -2 * PI, ALU.add, ALU.mult, [key_tmp], [key_tmp])
    _tt(S, eng, tmp, tmp, cs, ALU.add, [key_tmp, key_cs], [key_tmp])
    _act(S, cs, tmp, AF.Sin, [key_tmp], [key_cs])


def _horner(S, eng, out, z, coeffs, tmpkey, zkey, okey):
    _ts(S, eng, out, z, float(coeffs[0]), float(coeffs[1]), ALU.mult, ALU.add, [zkey], [okey])
    for c in coeffs[2:]:
        _tt(S, eng, out, out, z, ALU.mult, [okey, zkey], [okey])
        _ts(S, eng, out, out, float(c), None, ALU.add, None, [okey], [okey])


def _sincos_acc(S, eng, ang, cs, sn, t0, t1, t2, key):
    _ts(S, eng, t0, ang, 1.0 / (2 * PI), MAGIC, ALU.mult, ALU.add, [key], [key])
    _ts(S, eng, t0, t0, -MAGIC, -2 * PI, ALU.add, ALU.mult, [key], [key])
    _tt(S, eng, t0, t0, ang, ALU.add, [key], [key])
    _ts(S, eng, t0, t0, 0.125, None, ALU.mult, None, [key], [key])
    _tt(S, eng, t1, t0, t0, ALU.mult, [key], [key])
    _horner(S, eng, sn, t1, [1.0 / 362880, -1.0 / 5040, 1.0 / 120, -1.0 / 6, 1.0], key, key, key)
    _tt(S, eng, sn, sn, t0, ALU.mult, [key], [key])
    _horner(S, eng, cs, t1, [-1.0 / 3628800, 1.0 / 40320, -1.0 / 720, 1.0 / 24, -0.5, 1.0], key, key, key)
    for _ in range(3):
        _tt(S, eng, t0, cs, sn, ALU.mult, [key], [key])
        _tt(S, eng, t1, sn, sn, ALU.mult, [key], [key])
        _tt(S, eng, t2, cs, cs, ALU.mult, [key], [key])
        _tt(S, eng, cs, t2, t1, ALU.subtract, [key], [key])
        _ts(S, eng, sn, t0, 2.0, None, ALU.mult, None, [key], [key])


def s5_mixer(Kb, io, psb2, psy, pset):
    S = Kb.S
    sb = Kb.sb
    cw = sb("s5_cw", [128, 24, 16], F32)
    Ec = sb("s5_Ec", [128, 16, TC], F32)
    Es = sb("s5_Es", [128, 16, TC], F32)
    Pw = sb("s5_P", [128, 4, 16], F32)
    Bb = sb("s5_Bb", [128, 2, 16, 128], BF16)
    Cb = sb("s5_Cb", [128, 2, 16, 128], BF16)
    dcol = sb("s5_d", [128, 4], F32)
    G = sb("s5_G", [128, 2, 16], F32)
    gl_t = sb("s5_gl", [128, 2, 4], F32)
    yield "persist"
    col = sb("s5_col", [128, 3, 16], F32)
    tb = sb("s5_tb", [128, 2, 16, TC // 2], F32)
    Bz = sb("s5_Bz", [128, 2, 16, 128], F32)
    tB = sb("s5_tB", [128, 2, 128], F32)
    Cz = sb("s5_Cz", [128, 2, 16, 128], F32)
    ident = sb("s5_id", [128, 128], F32)
    dg = sb("s5_dg", [128, 2, 128], F32)
    ones = Kb.ones_f32()
    S.op("pool", lambda e: e.memset(ident[:], 1.0), writes=["s5_id"])
    S.op("pool", lambda e: e.affine_select(out=ident[:], in_=ident[:], pattern=[[-1, 128]], compare_op=ALU.is_equal, fill=0.0,
                                           base=0, channel_multiplier=1), reads=["s5_id"], writes=["s5_id"])
    S.op("sp", lambda e: e.dma_start(out=col[:], in_=io["s5col"]), writes=["s5_col"], stream="ld")
    k = "s5_cw"
    LR, LI = col[:, 0, :], col[:, 1, :]
    DT, AR, TH, RR, C1, S1 = (cw[:, i, :] for i in range(6))
    T0, T1, T2, LBR, LBI, DEN, KR, KI = (cw[:, i, :] for i in range(6, 14))
    _act(S, DT, col[:, 2, :], AF.Exp, ["s5_col"], [k])
    _tt(S, "dve", AR, LR, DT, ALU.mult, ["s5_col", k], [k])
    _tt(S, "dve", TH, LI, DT, ALU.mult, ["s5_col", k], [k])
    _horner(S, "dve", RR, AR, [1.0 / 720, 1.0 / 120, 1.0 / 24, 1.0 / 6, 0.5, 1.0, 1.0], k, k, k)
    _sincos_acc(S, "dve", TH, C1, S1, T0, T1, T2, k)
    _tt(S, "dve", LBR, RR, C1, ALU.mult, [k], [k])
    _tt(S, "dve", LBI, RR, S1, ALU.mult, [k], [k])
    _ts(S, "dve", LBR, LBR, -1.0, None, ALU.add, None, [k], [k])
    _tt(S, "dve", T0, LR, LR, ALU.mult, ["s5_col", k], [k])
    _tt(S, "dve", T1, LI, LI, ALU.mult, ["s5_col", k], [k])
    _tt(S, "dve", DEN, T0, T1, ALU.add, [k], [k])
    S.op("dve", lambda e: e.reciprocal(out=DEN, in_=DEN), reads=[k], writes=[k])
    _tt(S, "dve", T0, LBR, LR, ALU.mult, ["s5_col", k], [k])
    _tt(S, "dve", T1, LBI, LI, ALU.mult, ["s5_col", k], [k])
    _tt(S, "dve", KR, T0, T1, ALU.add, [k], [k])
    _tt(S, "dve", KR, KR, DEN, ALU.mult, [k], [k])
    _tt(S, "dve", T0, LBI, LR, ALU.mult, ["s5_col", k], [k])
    _tt(S, "dve", T1, LBR, LI, ALU.mult, ["s5_col", k], [k])
    _tt(S, "dve", KI, T0, T1, ALU.subtract, [k], [k])
    _tt(S, "dve", KI, KI, DEN, ALU.mult, [k], [k])
    kE = "s5_E"
    kP = "s5_P"
    S.op("pool", lambda e: e.memset(Ec[:, :, 0:1], 1.0), writes=[kE])
    S.op("pool", lambda e: e.memset(Es[:, :, 0:1], 0.0), writes=[kE])
    S.op("dve", lambda e: e.tensor_copy(out=Pw[:, 0, :], in_=C1), reads=[k], writes=[kP])
    S.op("dve", lambda e: e.tensor_copy(out=Pw[:, 1, :], in_=S1), reads=[k], writes=[kP])
    ln = 1
    while ln < TC:
        cPb = Pw[:, 0:1, :].rearrange("p o j -> p j o").to_broadcast([128, 16, ln])
        sPb = Pw[:, 1:2, :].rearrange("p o j -> p j o").to_broadcast([128, 16, ln])
        t0 = tb[:, 0, :, 0:ln]
        t1 = tb[:, 1, :, 0:ln]
        _tt(S, "dve", t0, Ec[:, :, 0:ln], cPb, ALU.mult, [kE, kP], ["s5_tb"])
        _tt(S, "dve", t1, Es[:, :, 0:ln], sPb, ALU.mult, [kE, kP], ["s5_tb"])
        _tt(S, "dve", Ec[:, :, ln:2 * ln], t0, t1, ALU.subtract, ["s5_tb"], [kE])
        _tt(S, "dve", t0, Ec[:, :, 0:ln], sPb, ALU.mult, [kE, kP], ["s5_tb"])
        _tt(S, "dve", t1, Es[:, :, 0:ln], cPb, ALU.mult, [kE, kP], ["s5_tb"])
        _tt(S, "dve", Es[:, :, ln:2 * ln], t0, t1, ALU.add, ["s5_tb"], [kE])
        _tt(S, "dve", Pw[:, 2, :], Pw[:, 0, :], Pw[:, 1, :], ALU.mult, [kP], [kP])
        _tt(S, "dve", Pw[:, 3, :], Pw[:, 1, :], Pw[:, 1, :], ALU.mult, [kP], [kP])
        _tt(S, "dve", Pw[:, 0, :], Pw[:, 0, :], Pw[:, 0, :], ALU.mult, [kP], [kP])
        _tt(S, "dve", Pw[:, 0, :], Pw[:, 0, :], Pw[:, 3, :], ALU.subtract, [kP], [kP])
        _ts(S, "dve", Pw[:, 1, :], Pw[:, 2, :], 2.0, None, ALU.mult, None, [kP], [kP])
        ln *= 2
    S.op("pool", lambda e: e.memset(Bz[:], 0.0), writes=["s5_Bz"])

    def ldB(e):
        r = []
        for ri, nm in enumerate(("bT_re", "bT_im")):
            for j in range(16):
                jj = j % 4
                for gl in range(2):
                    g = 2 * j + gl
                    p0 = (2 * jj + gl) * 16
                    r.append(e.dma_start(out=Bz[p0:p0 + 16, ri, j, gl * 64:(gl + 1) * 64], in_=io[nm][g]))
        return r
    S.op("sp", ldB, reads=[], writes=["s5_Bz"], stream="ld", ndma=64)
    pk, pkkey = pset
    for j in range(16):
        _ts(S, "dve", dg[:, 0, :], ident[:], cw[:, 12, j:j + 1], None, ALU.mult, None, ["s5_id", k], ["s5_dg"])
        _ts(S, "dve", dg[:, 1, :], ident[:], cw[:, 13, j:j + 1], None, ALU.mult, None, ["s5_id", k], ["s5_dg"])
        S.op("pe", lambda e: e.matmul(pk[:, 0:128], lhsT=ones[:], rhs=dg[:, 0, :], start=True, stop=True), reads=["c_ones", "s5_dg"], writes=[pkkey])
        S.op("pe", lambda e: e.matmul(pk[:, 128:256], lhsT=ones[:], rhs=dg[:, 1, :], start=True, stop=True), reads=["c_ones", "s5_dg"], writes=[pkkey])
        kr_, ki_ = pk[:, 0:128], pk[:, 128:256]
        _tt(S, "dve", tB[:, 0, :], kr_, Bz[:, 0, j, :], ALU.mult, ["s5_Bz", pkkey], ["s5_tB"])
        _tt(S, "dve", tB[:, 1, :], ki_, Bz[:, 1, j, :], ALU.mult, ["s5_Bz", pkkey], ["s5_tB"])
        _tt(S, "dve", Bb[:, 0, j, :], tB[:, 0, :], tB[:, 1, :], ALU.subtract, ["s5_tB"], [("s5_Bb", j)])
        _tt(S, "dve", tB[:, 0, :], ki_, Bz[:, 0, j, :], ALU.mult, ["s5_Bz", pkkey], ["s5_tB"])
        _tt(S, "dve", tB[:, 1, :], kr_, Bz[:, 1, j, :], ALU.mult, ["s5_Bz", pkkey], ["s5_tB"])
        _tt(S, "dve", Bb[:, 1, j, :], tB[:, 0, :], tB[:, 1, :], ALU.add, ["s5_tB"], [("s5_Bb", j)])
    S.op("pool", lambda e: e.memset(Cz[:], 0.0), writes=["s5_Cz"])

    def ldC(e):
        r = []
        for ri, nm in enumerate(("cT_re", "cT_im")):
            for j in range(16):
                jj = j % 4
                for gl in range(2):
                    g = 2 * j + gl
                    c0 = (2 * jj + gl) * 16
                    r.append(e.dma_start(out=Cz[gl * 64:(gl + 1) * 64, ri, j, c0:c0 + 16], in_=io[nm][g]))
        return r
    S.op("sp", ldC, reads=[], writes=["s5_Cz"], stream="ld", ndma=64)
    _ts(S, "pool", Cb[:, 0, :, :], Cz[:, 0, :, :], 0.5, None, ALU.mult, None, ["s5_Cz"], ["s5_Cb"])
    _ts(S, "pool", Cb[:, 1, :, :], Cz[:, 1, :, :], -0.5, None, ALU.mult, None, ["s5_Cz"], ["s5_Cb"])
    S.op("sp", lambda e: e.dma_start(out=dcol[:], in_=io["d_col"]), writes=["s5_d"], stream="ld")
    _ts(S, "pool", dcol[:], dcol[:], 0.5, None, ALU.mult, None, ["s5_d"], ["s5_d"])
    S.op("pool", lambda e: e.memset(G[:], 0.0), writes=[("s5_G", j) for j in range(16)])
    if "dbg_cw" in io:
        S.op("sp", lambda e: e.dma_start(out=io["dbg_cw"], in_=cw[:]), reads=[k], writes=["dbg1"], stream="st")
        S.op("sp", lambda e: e.dma_start(out=io["dbg_Ec"], in_=Ec[:]), reads=[kE], writes=["dbg2"], stream="st")
        S.op("sp", lambda e: e.dma_start(out=io["dbg_Es"], in_=Es[:]), reads=[kE], writes=["dbg3"], stream="st")
        S.op("sp", lambda e: e.dma_start(out=io["dbg_P"], in_=Pw[:]), reads=[kP], writes=["dbg4"], stream="st")
        S.op("sp", lambda e: e.dma_start(out=io["dbg_Bb"], in_=Bb[:]), reads=[("s5_Bb", j) for j in range(16)], writes=["dbg6"], stream="st")
        S.op("sp", lambda e: e.dma_start(out=io["dbg_Cb"], in_=Cb[:]), reads=["s5_Cb"], writes=["dbg7"], stream="st")
    yield "setup"
    ubuf = Rot([(sb("s5_u%d" % i, [128, 4, TC], F32), "s5_u%d" % i) for i in range(2)])
    ubb = Rot([(sb("s5_ub%d" % i, [128, 4, TC], BF16), "s5_ub%d" % i) for i in range(2)])
    wk = [(sb("s5_w%d" % i, [128, 8, TC], F32), "s5_w%d" % i) for i in range(2)]
    hb = Rot([(sb("s5_h%d" % i, [128, 2, TC], BF16), "s5_h%d" % i) for i in range(2)])
    ep = Rot([(sb("s5_e%d" % i, [128, 4, TC], F32), "s5_e%d" % i) for i in range(2)])
    uT = io["uT"]
    rcol = cw[:, 3, :]
    ln2 = Kb.const_col("c_ln2", float(np.log(2.0)))
    onec = Kb.one_col()
    for c in range(NCH):
        u, ukey = ubuf.next()
        ub, ubkey = ubb.next()
        S.op("sp", lambda e, u=u, c=c: e.dma_start(out=u[:], in_=uT[:, c * TC:(c + 1) * TC].rearrange("(q p) t -> p q t", p=128)),
             writes=[ukey], stream="ld")
        S.op("pool", lambda e, u=u, ub=ub: e.tensor_copy(out=ub[:], in_=u[:]), reads=[ukey], writes=[ubkey])
        for q in range(4):
            py, pykey = psy
            pyv = py[:, 0:TC]
            pyk = pykey
            for jp in range(2):
                js = [4 * q + 2 * jp, 4 * q + 2 * jp + 1]
                ctx = []
                for i, j in enumerate(js):
                    pb, pbkey = psb2[i]
                    S.op("pe", lambda e, pb=pb, ub=ub, j=j, q=q: e.matmul(pb[:, 0:TC], lhsT=Bb[:, 0, j, :], rhs=ub[:, q, :], start=True, stop=True),
                         reads=[("s5_Bb", j), ubkey], writes=[pbkey])
                    S.op("pe", lambda e, pb=pb, ub=ub, j=j, q=q: e.matmul(pb[:, TC:2 * TC], lhsT=Bb[:, 1, j, :], rhs=ub[:, q, :], start=True, stop=True),
                         reads=[("s5_Bb", j), ubkey], writes=[pbkey])
                    w, wkey = wk[i]
                    h, hkey = hb.next()
                    ctx.append(dict(j=j, pb=pb, pbkey=pbkey, w=w, wkey=wkey, h=h, hkey=hkey, cs=Ec[:, j, :], sn=Es[:, j, :],
                                    bre=pb[:, 0:TC], bim=pb[:, TC:2 * TC], glk=("s5_gl", i), gl=gl_t[:, i, :]))

                def both(f):
                    for x in ctx:
                        f(x)
                both(lambda x: _tt(S, "dve", x["w"][:, 0, :], x["bre"], x["cs"], ALU.mult, [x["pbkey"], kE], [(x["wkey"], 0)]))
                both(lambda x: _tt(S, "dve", x["w"][:, 1, :], x["bim"], x["sn"], ALU.mult, [x["pbkey"], kE], [(x["wkey"], 1)]))
                both(lambda x: _tt(S, "dve", x["w"][:, 2, :], x["bim"], x["cs"], ALU.mult, [x["pbkey"], kE], [(x["wkey"], 2)]))
                both(lambda x: _tt(S, "dve", x["w"][:, 3, :], x["bre"], x["sn"], ALU.mult, [x["pbkey"], kE], [(x["wkey"], 3)]))
                both(lambda x: _tt(S, "dve", x["w"][:, 0, :], x["w"][:, 0, :], x["w"][:, 1, :], ALU.add, [(x["wkey"], 0), (x["wkey"], 1)], [(x["wkey"], 0)]))
                both(lambda x: _tt(S, "dve", x["w"][:, 2, :], x["w"][:, 2, :], x["w"][:, 3, :], ALU.subtract, [(x["wkey"], 2), (x["wkey"], 3)], [(x["wkey"], 2)]))

                def scan(x, o, i_, gi):
                    j = x["j"]
                    w = x["w"]
                    rb = rcol[:, j:j + 1].to_broadcast([128, TC])
                    S.op("dve", lambda e: e.tensor_tensor_scan(out=w[:, o, :], data0=rb, data1=w[:, i_, :], initial=G[:, gi, j:j + 1],
                                                               op0=ALU.mult, op1=ALU.add),
                         reads=[(x["wkey"], i_), k, ("s5_G", j)], writes=[(x["wkey"], o)])
                both(lambda x: scan(x, 4, 0, 0))
                both(lambda x: scan(x, 5, 2, 1))

                def carry(x):
                    j = x["j"]
                    w = x["w"]
                    gl = x["gl"]
                    glk = x["glk"]
                    gre_l, gim_l = w[:, 4, TC - 1:TC], w[:, 5, TC - 1:TC]
                    rk = [(x["wkey"], 4), (x["wkey"], 5), kP]
                    _tt(S, "pool", gl[:, 0:1], gim_l, Pw[:, 1, j:j + 1], ALU.mult, rk, [glk])
                    _tt(S, "pool", gl[:, 1:2], gre_l, Pw[:, 0, j:j + 1], ALU.mult, rk, [glk])
                    _tt(S, "pool", gl[:, 2:3], gre_l, Pw[:, 1, j:j + 1], ALU.mult, rk, [glk])
                    _tt(S, "pool", gl[:, 3:4], gim_l, Pw[:, 0, j:j + 1], ALU.mult, rk, [glk])
                    _tt(S, "pool", G[:, 0, j:j + 1], gl[:, 1:2], gl[:, 0:1], ALU.subtract, [glk], [("s5_G", j)])
                    _tt(S, "pool", G[:, 1, j:j + 1], gl[:, 3:4], gl[:, 2:3], ALU.add, [glk], [("s5_G", j)])
                both(carry)
                both(lambda x: _tt(S, "dve", x["w"][:, 0, :], x["w"][:, 4, :], x["cs"], ALU.mult, [(x["wkey"], 4), kE], [(x["wkey"], 0)]))
                both(lambda x: _tt(S, "dve", x["w"][:, 1, :], x["w"][:, 5, :], x["sn"], ALU.mult, [(x["wkey"], 5), kE], [(x["wkey"], 1)]))
                both(lambda x: _tt(S, "dve", x["w"][:, 2, :], x["w"][:, 4, :], x["sn"], ALU.mult, [(x["wkey"], 4), kE], [(x["wkey"], 2)]))
                both(lambda x: _tt(S, "dve", x["w"][:, 3, :], x["w"][:, 5, :], x["cs"], ALU.mult, [(x["wkey"], 5), kE], [(x["wkey"], 3)]))
                both(lambda x: _tt(S, "dve", x["h"][:, 0, :], x["w"][:, 0, :], x["w"][:, 1, :], ALU.subtract, [(x["wkey"], 0), (x["wkey"], 1)], [x["hkey"]]))
                both(lambda x: _tt(S, "dve", x["h"][:, 1, :], x["w"][:, 2, :], x["w"][:, 3, :], ALU.add, [(x["wkey"], 2), (x["wkey"], 3)], [x["hkey"]]))
                for x in ctx:
                    j = x["j"]
                    h = x["h"]
                    jj = j % 4
                    S.op("pe", lambda e, h=h, j=j, jj=jj, pyv=pyv: e.matmul(pyv, lhsT=Cb[:, 0, j, :], rhs=h[:, 0, :], start=(jj == 0), stop=False),
                         reads=["s5_Cb", x["hkey"]], writes=[pyk])
                    S.op("pe", lambda e, h=h, j=j, jj=jj, pyv=pyv: e.matmul(pyv, lhsT=Cb[:, 1, j, :], rhs=h[:, 1, :], start=False, stop=(jj == 3)),
                         reads=["s5_Cb", x["hkey"]], writes=[pyk])
                yield
            e_, ekey = ep.next()
            yb = e_[:, 0, :]
            _stt(S, "dve", yb, u[:, q, :], dcol[:, q:q + 1], pyv, ALU.mult, ALU.add, [ukey, "s5_d", pyk], [ekey])
            _act(S, e_[:, 1, :], yb, AF.Square, [ekey], [ekey])
            _ts(S, "pool", e_[:, 1, :], e_[:, 1, :], 0.17886, 1.0, ALU.mult, ALU.add, [ekey], [ekey])
            _tt(S, "pool", e_[:, 1, :], e_[:, 1, :], yb, ALU.mult, [ekey], [ekey])
            _ts(S, "pool", e_[:, 1, :], e_[:, 1, :], -9.4, None, ALU.max, None, [ekey], [ekey])
            _act(S, e_[:, 2, :], e_[:, 1, :], AF.Exp, [ekey], [ekey], scale=-2.0 * 1.5957691216)
            _act(S, e_[:, 2, :], e_[:, 2, :], AF.Ln, [ekey, "c_onecol"], [ekey], scale=1.0, bias=onec)
            _act(S, e_[:, 2, :], e_[:, 2, :], AF.Exp, [ekey, "c_ln2"], [ekey], scale=-1.0, bias=ln2)
            _tt(S, "pool", e_[:, 3, :], e_[:, 2, :], yb, ALU.mult, [ekey], [ekey])
            S.op("sp", lambda e, e_=e_, q=q, c=c: e.dma_start(out=io["ysgT"][q * 128:(q + 1) * 128, c * TC:(c + 1) * TC], in_=e_[:, 3, :]),
                 reads=[ekey], writes=[("d_ysg", q, c)], stream="st")


def s5_host_params(inp, l, s):
    g0 = 32 * s
    f = np.float32
    def colr(a):
        return np.ascontiguousarray(a.reshape(16, 128).T)
    lre, lim = inp["s5_lam_re"][l][g0:g0 + 32], inp["s5_lam_im"][l][g0:g0 + 32]
    ldt = np.broadcast_to(inp["s5_log_dt"][l][g0:g0 + 32, None], (32, 64))
    s5col = np.stack([colr(lre), colr(lim), colr(ldt)], axis=1).astype(f)
    return dict(
        s5col=np.ascontiguousarray(s5col),
        bT_re=np.ascontiguousarray(inp["s5_b_re"][l][g0:g0 + 32].transpose(0, 2, 1)),
        bT_im=np.ascontiguousarray(inp["s5_b_im"][l][g0:g0 + 32].transpose(0, 2, 1)),
        cT_re=np.ascontiguousarray(inp["s5_c_re"][l][g0:g0 + 32].transpose(0, 2, 1)),
        cT_im=np.ascontiguousarray(inp["s5_c_im"][l][g0:g0 + 32].transpose(0, 2, 1)),
        d_col=np.ascontiguousarray(inp["s5_d"][l][512 * s:512 * (s + 1)].reshape(4, 128).T),
    )


def build_B_s5_only(dbg=False):
    Kb = KB()
    io = dict(
        uT=Kb.din("uT", [512, L], F32), s5col=Kb.din("s5col", [128, 3, 16], F32),
        bT_re=Kb.din("bT_re", [32, 16, 64], F32), bT_im=Kb.din("bT_im", [32, 16, 64], F32),
        cT_re=Kb.din("cT_re", [32, 64, 16], F32), cT_im=Kb.din("cT_im", [32, 64, 16], F32),
        d_col=Kb.din("d_col", [128, 4], F32), ysgT=Kb.dout("ysgT", [512, L], F32),
    )
    if dbg:
        io.update(dbg_cw=Kb.dout("dbg_cw", [128, 24, 16], F32), dbg_Ec=Kb.dout("dbg_Ec", [128, 16, TC], F32), dbg_Es=Kb.dout("dbg_Es", [128, 16, TC], F32),
                  dbg_P=Kb.dout("dbg_P", [128, 4, 16], F32), dbg_Bb=Kb.dout("dbg_Bb", [128, 2, 16, 128], BF16),
                  dbg_Cb=Kb.dout("dbg_Cb", [128, 2, 16, 128], BF16))
    g = s5_mixer(Kb, io, [Kb.ps[0], Kb.ps[1]], Kb.ps[2], Kb.ps[3])
    assert next(g) == "persist"
    m = Kb.mark()
    assert next(g) == "setup"
    Kb.S.barrier()
    Kb.release(m)
    for _ in g:
        pass
    Kb.S.emit()
    return Kb


def lru_mixer(Kb, io, psl):
    S = Kb.S
    sb = Kb.sb
    lc = sb("lr_col", [128, 4, 8], F32)
    cc = sb("lr_cc", [128, 6, 4], F32)
    Wbd = sb("lr_W", [128, 2, 4, 128], BF16)
    Dg = sb("lr_Dg", [128, 4, 4, 128], F32)
    carry = sb("lr_carry", [128, 4], F32)
    yield "persist"
    Wz = sb("lr_Wz", [128, 2, 4, 128], F32)
    ident = sb("lr_id", [128, 128], F32)
    onec = Kb.one_col()
    S.op("pool", lambda e: e.memset(ident[:], 1.0), writes=["lr_id"])
    S.op("pool", lambda e: e.affine_select(out=ident[:], in_=ident[:], pattern=[[-1, 128]], compare_op=ALU.is_equal, fill=0.0,
                                           base=0, channel_multiplier=1), reads=["lr_id"], writes=["lr_id"])
    S.op("sp", lambda e: e.dma_start(out=lc[:], in_=io["lru_col"]), writes=["lr_col"], stream="ld")
    S.op("pool", lambda e: e.memset(Wz[:], 0.0), writes=["lr_Wz"])

    def ldW(e):
        r = []
        for ri, nm in enumerate(("w_r", "w_i")):
            for q in range(4):
                for bl in range(2):
                    r.append(e.dma_start(out=Wz[bl * 64:(bl + 1) * 64, ri, q, bl * 64:(bl + 1) * 64], in_=io[nm][2 * q + bl]))
        return r
    S.op("sp", ldW, reads=[], writes=["lr_Wz"], stream="ld", ndma=16)
    S.op("pool", lambda e: e.tensor_copy(out=Wbd[:], in_=Wz[:]), reads=["lr_Wz"], writes=["lr_W"])
    for q in range(4):
        for kk in range(4):
            _ts(S, "pool", Dg[:, q, kk, :], ident[:], lc[:, q, kk:kk + 1], None, ALU.mult, None, ["lr_id", "lr_col"], ["lr_Dg"])
    kc = "lr_cc"
    lam = lc[:, :, 7]
    _act(S, cc[:, 4, :], lam, AF.Exp, ["lr_col"], [kc], scale=-1.0)
    _act(S, cc[:, 4, :], cc[:, 4, :], AF.Ln, [kc, "c_onecol"], [kc], bias=onec)
    _ts(S, "pool", cc[:, 0, :], cc[:, 4, :], -8.0, None, ALU.mult, None, [kc], [kc])
    _ts(S, "pool", cc[:, 1, :], cc[:, 4, :], -16.0, None, ALU.mult, None, [kc], [kc])
    _ts(S, "pool", cc[:, 2, :], lc[:, :, 5], -1.0, None, ALU.mult, None, ["lr_col"], [kc])
    _ts(S, "pool", cc[:, 3, :], lc[:, :, 6], -1.0, None, ALU.mult, None, ["lr_col"], [kc])
    S.op("pool", lambda e: e.memset(carry[:], 0.0), writes=["lr_carry"])
    yield "setup"
    HALF = 2048
    ubuf = Rot([(sb("lr_u%d" % i, [128, 8 + TC], F32), "lr_u%d" % i) for i in range(2)])
    xcf = Rot([(sb("lr_xc%d" % i, [128, TC], F32), "lr_xc%d" % i) for i in range(2)])
    xcb = Rot([(sb("lr_xb%d" % i, [128, TC], BF16), "lr_xb%d" % i) for i in range(2)])
    tmp = Rot([(sb("lr_t%d" % i, [128, 4, TC], F32), "lr_t%d" % i) for i in range(2)])
    abuf = Rot([(sb("lr_a%d" % i, [128, HALF], F32), "lr_a%d" % i) for i in range(1)])
    gbuf = Rot([(sb("lr_g%d" % i, [128, HALF], F32), "lr_g%d" % i) for i in range(1)])
    hbf = Rot([(sb("lr_hb%d" % i, [128, HALF], BF16), "lr_hb%d" % i) for i in range(1)])
    uT = io["uLT"]
    pl, plkey = psl
    for q in range(4):
        for half in range(2):
            af, akey = abuf.next()
            gf, gkey = gbuf.next()
            for cs in range(HALF // TC):
                c = half * (HALF // TC) + cs
                t0 = c * TC
                ub, ukey = ubuf.next()
                if c == 0:
                    S.op("pool", lambda e, ub=ub: e.memset(ub[:, 0:8], 0.0), writes=[ukey])
                    S.op("sp", lambda e, ub=ub, q=q: e.dma_start(out=ub[:, 8:8 + TC], in_=uT[q * 128:(q + 1) * 128, 0:TC]),
                         reads=[ukey], writes=[ukey], stream="ld")
                else:
                    S.op("sp", lambda e, ub=ub, q=q, t0=t0: e.dma_start(out=ub[:, 4:8 + TC], in_=uT[q * 128:(q + 1) * 128, t0 - 4:t0 + TC]),
                         writes=[ukey], stream="ld")
                for kk in range(4):
                    S.op("pe", lambda e, ub=ub, q=q, kk=kk: e.matmul(pl[:, 0:TC], lhsT=Dg[:, q, kk, :], rhs=ub[:, 5 + kk:5 + kk + TC],
                                                                   start=(kk == 0), stop=(kk == 3)),
                         reads=["lr_Dg", ukey], writes=[plkey])
                xc, xkey = xcf.next()
                xb, xbkey = xcb.next()
                _act(S, xc[:], pl[:, 0:TC], AF.Identity, [plkey, "lr_col"], [xkey], bias=lc[:, q, 4:5])
                S.op("pool", lambda e, xb=xb, xc=xc: e.tensor_copy(out=xb[:], in_=xc[:]), reads=[xkey], writes=[xbkey])
                S.op("pe", lambda e, xb=xb, q=q: e.matmul(pl[:, 0:TC], lhsT=Wbd[:, 0, q, :], rhs=xb[:], start=True, stop=True),
                     reads=["lr_W", xbkey], writes=[plkey])
                S.op("pe", lambda e, xb=xb, q=q: e.matmul(pl[:, TC:2 * TC], lhsT=Wbd[:, 1, q, :], rhs=xb[:], start=True, stop=True),
                     reads=["lr_W", xbkey], writes=[plkey])
                t, tkey = tmp.next()
                _act(S, t[:, 0, :], pl[:, 0:TC], AF.Exp, [plkey, kc], [tkey], scale=-1.0, bias=cc[:, 2, q:q + 1])
                _act(S, t[:, 1, :], pl[:, TC:2 * TC], AF.Exp, [plkey, kc], [tkey], scale=-1.0, bias=cc[:, 3, q:q + 1])
                _act(S, t[:, 0, :], t[:, 0, :], AF.Ln, [tkey, "c_onecol"], [tkey], bias=onec)
                _act(S, t[:, 1, :], t[:, 1, :], AF.Ln, [tkey, "c_onecol"], [tkey], bias=onec)
                _act(S, t[:, 0, :], t[:, 0, :], AF.Exp, [tkey], [tkey], scale=-1.0)
                _act(S, t[:, 1, :], t[:, 1, :], AF.Exp, [tkey], [tkey], scale=-1.0)
                asl = af[:, cs * TC:(cs + 1) * TC]
                gsl = gf[:, cs * TC:(cs + 1) * TC]
                _act(S, asl, t[:, 0, :], AF.Exp, [tkey, kc], [akey], scale=cc[:, 0, q:q + 1])
                _act(S, t[:, 2, :], t[:, 0, :], AF.Exp, [tkey, kc], [tkey], scale=cc[:, 1, q:q + 1])
                _act(S, t[:, 2, :], t[:, 2, :], AF.Ln, [tkey, "c_onecol"], [tkey], scale=-1.0, bias=onec)
                _act(S, t[:, 2, :], t[:, 2, :], AF.Exp, [tkey], [tkey], scale=0.5)
                _tt(S, "pool", t[:, 3, :], t[:, 1, :], xc[:], ALU.mult, [tkey, xkey], [tkey])
                _tt(S, "pool", gsl, t[:, 3, :], t[:, 2, :], ALU.mult, [tkey], [gkey])
                if "dbg_t" in io and q == 0 and c == 1:
                    S.op("sp", lambda e, t=t: e.dma_start(out=io["dbg_t"], in_=t[:]), reads=[tkey], writes=["dbgl1"], stream="st")
                    S.op("sp", lambda e, xc=xc: e.dma_start(out=io["dbg_xc"], in_=xc[:]), reads=[xkey], writes=["dbgl2"], stream="st")
                    S.op("sp", lambda e, af=af: e.dma_start(out=io["dbg_a"], in_=af[:, TC:2 * TC]), reads=[akey], writes=["dbgl3"], stream="st")
                    S.op("sp", lambda e, gf=gf: e.dma_start(out=io["dbg_g"], in_=gf[:, TC:2 * TC]), reads=[gkey], writes=["dbgl4"], stream="st")
                    S.op("sp", lambda e: e.dma_start(out=io["dbg_cc"], in_=cc[:]), reads=[kc], writes=["dbgl5"], stream="st")
                yield
            hb, hkey = hbf.next()
            S.op("dve", lambda e, af=af, gf=gf, q=q: e.tensor_tensor_scan(out=gf[:], data0=af[:], data1=gf[:], initial=carry[:, q:q + 1],
                                                                    op0=ALU.mult, op1=ALU.add),
                 reads=[akey, gkey, "lr_carry"], writes=[gkey])
            S.op("pool", lambda e, gf=gf, q=q: e.tensor_copy(out=carry[:, q:q + 1], in_=gf[:, HALF - 1:HALF]), reads=[gkey], writes=["lr_carry"])
            S.op("pool", lambda e, gf=gf, hb=hb: e.tensor_copy(out=hb[:], in_=gf[:]), reads=[gkey], writes=[hkey])
            S.op("sp", lambda e, hb=hb, q=q, half=half: e.dma_start(out=io["yLruT"][q * 128:(q + 1) * 128, half * HALF:(half + 1) * HALF], in_=hb[:]),
                 reads=[hkey], writes=[("d_ylru", q, half)], stream="st")
            yield


def lru_host_params(inp, l, s):
    ch = slice(512 * s, 512 * (s + 1))
    cols = [inp["lru_conv_w"][l][k, ch] for k in range(4)] + [inp["lru_conv_b"][l][ch], inp["lru_b_r"][l].reshape(-1)[ch],
                                                               inp["lru_b_i"][l].reshape(-1)[ch], inp["lru_lambda"][l][ch]]
    a = np.stack(cols, axis=-1).reshape(4, 128, 8).transpose(1, 0, 2)
    return dict(lru_col=np.ascontiguousarray(a, dtype=np.float32),
                w_r=np.ascontiguousarray(inp["lru_w_r"][l][8 * s:8 * (s + 1)]), w_i=np.ascontiguousarray(inp["lru_w_i"][l][8 * s:8 * (s + 1)]))


def attn_mixer(Kb, io, banksA, banksO):
    S = Kb.S
    sb = Kb.sb
    tri = sb("at_tri", [128, 128], BF16)
    onesn = sb("at_on", [128, 128], F32)
    dmask = sb("at_dm", [128, 128], BF16)
    yield "persist"
    tmpf = sb("at_tmpf", [128, 128], F32)
    S.op("pool", lambda e: e.memset(tmpf[:], -1.0), writes=["at_tmpf"])
    S.op("pool", lambda e: e.memset(onesn[:], -1.0), writes=["at_on"])
    S.op("pool", lambda e: e.affine_select(out=tri[:], in_=tmpf[:], pattern=[[-1, 128]], compare_op=ALU.is_ge, fill=0.0,
                                           base=0, channel_multiplier=1), reads=["at_tmpf"], writes=["at_tri"])
    S.op("pool", lambda e: e.memset(tmpf[:], 1.0), reads=["at_tri"], writes=["at_tmpf"])
    S.op("pool", lambda e: e.affine_select(out=dmask[:], in_=tmpf[:], pattern=[[1, 128]], compare_op=ALU.is_gt, fill=0.0,
                                           base=0, channel_multiplier=-1), reads=["at_tmpf"], writes=["at_dm"])
    yield "setup"
    onec = Kb.one_col()
    st = []
    for s_ in range(2):
        st.append(dict(
            q=sb("at_q%d" % s_, [128, L], BF16), k=sb("at_k%d" % s_, [128, L], BF16), v=sb("at_v%d" % s_, [128, 32, 128], BF16),
            e=Rot([(sb("at_e%d_%d" % (s_, i), [128, 512], F32), "at_e%d_%d" % (s_, i)) for i in range(1)]),
            sp=Rot([(sb("at_sp%d_%d" % (s_, i), [128, 512], BF16), "at_sp%d_%d" % (s_, i)) for i in range(2)]),
            w=Rot([(sb("at_w%d_%d" % (s_, i), [128, 512], BF16), "at_w%d_%d" % (s_, i)) for i in range(2)]),
            acc=sb("at_acc%d" % s_, [128, 512], F32), o=Rot([(sb("at_o%d_%d" % (s_, i), [128, 512], BF16), "at_o%d_%d" % (s_, i)) for i in range(1)]),
            A=banksA[s_], O=banksO[s_], key="at%d" % s_))

    def stream(s_):
        x = st[s_]
        kq, kk_, kv, kacc = x["key"] + "q", x["key"] + "k", x["key"] + "v", x["key"] + "acc"
        A, Akey = x["A"]
        O, Okey = x["O"]
        for h in (2 * s_, 2 * s_ + 1):
            S.op("sp", lambda e, h=h: e.dma_start(out=x["q"][:], in_=io["qT"][h * 128:(h + 1) * 128, :]), writes=[kq], stream="ld")
            S.op("sp", lambda e, h=h: e.dma_start(out=x["k"][:], in_=io["kT"][h * 128:(h + 1) * 128, :]), writes=[kk_], stream="ld")

            def ldv(e, h=h):
                r = []
                for b0 in range(0, 32, 8):
                    r.append(e.dma_start(out=x["v"][:, b0:b0 + 8, :],
                                         in_=io["v"][b0 * 128:(b0 + 8) * 128, h * 128:(h + 1) * 128].rearrange("(kb p) d -> p kb d", p=128)))
                return r
            S.op("sp", ldv, writes=[kv], stream="ld", ndma=4)
            for qg in range(8):
                S.op("pool", lambda e: e.memset(x["acc"][:], 0.0), writes=[kacc])
                nk = 4 * qg + 4
                for ui, kb in enumerate(range(nk - 1, -1, -1)):
                    j = kb - 4 * qg
                    c0 = 128 * j if j >= 0 else 0
                    cs = slice(c0, 512)
                    qs = slice(qg * 512 + c0, (qg + 1) * 512)
                    ks = slice(kb * 128, (kb + 1) * 128)
                    e_, ekey = x["e"].next()
                    sp, spkey = x["sp"].next()
                    w, wkey = x["w"].next()
                    S.op("pe", lambda e, ks=ks, qs=qs, cs=cs: e.matmul(A[:, cs], lhsT=x["k"][:, ks], rhs=x["q"][:, qs], start=True, stop=True),
                         reads=[kk_, kq], writes=[Akey])
                    _act(S, e_[:, cs], A[:, cs], AF.Exp, [Akey], [ekey])
                    _act(S, sp[:, cs], e_[:, cs], AF.Ln, [ekey, "c_onecol"], [spkey], bias=onec)
                    if j >= 0:
                        _tt(S, "pool", sp[:, c0:c0 + 128], sp[:, c0:c0 + 128], dmask[:], ALU.mult, [spkey, "at_dm"], [spkey])
                    first = (ui == 0)
                    S.op("pe", lambda e, ks=ks, qs=qs, cs=cs: e.matmul(A[:, cs], lhsT=x["k"][:, ks], rhs=x["q"][:, qs], start=True, stop=False),
                         reads=[kk_, kq, ekey], writes=[Akey])
                    S.op("pe", lambda e, sp=sp, cs=cs, first=first: e.matmul(A[:, cs], lhsT=tri[:], rhs=sp[:, cs], start=False, stop=first),
                         reads=["at_tri", spkey], writes=[Akey])
                    if not first:
                        S.op("pe", lambda e, cs=cs: e.matmul(A[:, cs], lhsT=onesn[:], rhs=x["acc"][:, cs], start=False, stop=True),
                             reads=["at_on", kacc], writes=[Akey])
                    _act(S, w[:, cs], A[:, cs], AF.Exp, [Akey], [wkey])
                    if j >= 0:
                        _tt(S, "pool", w[:, c0:c0 + 128], w[:, c0:c0 + 128], dmask[:], ALU.mult, [wkey, "at_dm"], [wkey])
                    if kb > 0:
                        _tt(S, "pool", x["acc"][:, cs], x["acc"][:, cs], sp[:, cs], ALU.add, [kacc, spkey], [kacc])
                    S.op("pe", lambda e, w=w, cs=cs, kb=kb, first=first: e.matmul(O[:, cs], lhsT=x["v"][:, kb, :], rhs=w[:, cs], start=first, stop=(kb == 0)),
                         reads=[kv, wkey], writes=[Okey])
                    yield
                o, okey = x["o"].next()
                S.op("dve", lambda e, o=o: e.tensor_copy(out=o[:], in_=O[:]), reads=[Okey], writes=[okey])
                S.op("sp", lambda e, o=o, h=h, qg=qg: e.dma_start(out=io["yAttT"][h * 128:(h + 1) * 128, qg * 512:(qg + 1) * 512], in_=o[:]),
                     reads=[okey], writes=[("d_yatt", h, qg)], stream="st")

    g0, g1 = stream(0), stream(1)
    alive = [g0, g1]
    while alive:
        for g in list(alive):
            try:
                next(g)
            except StopIteration:
                alive.remove(g)
        yield


def stage_B(Kb, io, which=("s5", "lru", "attn")):
    gens = []
    ps = Kb.ps
    if "s5" in which:
        gens.append(("s5", s5_mixer(Kb, io, [ps[0], ps[1]], ps[2], ps[3]), 128))
    if "lru" in which:
        gens.append(("lru", lru_mixer(Kb, io, ps[3]), 72))
    if "attn" in which:
        gens.append(("attn", attn_mixer(Kb, io, [ps[4], ps[5]], [ps[6], ps[7]]), 288))
    for _, g, _n in gens:
        assert next(g) == "persist"
    m = Kb.mark()
    for _, g, _n in gens:
        assert next(g) == "setup"
    Kb.S.barrier()
    Kb.release(m)
    prog = {n: 0 for n, _, _ in gens}
    alive = {n: (g, tot) for n, g, tot in gens}
    while alive:
        n = min(alive, key=lambda n_: prog[n_] / alive[n_][1])
        try:
            next(alive[n][0])
            prog[n] += 1
        except StopIteration:
            del alive[n]


def build_B(which=("s5", "lru", "attn"), dbg=False):
    Kb = KB()
    io = {}
    if "s5" in which:
        io.update(uT=Kb.din("uT", [512, L], F32), s5col=Kb.din("s5col", [128, 3, 16], F32),
                  bT_re=Kb.din("bT_re", [32, 16, 64], F32), bT_im=Kb.din("bT_im", [32, 16, 64], F32),
                  cT_re=Kb.din("cT_re", [32, 64, 16], F32), cT_im=Kb.din("cT_im", [32, 64, 16], F32),
                  d_col=Kb.din("d_col", [128, 4], F32), ysgT=Kb.dout("ysgT", [512, L], F32))
    if "lru" in which:
        io.update(uLT=Kb.din("uLT", [512, L], F32), lru_col=Kb.din("lru_col", [128, 4, 8], F32),
                  w_r=Kb.din("w_r", [8, 64, 64], F32), w_i=Kb.din("w_i", [8, 64, 64], F32), yLruT=Kb.dout("yLruT", [512, L], BF16))
        if dbg:
            io.update(dbg_t=Kb.dout("dbg_t", [128, 4, TC], F32), dbg_xc=Kb.dout("dbg_xc", [128, TC], F32), dbg_a=Kb.dout("dbg_a", [128, TC], F32),
                      dbg_g=Kb.dout("dbg_g", [128, TC], F32), dbg_cc=Kb.dout("dbg_cc", [128, 6, 4], F32))
    if "attn" in which:
        io.update(qT=Kb.din("qT", [512, L], BF16), kT=Kb.din("kT", [512, L], BF16), v=Kb.din("v", [L, 512], BF16),
                  yAttT=Kb.dout("yAttT", [512, L], BF16))
    stage_B(Kb, io, which)
    Kb.S.emit()
    return Kb


def _evac_alt(S, cnt, out_ap, ps, pkey, okey, scale=1.0):
    if cnt % 2 == 0:
        S.op("act", lambda e: e.activation(out=out_ap, in_=ps[:], func=AF.Copy, scale=scale), reads=[pkey], writes=[okey])
    else:
        S.op("dve", lambda e: e.tensor_scalar(out=out_ap, in0=ps[:], scalar1=scale, scalar2=None, op0=ALU.mult), reads=[pkey], writes=[okey])


def stage_C(Kb, io, final_norm=False):
    S = Kb.S
    sb = Kb.sb
    HT = 1024
    bgl = sb("C_bgl", [128, 8], F32)
    bgt = sb("C_bgt", [128, 48], F32)
    g2 = sb("C_g2", [128, 16], F32)
    S.op("sp", lambda e: e.dma_start(out=bgl[:], in_=io["b_glu"]), writes=["C_bgl"], stream="ld")
    S.op("sp", lambda e: e.dma_start(out=bgt[:], in_=io["b_gate"]), writes=["C_bgt"], stream="ld")
    S.op("sp", lambda e: e.dma_start(out=g2[:], in_=io["g2"]), writes=["C_g2"], stream="ld")
    if final_norm:
        gf = sb("C_gf", [128, 16], F32)
        S.op("sp", lambda e: e.dma_start(out=gf[:], in_=io["gf"]), writes=["C_gf"], stream="ld")
    mark0 = Kb.mark()
    ev = [0]
    ysb = sb("C_ysb", [128, 8, HT], BF16)
    ylr = sb("C_ylr", [128, 8, HT], BF16)
    yat = sb("C_yat", [128, 8, HT], BF16)
    xn = sb("C_xn", [128, 16, HT], BF16)
    mg = sb("C_mg", [128, 16, HT], BF16)
    wga = WStream(Kb, "wCa", 5, 16, 128)
    wgb_ = WStream(Kb, "wCb", 3, 8, 384)
    ysf = Rot([(sb("C_ysf%d" % i, [128, 512], F32), "C_ysf%d" % i) for i in range(2)])
    zt = Rot([(sb("C_z%d" % i, [128, 512], F32), "C_z%d" % i) for i in range(2)])
    gt = Rot([(sb("C_g%d" % i, [128, 512], F32), "C_g%d" % i) for i in range(3)])
    ma = Rot([(sb("C_ma%d" % i, [128, 512], F32), "C_ma%d" % i) for i in range(2)])
    hld = Rot([(sb("C_hl%d" % i, [128, 512], F32), "C_hl%d" % i) for i in range(2)])
    hst = Rot([(sb("C_hs%d" % i, [128, 512], F32), "C_hs%d" % i) for i in range(2)])
    for th in range(2):
        tsl = slice(th * HT, (th + 1) * HT)
        S.op("pool", lambda e, tsl=tsl: [e.dma_start(out=ysb[:, k0:k0 + 4, :], in_=io["ysgT"][k0 * 128:(k0 + 4) * 128, tsl].rearrange("(kt p) t -> p kt t", p=128))
                                         for k0 in (0, 4)], writes=[("C_ysb", th)], stream="ldc", ndma=2)
        S.op("sp", lambda e, tsl=tsl: [e.dma_start(out=ylr[:, k0:k0 + 4, :], in_=io["yLruT"][k0 * 128:(k0 + 4) * 128, tsl].rearrange("(kt p) t -> p kt t", p=128))
                                       for k0 in (0, 4)], writes=[("C_ylr", th)], stream="ld", ndma=2)
        S.op("sp", lambda e, tsl=tsl: [e.dma_start(out=yat[:, k0:k0 + 4, :], in_=io["yAttT"][k0 * 128:(k0 + 4) * 128, tsl].rearrange("(kt p) t -> p kt t", p=128))
                                       for k0 in (0, 4)], writes=[("C_yat", th)], stream="ld", ndma=2)
        S.op("sp", lambda e, tsl=tsl: [e.dma_start(out=xn[:, k0:k0 + 4, :], in_=io["xnT"][k0 * 128:(k0 + 4) * 128, tsl].rearrange("(kt p) t -> p kt t", p=128))
                                       for k0 in (0, 4, 8, 12)], writes=[("C_xn", th)], stream="ld", ndma=4)
        ys5 = sb("C_ys5", [128, 8, HT], BF16) if th == 0 else ys5
        for s in range(3):
            c0, c1 = s * 384, min(1024, (s + 1) * 384)
            wb, wkey = wgb_.load(io["w_glu"][:, c0:c1], 8, c1 - c0)
            for cbl in range((c1 - c0) // 128):
                cb = c0 // 128 + cbl
                for tt in range(HT // 512):
                    ps, pkey = Kb.psrot.next()
                    mm_group(S, ps[:], pkey, [(wb[:, kt, cbl * 128:(cbl + 1) * 128], ysb[:, kt, tt * 512:(tt + 1) * 512], [wkey, ("C_ysb", th)])
                                              for kt in range(8)])
                    z, zkey = zt.next()
                    yf, yfkey = ysf.next()
                    S.op("sp", lambda e, yf=yf, cb=cb, tt=tt, th=th: e.dma_start(out=yf[:], in_=io["ysgT"][cb * 128:(cb + 1) * 128, th * HT + tt * 512:th * HT + (tt + 1) * 512]),
                         writes=[yfkey], stream="ld")
                    _act(S, z[:], ps[:], AF.Sigmoid, [pkey, "C_bgl"], [zkey], bias=bgl[:, cb:cb + 1])
                    _tt(S, "dve", ys5[:, cb, tt * 512:(tt + 1) * 512], z[:], yf[:], ALU.mult, [zkey, yfkey], [("C_ys5", th)])
        ybr = [(ys5, ("C_ys5", th)), (ylr, ("C_ylr", th)), (yat, ("C_yat", th))]
        for cb in range(16):
            wgs = []
            for br in range(3):
                wgs.append(wga.load(io["w_gate"][:, br * 2048 + cb * 128:br * 2048 + (cb + 1) * 128], 16, 128))
            wbs = []
            for br in range(3):
                wbs.append(None)
            wbb, wbkey = wgb_.rot.next()

            def ldbr(e, wbb=wbb, cb=cb):
                r = []
                for br in range(3):
                    for k0 in (0, 4):
                        r.append(e.dma_start(out=wbb[:, k0:k0 + 4, br * 128:(br + 1) * 128],
                                             in_=io["w_br"][br, k0 * 128:(k0 + 4) * 128, cb * 128:(cb + 1) * 128].rearrange("(kt p) c -> p kt c", p=128)))
                return r
            S.op("pool", ldbr, writes=[wbkey], stream="wCb", ndma=6)
            for tt in range(HT // 512):
                tsl2 = slice(tt * 512, (tt + 1) * 512)
                m, mkey = ma.next()
                for br in range(3):
                    wgb, wgkey = wgs[br]
                    psg, pgkey = Kb.psrot.next()
                    mm_group(S, psg[:], pgkey, [(wgb[:, kt, 0:128], xn[:, kt, tsl2], [wgkey, ("C_xn", th)]) for kt in range(16)])
                    psp, ppkey = Kb.psrot.next()
                    yb_, ybkey = ybr[br]
                    mm_group(S, psp[:], ppkey, [(wbb[:, kt, br * 128:(br + 1) * 128], yb_[:, kt, tsl2], [wbkey, ybkey]) for kt in range(8)])
                    g, gkey = gt.next()
                    _act(S, g[:], psg[:], AF.Sigmoid, [pgkey, "C_bgt"], [gkey], bias=bgt[:, br * 16 + cb:br * 16 + cb + 1])
                    if br == 0:
                        _tt(S, "dve", m[:], psp[:], g[:], ALU.mult, [ppkey, gkey], [mkey])
                    else:
                        _tt(S, "dve", g[:], psp[:], g[:], ALU.mult, [ppkey, gkey], [gkey])
                        if br == 1:
                            _tt(S, "pool", m[:], m[:], g[:], ALU.add, [mkey, gkey], [mkey])
                        else:
                            _tt(S, "pool", mg[:, cb, tsl2], m[:], g[:], ALU.add, [mkey, gkey], [("C_mg", th, cb)])
        mgkeys = [("C_mg", th, cb) for cb in range(16)]
        for s in range(16):
            c0, c1 = s * 128, (s + 1) * 128
            wb, wkey = wga.load(io["w_out"][:, c0:c1], 16, 128)
            for cbl in range(1):
                cb = s
                for tt in range(HT // 512):
                    gsl = slice(th * HT + tt * 512, th * HT + (tt + 1) * 512)
                    ps, pkey = Kb.psrot.next()
                    mm_group(S, ps[:], pkey, [(wb[:, kt, cbl * 128:(cbl + 1) * 128], mg[:, kt, tt * 512:(tt + 1) * 512], [wkey] + mgkeys)
                                              for kt in range(16)])
                    hl, hlkey = hld.next()
                    hs, hskey = hst.next()
                    S.op("sp", lambda e, hl=hl, cb=cb, gsl=gsl: e.dma_start(out=hl[:], in_=io["hT"][cb * 128:(cb + 1) * 128, gsl]), writes=[hlkey], stream="ld")
                    _tt(S, "dve", hs[:], ps[:], hl[:], ALU.add, [pkey, hlkey], [hskey])
                    S.op("sp", lambda e, hs=hs, cb=cb, gsl=gsl: e.dma_start(out=io["h1T"][cb * 128:(cb + 1) * 128, gsl], in_=hs[:]),
                         reads=[hskey], writes=[("d_h1", cb, th, tt)], stream="st")
    h1keys = [("d_h1", cb, th, tt) for cb in range(16) for th in range(2) for tt in range(2)]
    S.barrier()
    Kb.release(mark0)
    hn = sb("C_hn", [128, 16, TOK], BF16)
    hb = sb("C_hb", [128, 16, 512], F32)
    nrm4 = make_nrm(Kb, "C4")
    for tt in range(4):
        def ld(e, tt=tt):
            r = []
            for f0 in range(0, 16, 4):
                r.append(e.dma_start(out=hb[:, f0:f0 + 4, :],
                                     in_=io["h1T"][f0 * 128:(f0 + 4) * 128, tt * 512:(tt + 1) * 512].rearrange("(ft p) t -> p ft t", p=128)))
            return r
        S.op("sp", ld, reads=[], writes=["C_hb"], stream="ld", ndma=4)
        rmsnorm_tile(Kb, hb, "C_hb", g2, "C_g2", lambda ft, tt=tt: hn[:, ft, tt * 512:(tt + 1) * 512], lambda ft, tt=tt: [("C_hn", tt)], nrm4)
    wu = WStream(Kb, "wU", 3, 16, 512)
    rl = Rot([(sb("C_rl%d" % i, [128, 512], F32), "C_rl%d" % i) for i in range(3)])
    hd = Rot([(sb("C_hd%d" % i, [128, TOK], BF16), "C_hd%d" % i) for i in range(2)])
    for s in range(16):
        wb, wkey = wu.load(io["w_up"][:, s * 512:(s + 1) * 512], 16, 512)
        for cbl in range(4):
            cb = s * 4 + cbl
            hdt, hdkey = hd.next()
            for tt in range(4):
                ps, pkey = Kb.psrot.next()
                mm_group(S, ps[:], pkey, [(wb[:, kt, cbl * 128:(cbl + 1) * 128], hn[:, kt, tt * 512:(tt + 1) * 512], [wkey, ("C_hn", tt)])
                                          for kt in range(16)])
                r_, rkey = rl.next()
                _act(S, r_[:], ps[:], AF.Relu, [pkey], [rkey])
                _tt(S, "pool" if tt % 2 else "dve", hdt[:, tt * 512:(tt + 1) * 512], r_[:], r_[:], ALU.mult, [rkey], [hdkey])
            S.op("sp", lambda e, hdt=hdt, cb=cb: e.dma_start(out=io["hidT"][cb * 128:(cb + 1) * 128, :], in_=hdt[:]),
                 reads=[hdkey], writes=[("d_hid", cb)], stream="st")
    S.barrier()
    Kb.release(mark0)
    hid = sb("C_hid", [128, 32, HT], BF16)
    acc = sb("C_acc", [128, 16, HT], F32)
    wd = WStream(Kb, "wD", 3, 32, 128)
    h1l = Rot([(sb("C_h1l%d" % i, [128, HT], F32), "C_h1l%d" % i) for i in range(2)])
    ost = Rot([(sb("C_ost%d" % i, [128, HT], F32), "C_ost%d" % i) for i in range(2)])
    nrm6 = make_nrm(Kb, "C6") if final_norm else None
    for th in range(2):
        for kh in range(2):
            def ldh(e, th=th, kh=kh):
                r = []
                for k0 in range(0, 32, 4):
                    r.append(e.dma_start(out=hid[:, k0:k0 + 4, :],
                                         in_=io["hidT"][(kh * 32 + k0) * 128:(kh * 32 + k0 + 4) * 128, th * HT:(th + 1) * HT].rearrange("(kt p) t -> p kt t", p=128)))
                return r
            S.op("sp", ldh, reads=[], writes=["C_hid"], stream="ld", ndma=8)
            for cb in range(16):
                wb, wkey = wd.load(io["w_down"][kh * 4096:(kh + 1) * 4096, cb * 128:(cb + 1) * 128], 32, 128)
                if kh == 0:
                    hl, hlkey = h1l.next()
                    S.op("sp", lambda e, hl=hl, cb=cb, th=th: e.dma_start(out=hl[:], in_=io["h1T"][cb * 128:(cb + 1) * 128, th * HT:(th + 1) * HT]),
                         writes=[hlkey], stream="ld")
                else:
                    o, okey = ost.next()
                for tt in range(HT // 512):
                    ps, pkey = Kb.psrot.next()
                    mm_group(S, ps[:], pkey, [(wb[:, kt, :], hid[:, kt, tt * 512:(tt + 1) * 512], [wkey, "C_hid"]) for kt in range(32)])
                    asl = acc[:, cb, tt * 512:(tt + 1) * 512]
                    if kh == 0:
                        _tt(S, "dve", asl, ps[:], hl[:, tt * 512:(tt + 1) * 512], ALU.add, [pkey, hlkey], [("C_acc", cb)])
                    elif not final_norm:
                        _tt(S, "dve", o[:, tt * 512:(tt + 1) * 512], ps[:], asl, ALU.add, [pkey, ("C_acc", cb)], [okey])
                    else:
                        _tt(S, "dve", asl, ps[:], asl, ALU.add, [pkey, ("C_acc", cb)], [("C_acc", cb)])
                if kh == 1 and not final_norm:
                    S.op("sp", lambda e, o=o, cb=cb, th=th: e.dma_start(out=io["hoT"][cb * 128:(cb + 1) * 128, th * HT:(th + 1) * HT], in_=o[:]),
                         reads=[okey], writes=[("d_ho", cb, th)], stream="st")
        if final_norm:
            for tt in range(HT // 512):
                ps, pkey = Kb.psrot.next()
                ones = Kb.ones_f32()
                c = nrm6
                for ft in range(16):
                    s_, skey = c["sq"].next()
                    S.op("act", lambda e, s_=s_, ft=ft, tt=tt: e.activation(out=s_[:], in_=acc[:, ft, tt * 512:(tt + 1) * 512], func=AF.Square),
                         reads=[("C_acc", ft)], writes=[skey])
                    S.op("pe", lambda e, s_=s_, ft=ft, ps=ps: e.matmul(ps[:], lhsT=ones[:], rhs=s_[:], start=(ft == 0), stop=(ft == 15)),
                         reads=[skey, "c_ones"], writes=[pkey])
                ln, lkey = c["ln"]
                rstd, rkey = c["rstd"]
                et = c["eps"][0]
                _act(S, ln[:], ps[:], AF.Ln, [pkey, c["eps"][1]], [lkey], scale=1.0 / D, bias=et[:, 0:1])
                _act(S, rstd[:], ln[:], AF.Exp, [lkey], [rkey], scale=-0.5)
                for ft in range(16):
                    o, okey = ost.next()
                    _stt(S, "dve", o[:, 0:512], acc[:, ft, tt * 512:(tt + 1) * 512], gf[:, ft:ft + 1], rstd[:], ALU.mult, ALU.mult,
                         [("C_acc", ft), rkey, "C_gf"], [okey])
                    S.op("sp", lambda e, o=o, ft=ft, th=th, tt=tt: e.dma_start(out=io["hoT"][ft * 128:(ft + 1) * 128, th * HT + tt * 512:th * HT + (tt + 1) * 512], in_=o[:, 0:512]),
                         reads=[okey], writes=[("d_ho", ft, th, tt)], stream="st")


def build_C(final_norm=False, dbg=False):
    Kb = KB()
    io = dict(
        hT=Kb.din("hT", [D, TOK], F32), xnT=Kb.din("xnT", [D, TOK], BF16), ysgT=Kb.din("ysgT", [1024, TOK], F32),
        yLruT=Kb.din("yLruT", [1024, TOK], BF16), yAttT=Kb.din("yAttT", [1024, TOK], BF16),
        w_glu=Kb.din("w_glu", [1024, 1024], F32), b_glu=Kb.din("b_glu", [128, 8], F32),
        w_gate=Kb.din("w_gate", [D, 6144], F32), b_gate=Kb.din("b_gate", [128, 48], F32),
        w_br=Kb.din("w_br", [3, 1024, D], F32), w_out=Kb.din("w_out", [D, D], F32), g2=Kb.din("g2", [128, 16], F32),
        w_up=Kb.din("w_up", [D, DFF], F32), w_down=Kb.din("w_down", [DFF, D], F32),
        mT=None, h1T=(Kb.dout if dbg else Kb.dint)("h1T", [D, TOK], F32), hidT=(Kb.dout if dbg else Kb.dint)("hidT", [DFF, TOK], BF16),
        hoT=Kb.dout("hoT", [D, TOK], F32),
    )
    if final_norm:
        io["gf"] = Kb.din("gf", [128, 16], F32)
    stage_C(Kb, io, final_norm)
    Kb.S.emit()
    return Kb


def _io_A(Kb, hT=None):
    return dict(
        hT=hT if hT is not None else Kb.din("hT", [D, TOK], F32), g1=Kb.din("g1", [128, 16], F32), w_in5=Kb.din("w_in5", [D, 5120], F32),
        xnT=Kb.dout("xnT", [D, TOK], BF16), uS5T=Kb.dout("uS5T", [1024, TOK], F32), uLruT=Kb.dout("uLruT", [1024, TOK], F32),
        qT=Kb.dout("qT", [1024, TOK], BF16), kT=Kb.dout("kT", [1024, TOK], BF16), v=Kb.dout("v", [TOK, 1024], BF16))


def _io_C(Kb, final_norm):
    io = dict(
        hT=Kb.din("hT", [D, TOK], F32), xnT=Kb.din("xnT_in", [D, TOK], BF16), ysgT=Kb.din("ysgT", [1024, TOK], F32),
        yLruT=Kb.din("yLruT", [1024, TOK], BF16), yAttT=Kb.din("yAttT", [1024, TOK], BF16),
        w_glu=Kb.din("w_glu", [1024, 1024], F32), b_glu=Kb.din("b_glu", [128, 8], F32),
        w_gate=Kb.din("w_gate", [D, 6144], F32), b_gate=Kb.din("b_gate", [128, 48], F32),
        w_br=Kb.din("w_br", [3, 1024, D], F32), w_out=Kb.din("w_out", [D, D], F32), g2=Kb.din("g2", [128, 16], F32),
        w_up=Kb.din("w_up", [D, DFF], F32), w_down=Kb.din("w_down", [DFF, D], F32),
        h1T=Kb.dint("h1T", [D, TOK], F32), hidT=Kb.dint("hidT", [DFF, TOK], BF16),
        hoT=Kb.dout("hoT", [D, TOK], F32))
    if final_norm:
        io["gf"] = Kb.din("gf", [128, 16], F32)
    return io


def build_CA():
    Kb = KB()
    m0 = Kb.mark()
    ioC = _io_C(Kb, False)
    stage_C(Kb, ioC, False)
    Kb.S.barrier()
    Kb.release(m0)
    ioA = _io_A(Kb, hT=ioC["hoT"])
    keys = [("d_ho", cb, th) for cb in range(16) for th in range(2)]
    stage_A(Kb, ioA, h_dep_keys=keys)
    Kb.S.emit()
    return Kb


def build_Cfinal():
    Kb = KB()
    io = _io_C(Kb, True)
    stage_C(Kb, io, True)
    Kb.S.emit()
    return Kb


_PROGS = {}


def _prog(name):
    if name not in _PROGS:
        _PROGS[name] = dict(A=build_A, B=build_B, CA=build_CA, CF=build_Cfinal)[name]()
    return _PROGS[name]


def _col(a, n):
    return np.ascontiguousarray(np.asarray(a, np.float32).reshape(n, 128).T)


def _run(name, in_maps):
    Kb = _prog(name)
    res = run_bass_kernel_spmd(Kb.nc, in_maps, core_ids=list(range(NCORES)))
    return res.results


def _A_inputs(inp, l, hT_list):
    w5 = np.ascontiguousarray(inp["w_in"][l][:, :5120])
    g1 = _col(inp["norm_mix_g"][l], 16)
    return [dict(hT=hT_list[c], g1=g1, w_in5=w5) for c in range(NCORES)]


def _B_inputs(inp, l, rA):
    maps = []
    for c in range(NCORES):
        b, s = c // 2, c % 2
        ch = slice(512 * s, 512 * (s + 1))
        m = {}
        m.update(s5_host_params(inp, l, s))
        m.update(lru_host_params(inp, l, s))
        m["uT"] = np.ascontiguousarray(np.concatenate([rA[2 * b]["uS5T"][ch], rA[2 * b + 1]["uS5T"][ch]], axis=1))
        m["uLT"] = np.ascontiguousarray(np.concatenate([rA[2 * b]["uLruT"][ch], rA[2 * b + 1]["uLruT"][ch]], axis=1))
        m["qT"] = np.ascontiguousarray(np.concatenate([rA[2 * b]["qT"][ch], rA[2 * b + 1]["qT"][ch]], axis=1))
        m["kT"] = np.ascontiguousarray(np.concatenate([rA[2 * b]["kT"][ch], rA[2 * b + 1]["kT"][ch]], axis=1))
        m["v"] = np.ascontiguousarray(np.concatenate([rA[2 * b]["v"][:, ch], rA[2 * b + 1]["v"][:, ch]], axis=0))
        maps.append(m)
    return maps


def _C_inputs(inp, l, hT_list, rA, rB, final):
    wts = dict(
        w_glu=np.ascontiguousarray(inp["s5_w_glu"][l]), b_glu=_col(inp["s5_b_glu"][l], 8),
        w_gate=np.ascontiguousarray(inp["w_in"][l][:, 5120:]), b_gate=_col(inp["b_gate"][l], 48),
        w_br=np.ascontiguousarray(np.stack([inp["w_br_s5"][l], inp["w_br_lru"][l], inp["w_br_attn"][l]])),
        w_out=np.ascontiguousarray(inp["w_out"][l]), g2=_col(inp["norm_mlp_g"][l], 16),
        w_up=np.ascontiguousarray(inp["w_up"][l]), w_down=np.ascontiguousarray(inp["w_down"][l]))
    if final:
        wts["gf"] = _col(inp["final_norm_g"], 16)
    maps = []
    for c in range(NCORES):
        b, s = c // 2, c % 2
        tk = slice(2048 * s, 2048 * (s + 1))
        m = dict(wts)
        m["hT"] = hT_list[c]
        m["xnT_in"] = rA[c]["xnT"]
        for nm in ("ysgT", "yLruT", "yAttT"):
            m[nm] = np.ascontiguousarray(np.concatenate([rB[2 * b][nm][:, tk], rB[2 * b + 1][nm][:, tk]], axis=0))
        maps.append(m)
    return maps


def kernel_unfused(**inp):
    inp = {k: np.asarray(v) for k, v in inp.items()}
    x = inp["x"]
    hT = [np.ascontiguousarray(x[c // 2, 2048 * (c % 2):2048 * (c % 2 + 1), :].T) for c in range(NCORES)]
    rA = _run("A", _A_inputs(inp, 0, hT))
    rB = _run("B", _B_inputs(inp, 0, rA))
    mC = _C_inputs(inp, 0, hT, rA, rB, False)
    aw = _A_inputs(inp, 1, hT)
    for c in range(NCORES):
        mC[c]["g1"] = aw[c]["g1"]
        mC[c]["w_in5"] = aw[c]["w_in5"]
    rCA = _run("CA", mC)
    hT1 = [r["hoT"] for r in rCA]
    rB1 = _run("B", _B_inputs(inp, 1, rCA))
    rF = _run("CF", _C_inputs(inp, 1, hT1, rCA, rB1, True))
    out = np.empty((NBATCH, L, D), np.float32)
    for c in range(NCORES):
        out[c // 2, 2048 * (c % 2):2048 * (c % 2 + 1), :] = np.asarray(rF[c]["hoT"], np.float32).T
    return out


S5_NAMES = ("s5col", "bT_re", "bT_im", "cT_re", "cT_im", "d_col")
S5_SHAPES = dict(s5col=[128, 3, 16], bT_re=[32, 16, 64], bT_im=[32, 16, 64], cT_re=[32, 64, 16], cT_im=[32, 64, 16], d_col=[128, 4])
LRU_SHAPES = dict(lru_col=[128, 4, 8], w_r=[8, 64, 64], w_i=[8, 64, 64])


def build_fused():
    Kb = KB()
    S = Kb.S
    xT = Kb.din("xT", [D, L], F32)
    outT = Kb.dout("outT", [D, L], F32)
    W = {}
    for l in range(2):
        W[l] = dict(
            g1=Kb.din("g1_%d" % l, [128, 16], F32), w_in=Kb.din("w_in_%d" % l, [D, 11264], F32),
            w_glu=Kb.din("w_glu_%d" % l, [1024, 1024], F32), b_glu=Kb.din("b_glu_%d" % l, [128, 8], F32),
            b_gate=Kb.din("b_gate_%d" % l, [128, 48], F32), w_br=Kb.din("w_br_%d" % l, [3, 1024, D], F32),
            w_out=Kb.din("w_out_%d" % l, [D, D], F32), g2=Kb.din("g2_%d" % l, [128, 16], F32),
            w_up=Kb.din("w_up_%d" % l, [D, DFF], F32), w_down=Kb.din("w_down_%d" % l, [DFF, D], F32))
        for s in range(2):
            for nm, shp in S5_SHAPES.items():
                W[l]["%s_%d" % (nm, s)] = Kb.din("%s_%d_%d" % (nm, l, s), shp, F32)
            for nm, shp in LRU_SHAPES.items():
                W[l]["%s_%d" % (nm, s)] = Kb.din("%s_%d_%d" % (nm, l, s), shp, F32)
    gf = Kb.din("gf", [128, 16], F32)
    sc = dict(
        xnT=Kb.dint("sc_xnT", [D, L], BF16), uS5T=Kb.dint("sc_uS5T", [1024, L], F32), uLruT=Kb.dint("sc_uLruT", [1024, L], F32),
        qT=Kb.dint("sc_qT", [1024, L], BF16), kT=Kb.dint("sc_kT", [1024, L], BF16), v=Kb.dint("sc_v", [L, 1024], BF16),
        ysgT=Kb.dint("sc_ysgT", [1024, L], F32), yLruT=Kb.dint("sc_yLruT", [1024, L], BF16), yAttT=Kb.dint("sc_yAttT", [1024, L], BF16),
        h1T=Kb.dint("sc_h1T", [D, TOK], F32), hidT=Kb.dint("sc_hidT", [DFF, TOK], BF16), hmid=Kb.dint("sc_hmid", [D, L], F32))
    m0 = Kb.mark()

    def sep():
        S.barrier()
        Kb.release(m0)

    h_in = xT
    for l in range(2):
        w = W[l]
        h_out = sc["hmid"] if l == 0 else outT
        for th in range(2):
            ts_ = slice(th * TOK, (th + 1) * TOK)
            stage_A(Kb, dict(hT=h_in[:, ts_], g1=w["g1"], w_in5=w["w_in"][:, 0:5120], xnT=sc["xnT"][:, ts_], uS5T=sc["uS5T"][:, ts_],
                             uLruT=sc["uLruT"][:, ts_], qT=sc["qT"][:, ts_], kT=sc["kT"][:, ts_], v=sc["v"][ts_, :]))
            sep()
        for s in range(2):
            ch = slice(512 * s, 512 * (s + 1))
            io = dict(uT=sc["uS5T"][ch, :], uLT=sc["uLruT"][ch, :], qT=sc["qT"][ch, :], kT=sc["kT"][ch, :], v=sc["v"][:, ch],
                      ysgT=sc["ysgT"][ch, :], yLruT=sc["yLruT"][ch, :], yAttT=sc["yAttT"][ch, :])
            for nm in S5_SHAPES:
                io[nm] = w["%s_%d" % (nm, s)]
            for nm in LRU_SHAPES:
                io[nm] = w["%s_%d" % (nm, s)]
            stage_B(Kb, io)
            sep()
        for th in range(2):
            ts_ = slice(th * TOK, (th + 1) * TOK)
            io = dict(hT=h_in[:, ts_], xnT=sc["xnT"][:, ts_], ysgT=sc["ysgT"][:, ts_], yLruT=sc["yLruT"][:, ts_], yAttT=sc["yAttT"][:, ts_],
                      w_glu=w["w_glu"], b_glu=w["b_glu"], w_gate=w["w_in"][:, 5120:11264], b_gate=w["b_gate"], w_br=w["w_br"], w_out=w["w_out"],
                      g2=w["g2"], w_up=w["w_up"], w_down=w["w_down"], h1T=sc["h1T"], hidT=sc["hidT"], hoT=h_out[:, ts_], gf=gf)
            stage_C(Kb, io, final_norm=(l == 1))
            sep()
        h_in = h_out
    S.emit()
    return Kb


def _fused_inputs(inp, b):
    m = dict(xT=np.ascontiguousarray(inp["x"][b].T), gf=_col(inp["final_norm_g"], 16))
    for l in range(2):
        m["g1_%d" % l] = _col(inp["norm_mix_g"][l], 16)
        m["w_in_%d" % l] = np.ascontiguousarray(inp["w_in"][l])
        m["w_glu_%d" % l] = np.ascontiguousarray(inp["s5_w_glu"][l])
        m["b_glu_%d" % l] = _col(inp["s5_b_glu"][l], 8)
        m["b_gate_%d" % l] = _col(inp["b_gate"][l], 48)
        m["w_br_%d" % l] = np.ascontiguousarray(np.stack([inp["w_br_s5"][l], inp["w_br_lru"][l], inp["w_br_attn"][l]]))
        m["w_out_%d" % l] = np.ascontiguousarray(inp["w_out"][l])
        m["g2_%d" % l] = _col(inp["norm_mlp_g"][l], 16)
        m["w_up_%d" % l] = np.ascontiguousarray(inp["w_up"][l])
        m["w_down_%d" % l] = np.ascontiguousarray(inp["w_down"][l])
        for s in range(2):
            for k_, v_ in s5_host_params(inp, l, s).items():
                m["%s_%d_%d" % (k_, l, s)] = v_
            for k_, v_ in lru_host_params(inp, l, s).items():
                m["%s_%d_%d" % (k_, l, s)] = v_
    return m


def kernel_fused(**inp):
    inp = {k: np.asarray(v) for k, v in inp.items()}
    if "F" not in _PROGS:
        _PROGS["F"] = build_fused()
    Kb = _PROGS["F"]
    per_b = [_fused_inputs(inp, b) for b in range(NBATCH)]
    in_maps = [per_b[c // 2] for c in range(NCORES)]
    res = run_bass_kernel_spmd(Kb.nc, in_maps, core_ids=list(range(NCORES)))
    out = np.empty((NBATCH, L, D), np.float32)
    for c in range(NCORES):
        b, s = c // 2, c % 2
        out[b, 2048 * s:2048 * (s + 1), :] = np.asarray(res.results[c]["outT"], np.float32)[:, 2048 * s:2048 * (s + 1)].T
    return out


FUSED = True


def kernel(**inp):
    return kernel_fused(**inp) if FUSED else kernel_unfused(**inp)
```

```python
import numpy as np
import ml_dtypes
import concourse.bass as bass
import concourse.mybir as mybir
from concourse.bass_utils import run_bass_kernel_spmd

F32 = mybir.dt.float32
BF16 = mybir.dt.bfloat16
AF = mybir.ActivationFunctionType
ALU = mybir.AluOpType

D = 2048
L = 4096
NBATCH = 4
TOK = 2048
DFF = 8192
EPS = 1e-6
NCORES = 8
QSCALE = 128.0 ** -0.5


class _Op:
    __slots__ = ("eng", "fn", "deps", "idx", "sig", "stream", "ndma", "slot", "cnt", "prev_cnt", "dur", "nbytes")


import heapq


class Sched:
    ENG_NAMES = ("pe", "act", "dve", "pool", "sp")
    RESCHEDULE = True

    def __init__(self, nc, dma_k=8):
        self.nc = nc
        self.ops = []
        self.lastw = {}
        self.readers = {}
        self.dma_k = dma_k
        self.streams = {}
        self.barrier_idx = None

    def op(self, eng, fn, reads=(), writes=(), stream=None, ndma=1, n=None, passes=1, nbytes=None):
        o = _Op()
        o.eng = eng
        o.fn = fn
        o.idx = len(self.ops)
        o.stream = stream
        o.ndma = ndma
        o.sig = False
        o.cnt = 0
        o.slot = 0
        o.prev_cnt = 0
        n = 512 if n is None else n
        if stream is not None:
            o.dur = 1500.0 if eng == "pool" else 60.0
            o.nbytes = float(nbytes if nbytes is not None else 262144)
        elif eng == "pe":
            o.dur = passes * max(n, 64) / 2.4 + 6.0
        elif eng == "act":
            o.dur = (n + 224) / 1.2
        elif eng == "dve":
            o.dur = (n + 151) / 0.96
        elif eng == "pool":
            o.dur = (1.3 * n + 200) / 1.2
        else:
            o.dur = 50.0
        deps = set()
        fresh = False
        for r in reads:
            w = self.lastw.get(r)
            if w is not None:
                deps.add(w)
            else:
                fresh = True
        for w_ in writes:
            w = self.lastw.get(w_)
            if w is not None:
                deps.add(w)
            else:
                fresh = True
            rd = self.readers.get(w_)
            if rd:
                deps.update(rd)
        if fresh and self.barrier_idx is not None:
            deps.add(self.barrier_idx)
        deps.discard(o.idx)
        o.deps = deps
        for r in reads:
            self.readers.setdefault(r, []).append(o.idx)
        for w_ in writes:
            self.lastw[w_] = o.idx
            self.readers[w_] = []
        self.ops.append(o)
        if stream is not None:
            self.streams.setdefault(stream, []).append(o.idx)
        return o

    def barrier(self):
        keys = list(self.lastw.keys())
        o = self.op("sp", lambda e: e.nop(), reads=keys, writes=keys)
        self.lastw = {}
        self.readers = {}
        self.barrier_idx = o.idx
        return o

    def _schedule(self):
        ops = self.ops
        n = len(ops)
        if not self.RESCHEDULE:
            per = {e: [] for e in self.ENG_NAMES}
            for o in ops:
                per[o.eng].append(o)
            return per
        succ = [[] for _ in range(n)]
        indeg = [0] * n
        for o in ops:
            indeg[o.idx] = len(o.deps)
            for d in o.deps:
                succ[d].append(o.idx)
        ready_t = [0.0] * n
        heaps = {e: [] for e in self.ENG_NAMES}
        avail = {e: [] for e in self.ENG_NAMES}
        free = {e: 0.0 for e in self.ENG_NAMES}
        for o in ops:
            if indeg[o.idx] == 0:
                heapq.heappush(heaps[o.eng], (0.0, o.idx))
        per = {e: [] for e in self.ENG_NAMES}
        dma_free = 0.0
        done = 0
        XLAT = 250.0
        while done < n:
            best = None
            for e in self.ENG_NAMES:
                h = heaps[e]
                av = avail[e]
                while h and h[0][0] <= free[e]:
                    heapq.heappush(av, heapq.heappop(h)[1])
                if av:
                    cand = (free[e], av[0], e, True)
                elif h:
                    cand = (h[0][0], h[0][1], e, False)
                else:
                    continue
                if best is None or cand[:2] < best[:2]:
                    best = cand
            start, idx, e, from_av = best
            if from_av:
                heapq.heappop(avail[e])
            else:
                heapq.heappop(heaps[e])
            o = ops[idx]
            per[e].append(o)
            if o.stream is not None:
                free[e] = start + o.dur
                dma_free = max(dma_free, start + o.dur) + o.nbytes / 160.0
                fin = dma_free + 2000.0
            else:
                fin = start + o.dur
                free[e] = fin
            done += 1
            for s_ in succ[idx]:
                so = ops[s_]
                lat = 0.0 if (so.eng == e and so.stream is None and o.stream is None) else XLAT
                if fin + lat > ready_t[s_]:
                    ready_t[s_] = fin + lat
                indeg[s_] -= 1
                if indeg[s_] == 0:
                    heapq.heappush(heaps[so.eng], (ready_t[s_], s_))
            self.sim_time = max(getattr(self, "sim_time", 0.0), fin)
        return per

    def emit(self):
        nc = self.nc
        ops = self.ops
        per_eng = self._schedule()
        pos = {}
        for e in self.ENG_NAMES:
            p = 0
            for o in per_eng[e]:
                if o.stream is None:
                    pos[o.idx] = p
                    p += 1
        for o in ops:
            nd = set()
            best = {}
            for d in o.deps:
                do = ops[d]
                if do.stream is not None:
                    nd.add(d)
                    continue
                if o.stream is None and do.eng == o.eng:
                    if o.eng == "pe" or pos[o.idx] - pos[d] >= 2:
                        continue
                if do.eng not in best or pos[d] > pos[best[do.eng]]:
                    best[do.eng] = d
            nd.update(best.values())
            o.deps = nd
            for d in nd:
                ops[d].sig = True
        esem = {e: nc.alloc_semaphore("s_" + e) for e in self.ENG_NAMES}
        ssem = {}
        for s in self.streams:
            ssem[s] = [nc.alloc_semaphore("d_%s_%d" % (s, k)) for k in range(self.dma_k)]
        ecnt = {e: 0 for e in self.ENG_NAMES}
        scnt = {s: [0] * self.dma_k for s in self.streams}
        spos = {s: 0 for s in self.streams}
        seng = {}
        for e in self.ENG_NAMES:
            for o in per_eng[e]:
                if o.stream is None:
                    if o.sig:
                        ecnt[e] += 1
                        o.cnt = ecnt[e]
                else:
                    assert seng.setdefault(o.stream, e) == e, "a DMA stream must be issued from one engine"
                    j = spos[o.stream]
                    spos[o.stream] += 1
                    o.slot = j % self.dma_k
                    o.prev_cnt = scnt[o.stream][o.slot]
                    scnt[o.stream][o.slot] += 16 * o.ndma
                    o.cnt = scnt[o.stream][o.slot]
        final_streams = {e: set() for e in self.ENG_NAMES}
        for o in ops:
            if o.stream is not None:
                final_streams[o.eng].add(o.stream)
        self.stats = {e: len(per_eng[e]) for e in self.ENG_NAMES}
        self.stats["sem_max"] = dict(ecnt)
        self.stats["sim_us"] = getattr(self, "sim_time", 0.0) / 1000.0

        def run_engine(ename, eng):
            waited = {}
            for o in per_eng[ename]:
                need = {}
                for d in o.deps:
                    do = ops[d]
                    if do.stream is None:
                        k = ("e", do.eng)
                        sem = esem[do.eng]
                    else:
                        k = ("s", do.stream, do.slot)
                        sem = ssem[do.stream][do.slot]
                    if need.get(k, (None, 0))[1] < do.cnt:
                        need[k] = (sem, do.cnt)
                if o.stream is not None:
                    k = ("s", o.stream, o.slot)
                    if o.prev_cnt > 0 and need.get(k, (None, 0))[1] < o.prev_cnt:
                        need[k] = (ssem[o.stream][o.slot], o.prev_cnt)
                for k, (sem, val) in need.items():
                    if val > 0 and waited.get(k, 0) < val:
                        waited[k] = val
                        eng.wait_ge(sem, val)
                if o.stream is not None:
                    insts = o.fn(eng)
                    if not isinstance(insts, (list, tuple)):
                        insts = [insts]
                    assert len(insts) == o.ndma, (len(insts), o.ndma)
                    for ins in insts:
                        ins.then_inc(ssem[o.stream][o.slot], 16)
                else:
                    ins = o.fn(eng)
                    if o.sig:
                        ins.then_inc(esem[ename], 1)
            for s in sorted(final_streams[ename]):
                for k in range(self.dma_k):
                    key = ("s", s, k)
                    if scnt[s][k] > 0 and waited.get(key, 0) < scnt[s][k]:
                        waited[key] = scnt[s][k]
                        eng.wait_ge(ssem[s][k], scnt[s][k])

        with nc.Block() as block:
            @block.tensor
            def _(e):
                run_engine("pe", e)

            @block.scalar
            def _(e):
                run_engine("act", e)

            @block.vector
            def _(e):
                run_engine("dve", e)

            @block.gpsimd
            def _(e):
                run_engine("pool", e)

            @block.sync
            def _(e):
                run_engine("sp", e)


class Rot:
    def __init__(self, items):
        self.items = list(items)
        self.i = 0

    def next(self):
        it = self.items[self.i % len(self.items)]
        self.i += 1
        return it


class KB:
    def __init__(self):
        if KB.NUM_DEVICES is None:
            self.nc = bass.Bass("TRN2", target_bir_lowering=False)
        else:
            self.nc = bass.Bass("TRN2", target_bir_lowering=False, num_devices=KB.NUM_DEVICES)
        self.S = Sched(self.nc)
        self.ps = [(self.nc.alloc_psum_tensor("ps%d" % i, [128, 512], F32), "ps%d" % i) for i in range(8)]
        self.psrot = Rot(self.ps)
        self._n = 0
        self.consts = {}
        self.ones_f32()
        self.one_col()
        self.const_col("c_ln2", float(np.log(2.0)))

    ARENA_WORDS = 52000
    NUM_DEVICES = None

    def sb(self, name, shape, dt):
        if not hasattr(self, "arena"):
            self.arena = self.nc.alloc_sbuf_tensor("arena", [128, self.ARENA_WORDS], F32)
            self.aoff = 0
        assert shape[0] == 128
        elems = int(np.prod(shape[1:]))
        words = (elems * (4 if dt == F32 else 2) + 3) // 4
        words = (words + 7) // 8 * 8
        assert self.aoff + words <= self.ARENA_WORDS, ("SBUF arena overflow", name, self.aoff, words)
        v = self.arena[:, self.aoff:self.aoff + words]
        self.aoff += words
        if dt != F32:
            v = v.bitcast(dt)
        v = v[:, 0:elems]
        if len(shape) > 2:
            names = "abcd"[:len(shape) - 1]
            pat = "p (%s) -> p %s" % (" ".join(names), " ".join(names))
            v = v.rearrange(pat, **{n: int(sz) for n, sz in zip(names[:-1], shape[1:-1])})
        return v

    def mark(self):
        return getattr(self, "aoff", 0)

    def release(self, mark):
        self.aoff = mark

    def din(self, name, shape, dt):
        return self.nc.dram_tensor(name, list(shape), dt, kind="ExternalInput").ap()

    def dout(self, name, shape, dt):
        return self.nc.dram_tensor(name, list(shape), dt, kind="ExternalOutput").ap()

    def dint(self, name, shape, dt):
        return self.nc.dram_tensor(name, list(shape), dt, kind="Internal").ap()

    def const_col(self, key, val):
        if key not in self.consts:
            t = self.sb(key, [128, 8], F32)
            self.S.op("pool", lambda e: e.memset(t[:], val), writes=[key])
            self.consts[key] = t
        return self.consts[key][:, 0:1]

    def one_col(self):
        if "one_col" not in self.consts:
            t = self.sb("c_onecol", [128, 1], F32)
            self.S.op("pool", lambda e: e.memset(t[:], 1.0), writes=["c_onecol"])
            self.consts["one_col"] = t
        return self.consts["one_col"][:, 0:1]

    def ones_f32(self):
        if "ones" not in self.consts:
            t = self.sb("c_ones", [128, 128], F32)
            self.S.op("pool", lambda e: e.memset(t[:], 1.0), writes=["c_ones"])
            self.consts["ones"] = t
        return self.consts["ones"]


def load_cast(Kb, stream, out_ap, w_ap, kt_n, wkey, extra_reads=()):
    S = Kb.S
    pieces = []
    step = 4
    for k0 in range(0, kt_n, step):
        k1 = min(kt_n, k0 + step)
        pieces.append((k0, k1))

    def fn(e):
        r = []
        for (k0, k1) in pieces:
            r.append(e.dma_start(out=out_ap[:, k0:k1, :],
                                 in_=w_ap[k0 * 128:k1 * 128, :].rearrange("(kt p) c -> p kt c", p=128)))
        return r

    return S.op("pool", fn, reads=list(extra_reads), writes=[wkey], stream=stream, ndma=len(pieces),
                nbytes=4 * 128 * kt_n * int(out_ap.shape[-1]))


class WStream:
    def __init__(self, Kb, name, nbuf, kt, cols):
        self.Kb = Kb
        self.name = name
        self.bufs = [(Kb.sb("%s_%d" % (name, i), [128, kt, cols], BF16), "%s_%d" % (name, i)) for i in range(nbuf)]
        self.rot = Rot(self.bufs)

    def load(self, w_ap, kt_n, ncols):
        buf, key = self.rot.next()
        load_cast(self.Kb, self.name, buf[:, 0:kt_n, 0:ncols], w_ap, kt_n, key)
        return buf, key


def mm_group(S, ps_ap, ps_key, pairs, first=True, last=True):
    n = len(pairs)
    for i, (l, r, keys) in enumerate(pairs):
        S.op("pe", lambda e, l=l, r=r, st=(first and i == 0), sp=(last and i == n - 1):
             e.matmul(ps_ap, lhsT=l, rhs=r, start=st, stop=sp), reads=keys, writes=[ps_key], n=_fsz(r), passes=(4 if r.dtype == F32 else 1))


def make_nrm(Kb, tag):
    S = Kb.S
    c = dict(
        sq=Rot([(Kb.sb("n%s_sq%d" % (tag, i), [128, 512], F32), "n%s_sq%d" % (tag, i)) for i in range(3)]),
        ln=(Kb.sb("n%s_ln" % tag, [128, 512], F32), "n%s_ln" % tag),
        rstd=(Kb.sb("n%s_rstd" % tag, [128, 512], F32), "n%s_rstd" % tag),
        eps=(Kb.sb("n%s_eps" % tag, [128, 8], F32), "n%s_eps" % tag),
    )
    et = c["eps"][0]
    S.op("pool", lambda e: e.memset(et[:], EPS), writes=[c["eps"][1]])
    return c


def rmsnorm_tile(Kb, hb, hkey, g_sb, gkey, out_fn, out_keys_fn, c):
    S = Kb.S
    ones = Kb.ones_f32()
    ps, pkey = Kb.psrot.next()
    for ft in range(16):
        s, skey = c["sq"].next()
        S.op("act", lambda e, s=s, ft=ft: e.activation(out=s[:], in_=hb[:, ft, :], func=AF.Square),
             reads=[hkey], writes=[skey])
        S.op("pe", lambda e, s=s, ft=ft: e.matmul(ps[:], lhsT=ones[:], rhs=s[:], start=(ft == 0), stop=(ft == 15)),
             reads=[skey, "c_ones"], writes=[pkey])
    ln, lkey = c["ln"]
    rstd, rkey = c["rstd"]
    et = c["eps"][0]
    S.op("act", lambda e: e.activation(out=ln[:], in_=ps[:], func=AF.Ln, bias=et[:, 0:1], scale=1.0 / D),
         reads=[pkey, c["eps"][1]], writes=[lkey])
    S.op("act", lambda e: e.activation(out=rstd[:], in_=ln[:], func=AF.Exp, scale=-0.5),
         reads=[lkey], writes=[rkey])
    for ft in range(16):
        S.op("dve", lambda e, ft=ft: e.scalar_tensor_tensor(out=out_fn(ft), in0=hb[:, ft, :], scalar=g_sb[:, ft:ft + 1],
                                                            in1=rstd[:], op0=ALU.mult, op1=ALU.mult),
             reads=[hkey, rkey, gkey], writes=out_keys_fn(ft))


def stage_A(Kb, io, h_dep_keys=()):
    S = Kb.S
    xn = Kb.sb("A_xn", [128, 16, TOK], BF16)
    hbuf = [(Kb.sb("A_h%d" % i, [128, 16, 512], F32), "A_h%d" % i) for i in range(1)]
    g_sb = Kb.sb("A_g", [128, 16], F32)
    nrmA = make_nrm(Kb, "A")
    S.op("sp", lambda e: e.dma_start(out=g_sb[:], in_=io["g1"]), writes=["A_g"], stream="ld")
    hT = io["hT"]
    for tt in range(4):
        hb, hkey = hbuf[0]

        def ld(e, hb=hb, tt=tt):
            r = []
            for f0 in range(0, 16, 4):
                r.append(e.dma_start(out=hb[:, f0:f0 + 4, :],
                                     in_=hT[f0 * 128:(f0 + 4) * 128, tt * 512:(tt + 1) * 512].rearrange("(ft p) t -> p ft t", p=128)))
            return r
        S.op("sp", ld, reads=list(h_dep_keys), writes=[hkey], stream="ld", ndma=4)
        rmsnorm_tile(Kb, hb, hkey, g_sb, "A_g",
                     lambda ft, tt=tt: xn[:, ft, tt * 512:(tt + 1) * 512],
                     lambda ft, tt=tt: [("A_xn", tt)], nrmA)
    xkeys = [("A_xn", tt) for tt in range(4)]

    def st_xn(e):
        r = []
        for f0 in range(0, 16, 4):
            r.append(e.dma_start(out=io["xnT"][f0 * 128:(f0 + 4) * 128, :].rearrange("(ft p) t -> p ft t", p=128),
                                 in_=xn[:, f0:f0 + 4, :]))
        return r
    S.op("sp", st_xn, reads=xkeys, writes=["d_xnT"], stream="st", ndma=4)

    ws = WStream(Kb, "wA", 3, 16, 512)
    stF = Rot([(Kb.sb("A_stF%d" % i, [128, TOK], F32), "A_stF%d" % i) for i in range(2)])
    stB = Rot([(Kb.sb("A_stB%d" % i, [128, TOK], BF16), "A_stB%d" % i) for i in range(2)])
    w5 = io["w_in5"]
    ev = 0
    for s in range(8):
        wb, wkey = ws.load(w5[:, s * 512:(s + 1) * 512], 16, 512)
        for cbl in range(4):
            cb = s * 4 + cbl
            if cb < 16:
                st, stkey = stF.next()
            else:
                st, stkey = stB.next()
            scale = QSCALE if 16 <= cb < 24 else 1.0
            for tt in range(4):
                ps, pkey = Kb.psrot.next()
                mm_group(S, ps[:], pkey, [(wb[:, kt, cbl * 128:(cbl + 1) * 128], xn[:, kt, tt * 512:(tt + 1) * 512],
                                           [wkey, ("A_xn", tt)]) for kt in range(16)])
                o_ap = st[:, tt * 512:(tt + 1) * 512]
                if ev % 2 == 0:
                    S.op("act", lambda e, o=o_ap, ps=ps, sc=scale: e.activation(out=o, in_=ps[:], func=AF.Copy, scale=sc),
                         reads=[pkey], writes=[stkey])
                else:
                    S.op("dve", lambda e, o=o_ap, ps=ps, sc=scale: e.tensor_scalar(out=o, in0=ps[:], scalar1=sc, scalar2=None, op0=ALU.mult),
                         reads=[pkey], writes=[stkey])
                ev += 1
            if cb < 8:
                dst = io["uS5T"][cb * 128:(cb + 1) * 128, :]
            elif cb < 16:
                dst = io["uLruT"][(cb - 8) * 128:(cb - 7) * 128, :]
            elif cb < 24:
                dst = io["qT"][(cb - 16) * 128:(cb - 15) * 128, :]
            else:
                dst = io["kT"][(cb - 24) * 128:(cb - 23) * 128, :]
            S.op("sp", lambda e, dst=dst, st=st: e.dma_start(out=dst, in_=st[:]), reads=[stkey], writes=[("d_A", cb)], stream="st")
    vs = []
    for s in range(2):
        vs.append(ws.load(w5[:, 4096 + s * 512:4096 + (s + 1) * 512], 16, 512))
    stV = Rot([(Kb.sb("A_stV%d" % i, [128, 1024], BF16), "A_stV%d" % i) for i in range(2)])
    for tb in range(16):
        st, stkey = stV.next()
        for s in range(2):
            wb, wkey = vs[s]
            ps, pkey = Kb.psrot.next()
            mm_group(S, ps[:], pkey, [(xn[:, kt, tb * 128:(tb + 1) * 128], wb[:, kt, :], [wkey, ("A_xn", tb // 4)])
                                      for kt in range(16)])
            o_ap = st[:, s * 512:(s + 1) * 512]
            if ev % 2 == 0:
                S.op("act", lambda e, o=o_ap, ps=ps: e.activation(out=o, in_=ps[:], func=AF.Copy), reads=[pkey], writes=[stkey])
            else:
                S.op("dve", lambda e, o=o_ap, ps=ps: e.tensor_copy(out=o, in_=ps[:]), reads=[pkey], writes=[stkey])
            ev += 1
        S.op("sp", lambda e, st=st, tb=tb: e.dma_start(out=io["v"][tb * 128:(tb + 1) * 128, :], in_=st[:]),
             reads=[stkey], writes=[("d_v", tb)], stream="st")


def build_A():
    Kb = KB()
    io = dict(
        hT=Kb.din("hT", [D, TOK], F32), g1=Kb.din("g1", [128, 16], F32), w_in5=Kb.din("w_in5", [D, 5120], F32),
        xnT=Kb.dout("xnT", [D, TOK], BF16), uS5T=Kb.dout("uS5T", [1024, TOK], F32), uLruT=Kb.dout("uLruT", [1024, TOK], F32),
        qT=Kb.dout("qT", [1024, TOK], BF16), kT=Kb.dout("kT", [1024, TOK], BF16), v=Kb.dout("v", [TOK, 1024], BF16),
    )
    stage_A(Kb, io)
    Kb.S.emit()
    return Kb


TC = 256
NCH = L // TC
PI = float(np.pi)


def _fsz(ap):
    return int(np.prod(ap.shape[1:]))


def _ts(S, eng, out, in0, s1, s2, op0, op1, reads, writes):
    if s2 is None:
        S.op(eng, lambda e: e.tensor_scalar(out=out, in0=in0, scalar1=s1, scalar2=None, op0=op0), reads=reads, writes=writes, n=_fsz(out))
    else:
        S.op(eng, lambda e: e.tensor_scalar(out=out, in0=in0, scalar1=s1, scalar2=s2, op0=op0, op1=op1), reads=reads, writes=writes, n=_fsz(out))


def _tt(S, eng, out, in0, in1, op, reads, writes):
    S.op(eng, lambda e: e.tensor_tensor(out=out, in0=in0, in1=in1, op=op), reads=reads, writes=writes, n=_fsz(out))


def _stt(S, eng, out, in0, scalar, in1, op0, op1, reads, writes):
    S.op(eng, lambda e: e.scalar_tensor_tensor(out=out, in0=in0, scalar=scalar, in1=in1, op0=op0, op1=op1),
         reads=reads, writes=writes, n=_fsz(out))


def _act(S, out, in_, func, reads, writes, scale=1.0, bias=None):
    if bias is None:
        S.op("act", lambda e: e.activation(out=out, in_=in_, func=func, scale=scale), reads=reads, writes=writes, n=_fsz(out))
    else:
        S.op("act", lambda e: e.activation(out=out, in_=in_, func=func, scale=scale, bias=bias), reads=reads, writes=writes, n=_fsz(out))


MAGIC = 12582912.0


def _sincos(S, eng, ang, cs, sn, tmp, key_ang, key_cs, key_sn, key_tmp):
    _ts(S, eng, tmp, ang, 1.0 / (2 * PI), MAGIC, ALU.mult, ALU.add, [key_ang], [key_tmp])
    _ts(S, eng, tmp, tmp, -MAGIC, -2 * PI, ALU.add, ALU.mult, [key_tmp], [key_tmp])
    _tt(S, eng, tmp, tmp, ang, ALU.add, [key_tmp, key_ang], [key_tmp])
    _act(S, sn, tmp, AF.Sin, [key_tmp], [key_sn])
    _ts(S, eng, cs, ang, 0.5 * PI, None, ALU.add, None, [key_ang], [key_cs])
    _ts(S, eng, tmp, cs, 1.0 / (2 * PI), MAGIC, ALU.mult, ALU.add, [key_cs, key_sn], [key_tmp])
    _ts(S, eng, tmp, tmp, -MAGIC, -2 * PI, ALU.add, ALU.mult, [key_tmp], [key_tmp])
    _tt(S, eng, tmp, tmp, cs, ALU.add, [key_tmp, key_cs], [key_tmp])
    _act(S, cs, tmp, AF.Sin, [key_tmp], [key_cs])


def _horner(S, eng, out, z, coeffs, tmpkey, zkey, okey):
    _ts(S, eng, out, z, float(coeffs[0]), float(coeffs[1]), ALU.mult, ALU.add, [zkey], [okey])
    for c in coeffs[2:]:
        _tt(S, eng, out, out, z, ALU.mult, [okey, zkey], [okey])
        _ts(S, eng, out, out, float(c), None, ALU.add, None, [okey], [okey])


def _sincos_acc(S, eng, ang, cs, sn, t0, t1, t2, key):
    _ts(S, eng, t0, ang, 1.0 / (2 * PI), MAGIC, ALU.mult, ALU.add, [key], [key])
    _ts(S, eng, t0, t0, -MAGIC, -2 * PI, ALU.add, ALU.mult, [key], [key])
    _tt(S, eng, t0, t0, ang, ALU.add, [key], [key])
    _ts(S, eng, t0, t0, 0.125, None, ALU.mult, None, [key], [key])
    _tt(S, eng, t1, t0, t0, ALU.mult, [key], [key])
    _horner(S, eng, sn, t1, [1.0 / 362880, -1.0 / 5040, 1.0 / 120, -1.0 / 6, 1.0], key, key, key)
    _tt(S, eng, sn, sn, t0, ALU.mult, [key], [key])
    _horner(S, eng, cs, t1, [-1.0 / 3628800, 1.0 / 40320, -1.0 / 720, 1.0 / 24, -0.5, 1.0], key, key, key)
    for _ in range(3):
        _tt(S, eng, t0, cs, sn, ALU.mult, [key], [key])
        _tt(S, eng, t1, sn, sn, ALU.mult, [key], [key])
        _tt(S, eng, t2, cs, cs, ALU.mult, [key], [key])
        _tt(S, eng, cs, t2, t1, ALU.subtract, [key], [key])
        _ts(S, eng, sn, t0, 2.0, None, ALU.mult, None, [key], [key])


def s5_mixer(Kb, io, psb2, psy, pset):
    S = Kb.S
    sb = Kb.sb
    cw = sb("s5_cw", [128, 24, 16], F32)
    Ec = sb("s5_Ec", [128, 16, TC], F32)
    Es = sb("s5_Es", [128, 16, TC], F32)
    Pw = sb("s5_P", [128, 4, 16], F32)
    Bb = sb("s5_Bb", [128, 2, 16, 128], BF16)
    Cb = sb("s5_Cb", [128, 2, 16, 128], BF16)
    dcol = sb("s5_d", [128, 4], F32)
    G = sb("s5_G", [128, 2, 16], F32)
    gl_t = sb("s5_gl", [128, 2, 4], F32)
    yield "persist"
    col = sb("s5_col", [128, 3, 16], F32)
    tb = sb("s5_tb", [128, 2, 16, TC // 2], F32)
    Bz = sb("s5_Bz", [128, 2, 16, 128], F32)
    tB = sb("s5_tB", [128, 2, 128], F32)
    Cz = sb("s5_Cz", [128, 2, 16, 128], F32)
    ident = sb("s5_id", [128, 128], F32)
    dg = sb("s5_dg", [128, 2, 128], F32)
    ones = Kb.ones_f32()
    S.op("pool", lambda e: e.memset(ident[:], 1.0), writes=["s5_id"])
    S.op("pool", lambda e: e.affine_select(out=ident[:], in_=ident[:], pattern=[[-1, 128]], compare_op=ALU.is_equal, fill=0.0,
                                           base=0, channel_multiplier=1), reads=["s5_id"], writes=["s5_id"])
    S.op("sp", lambda e: e.dma_start(out=col[:], in_=io["s5col"]), writes=["s5_col"], stream="ld")
    k = "s5_cw"
    LR, LI = col[:, 0, :], col[:, 1, :]
    DT, AR, TH, RR, C1, S1 = (cw[:, i, :] for i in range(6))
    T0, T1, T2, LBR, LBI, DEN, KR, KI = (cw[:, i, :] for i in range(6, 14))
    _act(S, DT, col[:, 2, :], AF.Exp, ["s5_col"], [k])
    _tt(S, "dve", AR, LR, DT, ALU.mult, ["s5_col", k], [k])
    _tt(S, "dve", TH, LI, DT, ALU.mult, ["s5_col", k], [k])
    _horner(S, "dve", RR, AR, [1.0 / 720, 1.0 / 120, 1.0 / 24, 1.0 / 6, 0.5, 1.0, 1.0], k, k, k)
    _sincos_acc(S, "dve", TH, C1, S1, T0, T1, T2, k)
    _tt(S, "dve", LBR, RR, C1, ALU.mult, [k], [k])
    _tt(S, "dve", LBI, RR, S1, ALU.mult, [k], [k])
    _ts(S, "dve", LBR, LBR, -1.0, None, ALU.add, None, [k], [k])
    _tt(S, "dve", T0, LR, LR, ALU.mult, ["s5_col", k], [k])
    _tt(S, "dve", T1, LI, LI, ALU.mult, ["s5_col", k], [k])
    _tt(S, "dve", DEN, T0, T1, ALU.add, [k], [k])
    S.op("dve", lambda e: e.reciprocal(out=DEN, in_=DEN), reads=[k], writes=[k])
    _tt(S, "dve", T0, LBR, LR, ALU.mult, ["s5_col", k], [k])
    _tt(S, "dve", T1, LBI, LI, ALU.mult, ["s5_col", k], [k])
    _tt(S, "dve", KR, T0, T1, ALU.add, [k], [k])
    _tt(S, "dve", KR, KR, DEN, ALU.mult, [k], [k])
    _tt(S, "dve", T0, LBI, LR, ALU.mult, ["s5_col", k], [k])
    _tt(S, "dve", T1, LBR, LI, ALU.mult, ["s5_col", k], [k])
    _tt(S, "dve", KI, T0, T1, ALU.subtract, [k], [k])
    _tt(S, "dve", KI, KI, DEN, ALU.mult, [k], [k])
    kE = "s5_E"
    kP = "s5_P"
    S.op("pool", lambda e: e.memset(Ec[:, :, 0:1], 1.0), writes=[kE])
    S.op("pool", lambda e: e.memset(Es[:, :, 0:1], 0.0), writes=[kE])
    S.op("dve", lambda e: e.tensor_copy(out=Pw[:, 0, :], in_=C1), reads=[k], writes=[kP])
    S.op("dve", lambda e: e.tensor_copy(out=Pw[:, 1, :], in_=S1), reads=[k], writes=[kP])
    ln = 1
    while ln < TC:
        cPb = Pw[:, 0:1, :].rearrange("p o j -> p j o").to_broadcast([128, 16, ln])
        sPb = Pw[:, 1:2, :].rearrange("p o j -> p j o").to_broadcast([128, 16, ln])
        t0 = tb[:, 0, :, 0:ln]
        t1 = tb[:, 1, :, 0:ln]
        _tt(S, "dve", t0, Ec[:, :, 0:ln], cPb, ALU.mult, [kE, kP], ["s5_tb"])
        _tt(S, "dve", t1, Es[:, :, 0:ln], sPb, ALU.mult, [kE, kP], ["s5_tb"])
        _tt(S, "dve", Ec[:, :, ln:2 * ln], t0, t1, ALU.subtract, ["s5_tb"], [kE])
        _tt(S, "dve", t0, Ec[:, :, 0:ln], sPb, ALU.mult, [kE, kP], ["s5_tb"])
        _tt(S, "dve", t1, Es[:, :, 0:ln], cPb, ALU.mult, [kE, kP], ["s5_tb"])
        _tt(S, "dve", Es[:, :, ln:2 * ln], t0, t1, ALU.add, ["s5_tb"], [kE])
        _tt(S, "dve", Pw[:, 2, :], Pw[:, 0, :], Pw[:, 1, :], ALU.mult, [kP], [kP])
        _tt(S, "dve", Pw[:, 3, :], Pw[:, 1, :], Pw[:, 1, :], ALU.mult, [kP], [kP])
        _tt(S, "dve", Pw[:, 0, :], Pw[:, 0, :], Pw[:, 0, :], ALU.mult, [kP], [kP])
        _tt(S, "dve", Pw[:, 0, :], Pw[:, 0, :], Pw[:, 3, :], ALU.subtract, [kP], [kP])
        _ts(S, "dve", Pw[:, 1, :], Pw[:, 2, :], 2.0, None, ALU.mult, None, [kP], [kP])
        ln *= 2
    S.op("pool", lambda e: e.memset(Bz[:], 0.0), writes=["s5_Bz"])

    def ldB(e):
        r = []
        for ri, nm in enumerate(("bT_re", "bT_im")):
            for j in range(16):
                jj = j % 4
                for gl in range(2):
                    g = 2 * j + gl
                    p0 = (2 * jj + gl) * 16
                    r.append(e.dma_start(out=Bz[p0:p0 + 16, ri, j, gl * 64:(gl + 1) * 64], in_=io[nm][g]))
        return r
    S.op("sp", ldB, reads=[], writes=["s5_Bz"], stream="ld", ndma=64)
    pk, pkkey = pset
    for j in range(16):
        _ts(S, "dve", dg[:, 0, :], ident[:], cw[:, 12, j:j + 1], None, ALU.mult, None, ["s5_id", k], ["s5_dg"])
        _ts(S, "dve", dg[:, 1, :], ident[:], cw[:, 13, j:j + 1], None, ALU.mult, None, ["s5_id", k], ["s5_dg"])
        S.op("pe", lambda e: e.matmul(pk[:, 0:128], lhsT=ones[:], rhs=dg[:, 0, :], start=True, stop=True), reads=["c_ones", "s5_dg"], writes=[pkkey])
        S.op("pe", lambda e: e.matmul(pk[:, 128:256], lhsT=ones[:], rhs=dg[:, 1, :], start=True, stop=True), reads=["c_ones", "s5_dg"], writes=[pkkey])
        kr_, ki_ = pk[:, 0:128], pk[:, 128:256]
        _tt(S, "dve", tB[:, 0, :], kr_, Bz[:, 0, j, :], ALU.mult, ["s5_Bz", pkkey], ["s5_tB"])
        _tt(S, "dve", tB[:, 1, :], ki_, Bz[:, 1, j, :], ALU.mult, ["s5_Bz", pkkey], ["s5_tB"])
        _tt(S, "dve", Bb[:, 0, j, :], tB[:, 0, :], tB[:, 1, :], ALU.subtract, ["s5_tB"], [("s5_Bb", j)])
        _tt(S, "dve", tB[:, 0, :], ki_, Bz[:, 0, j, :], ALU.mult, ["s5_Bz", pkkey], ["s5_tB"])
        _tt(S, "dve", tB[:, 1, :], kr_, Bz[:, 1, j, :], ALU.mult, ["s5_Bz", pkkey], ["s5_tB"])
        _tt(S, "dve", Bb[:, 1, j, :], tB[:, 0, :], tB[:, 1, :], ALU.add, ["s5_tB"], [("s5_Bb", j)])
    S.op("pool", lambda e: e.memset(Cz[:], 0.0), writes=["s5_Cz"])

    def ldC(e):
        r = []
        for ri, nm in enumerate(("cT_re", "cT_im")):
            for j in range(16):
                jj = j % 4
                for gl in range(2):
                    g = 2 * j + gl
                    c0 = (2 * jj + gl) * 16
                    r.append(e.dma_start(out=Cz[gl * 64:(gl + 1) * 64, ri, j, c0:c0 + 16], in_=io[nm][g]))
        return r
    S.op("sp", ldC, reads=[], writes=["s5_Cz"], stream="ld", ndma=64)
    _ts(S, "pool", Cb[:, 0, :, :], Cz[:, 0, :, :], 0.5, None, ALU.mult, None, ["s5_Cz"], ["s5_Cb"])
    _ts(S, "pool", Cb[:, 1, :, :], Cz[:, 1, :, :], -0.5, None, ALU.mult, None, ["s5_Cz"], ["s5_Cb"])
    S.op("sp", lambda e: e.dma_start(out=dcol[:], in_=io["d_col"]), writes=["s5_d"], stream="ld")
    _ts(S, "pool", dcol[:], dcol[:], 0.5, None, ALU.mult, None, ["s5_d"], ["s5_d"])
    S.op("pool", lambda e: e.memset(G[:], 0.0), writes=[("s5_G", j) for j in range(16)])
    if "dbg_cw" in io:
        S.op("sp", lambda e: e.dma_start(out=io["dbg_cw"], in_=cw[:]), reads=[k], writes=["dbg1"], stream="st")
        S.op("sp", lambda e: e.dma_start(out=io["dbg_Ec"], in_=Ec[:]), reads=[kE], writes=["dbg2"], stream="st")
        S.op("sp", lambda e: e.dma_start(out=io["dbg_Es"], in_=Es[:]), reads=[kE], writes=["dbg3"], stream="st")
        S.op("sp", lambda e: e.dma_start(out=io["dbg_P"], in_=Pw[:]), reads=[kP], writes=["dbg4"], stream="st")
        S.op("sp", lambda e: e.dma_start(out=io["dbg_Bb"], in_=Bb[:]), reads=[("s5_Bb", j) for j in range(16)], writes=["dbg6"], stream="st")
        S.op("sp", lambda e: e.dma_start(out=io["dbg_Cb"], in_=Cb[:]), reads=["s5_Cb"], writes=["dbg7"], stream="st")
    yield "setup"
    ubuf = Rot([(sb("s5_u%d" % i, [128, 4, TC], F32), "s5_u%d" % i) for i in range(2)])
    ubb = Rot([(sb("s5_ub%d" % i, [128, 4, TC], BF16), "s5_ub%d" % i) for i in range(2)])
    wk = [(sb("s5_w%d" % i, [128, 8, TC], F32), "s5_w%d" % i) for i in range(2)]
    hb = Rot([(sb("s5_h%d" % i, [128, 2, TC], BF16), "s5_h%d" % i) for i in range(2)])
    ep = Rot([(sb("s5_e%d" % i, [128, 4, TC], F32), "s5_e%d" % i) for i in range(2)])
    uT = io["uT"]
    rcol = cw[:, 3, :]
    ln2 = Kb.const_col("c_ln2", float(np.log(2.0)))
    onec = Kb.one_col()
    for c in range(NCH):
        u, ukey = ubuf.next()
        ub, ubkey = ubb.next()
        S.op("sp", lambda e, u=u, c=c: e.dma_start(out=u[:], in_=uT[:, c * TC:(c + 1) * TC].rearrange("(q p) t -> p q t", p=128)),
             writes=[ukey], stream="ld")
        S.op("pool", lambda e, u=u, ub=ub: e.tensor_copy(out=ub[:], in_=u[:]), reads=[ukey], writes=[ubkey])
        for q in range(4):
            py, pykey = psy
            pyv = py[:, 0:TC]
            pyk = pykey
            for jp in range(2):
                js = [4 * q + 2 * jp, 4 * q + 2 * jp + 1]
                ctx = []
                for i, j in enumerate(js):
                    pb, pbkey = psb2[i]
                    S.op("pe", lambda e, pb=pb, ub=ub, j=j, q=q: e.matmul(pb[:, 0:TC], lhsT=Bb[:, 0, j, :], rhs=ub[:, q, :], start=True, stop=True),
                         reads=[("s5_Bb", j), ubkey], writes=[pbkey])
                    S.op("pe", lambda e, pb=pb, ub=ub, j=j, q=q: e.matmul(pb[:, TC:2 * TC], lhsT=Bb[:, 1, j, :], rhs=ub[:, q, :], start=True, stop=True),
                         reads=[("s5_Bb", j), ubkey], writes=[pbkey])
                    w, wkey = wk[i]
                    h, hkey = hb.next()
                    ctx.append(dict(j=j, pb=pb, pbkey=pbkey, w=w, wkey=wkey, h=h, hkey=hkey, cs=Ec[:, j, :], sn=Es[:, j, :],
                                    bre=pb[:, 0:TC], bim=pb[:, TC:2 * TC], glk=("s5_gl", i), gl=gl_t[:, i, :]))

                def both(f):
                    for x in ctx:
                        f(x)
                both(lambda x: _tt(S, "dve", x["w"][:, 0, :], x["bre"], x["cs"], ALU.mult, [x["pbkey"], kE], [(x["wkey"], 0)]))
                both(lambda x: _tt(S, "dve", x["w"][:, 1, :], x["bim"], x["sn"], ALU.mult, [x["pbkey"], kE], [(x["wkey"], 1)]))
                both(lambda x: _tt(S, "dve", x["w"][:, 2, :], x["bim"], x["cs"], ALU.mult, [x["pbkey"], kE], [(x["wkey"], 2)]))
                both(lambda x: _tt(S, "dve", x["w"][:, 3, :], x["bre"], x["sn"], ALU.mult, [x["pbkey"], kE], [(x["wkey"], 3)]))
                both(lambda x: _tt(S, "dve", x["w"][:, 0, :], x["w"][:, 0, :], x["w"][:, 1, :], ALU.add, [(x["wkey"], 0), (x["wkey"], 1)], [(x["wkey"], 0)]))
                both(lambda x: _tt(S, "dve", x["w"][:, 2, :], x["w"][:, 2, :], x["w"][:, 3, :], ALU.subtract, [(x["wkey"], 2), (x["wkey"], 3)], [(x["wkey"], 2)]))

                def scan(x, o, i_, gi):
                    j = x["j"]
                    w = x["w"]
                    rb = rcol[:, j:j + 1].to_broadcast([128, TC])
                    S.op("dve", lambda e: e.tensor_tensor_scan(out=w[:, o, :], data0=rb, data1=w[:, i_, :], initial=G[:, gi, j:j + 1],
                                                               op0=ALU.mult, op1=ALU.add),
                         reads=[(x["wkey"], i_), k, ("s5_G", j)], writes=[(x["wkey"], o)])
                both(lambda x: scan(x, 4, 0, 0))
                both(lambda x: scan(x, 5, 2, 1))

                def carry(x):
                    j = x["j"]
                    w = x["w"]
                    gl = x["gl"]
                    glk = x["glk"]
                    gre_l, gim_l = w[:, 4, TC - 1:TC], w[:, 5, TC - 1:TC]
                    rk = [(x["wkey"], 4), (x["wkey"], 5), kP]
                    _tt(S, "pool", gl[:, 0:1], gim_l, Pw[:, 1, j:j + 1], ALU.mult, rk, [glk])
                    _tt(S, "pool", gl[:, 1:2], gre_l, Pw[:, 0, j:j + 1], ALU.mult, rk, [glk])
                    _tt(S, "pool", gl[:, 2:3], gre_l, Pw[:, 1, j:j + 1], ALU.mult, rk, [glk])
                    _tt(S, "pool", gl[:, 3:4], gim_l, Pw[:, 0, j:j + 1], ALU.mult, rk, [glk])
                    _tt(S, "pool", G[:, 0, j:j + 1], gl[:, 1:2], gl[:, 0:1], ALU.subtract, [glk], [("s5_G", j)])
                    _tt(S, "pool", G[:, 1, j:j + 1], gl[:, 3:4], gl[:, 2:3], ALU.add, [glk], [("s5_G", j)])
                both(carry)
                both(lambda x: _tt(S, "dve", x["w"][:, 0, :], x["w"][:, 4, :], x["cs"], ALU.mult, [(x["wkey"], 4), kE], [(x["wkey"], 0)]))
                both(lambda x: _tt(S, "dve", x["w"][:, 1, :], x["w"][:, 5, :], x["sn"], ALU.mult, [(x["wkey"], 5), kE], [(x["wkey"], 1)]))
                both(lambda x: _tt(S, "dve", x["w"][:, 2, :], x["w"][:, 4, :], x["sn"], ALU.mult, [(x["wkey"], 4), kE], [(x["wkey"], 2)]))
                both(lambda x: _tt(S, "dve", x["w"][:, 3, :], x["w"][:, 5, :], x["cs"], ALU.mult, [(x["wkey"], 5), kE], [(x["wkey"], 3)]))
                both(lambda x: _tt(S, "dve", x["h"][:, 0, :], x["w"][:, 0, :], x["w"][:, 1, :], ALU.subtract, [(x["wkey"], 0), (x["wkey"], 1)], [x["hkey"]]))
                both(lambda x: _tt(S, "dve", x["h"][:, 1, :], x["w"][:, 2, :], x["w"][:, 3, :], ALU.add, [(x["wkey"], 2), (x["wkey"], 3)], [x["hkey"]]))
                for x in ctx:
                    j = x["j"]
                    h = x["h"]
                    jj = j % 4
                    S.op("pe", lambda e, h=h, j=j, jj=jj, pyv=pyv: e.matmul(pyv, lhsT=Cb[:, 0, j, :], rhs=h[:, 0, :], start=(jj == 0), stop=False),
                         reads=["s5_Cb", x["hkey"]], writes=[pyk])
                    S.op("pe", lambda e, h=h, j=j, jj=jj, pyv=pyv: e.matmul(pyv, lhsT=Cb[:, 1, j, :], rhs=h[:, 1, :], start=False, stop=(jj == 3)),
                         reads=["s5_Cb", x["hkey"]], writes=[pyk])
                yield
            e_, ekey = ep.next()
            yb = e_[:, 0, :]
            _stt(S, "dve", yb, u[:, q, :], dcol[:, q:q + 1], pyv, ALU.mult, ALU.add, [ukey, "s5_d", pyk], [ekey])
            _act(S, e_[:, 1, :], yb, AF.Square, [ekey], [ekey])
            _ts(S, "pool", e_[:, 1, :], e_[:, 1, :], 0.17886, 1.0, ALU.mult, ALU.add, [ekey], [ekey])
            _tt(S, "pool", e_[:, 1, :], e_[:, 1, :], yb, ALU.mult, [ekey], [ekey])
            _ts(S, "pool", e_[:, 1, :], e_[:, 1, :], -9.4, None, ALU.max, None, [ekey], [ekey])
            _act(S, e_[:, 2, :], e_[:, 1, :], AF.Exp, [ekey], [ekey], scale=-2.0 * 1.5957691216)
            _act(S, e_[:, 2, :], e_[:, 2, :], AF.Ln, [ekey, "c_onecol"], [ekey], scale=1.0, bias=onec)
            _act(S, e_[:, 2, :], e_[:, 2, :], AF.Exp, [ekey, "c_ln2"], [ekey], scale=-1.0, bias=ln2)
            _tt(S, "pool", e_[:, 3, :], e_[:, 2, :], yb, ALU.mult, [ekey], [ekey])
            S.op("sp", lambda e, e_=e_, q=q, c=c: e.dma_start(out=io["ysgT"][q * 128:(q + 1) * 128, c * TC:(c + 1) * TC], in_=e_[:, 3, :]),
                 reads=[ekey], writes=[("d_ysg", q, c)], stream="st")


def s5_host_params(inp, l, s):
    g0 = 32 * s
    f = np.float32
    def colr(a):
        return np.ascontiguousarray(a.reshape(16, 128).T)
    lre, lim = inp["s5_lam_re"][l][g0:g0 + 32], inp["s5_lam_im"][l][g0:g0 + 32]
    ldt = np.broadcast_to(inp["s5_log_dt"][l][g0:g0 + 32, None], (32, 64))
    s5col = np.stack([colr(lre), colr(lim), colr(ldt)], axis=1).astype(f)
    return dict(
        s5col=np.ascontiguousarray(s5col),
        bT_re=np.ascontiguousarray(inp["s5_b_re"][l][g0:g0 + 32].transpose(0, 2, 1)),
        bT_im=np.ascontiguousarray(inp["s5_b_im"][l][g0:g0 + 32].transpose(0, 2, 1)),
        cT_re=np.ascontiguousarray(inp["s5_c_re"][l][g0:g0 + 32].transpose(0, 2, 1)),
        cT_im=np.ascontiguousarray(inp["s5_c_im"][l][g0:g0 + 32].transpose(0, 2, 1)),
        d_col=np.ascontiguousarray(inp["s5_d"][l][512 * s:512 * (s + 1)].reshape(4, 128).T),
    )


def build_B_s5_only(dbg=False):
    Kb = KB()
    io = dict(
        uT=Kb.din("uT", [512, L], F32), s5col=Kb.din("s5col", [128, 3, 16], F32),
        bT_re=Kb.din("bT_re", [32, 16, 64], F32), bT_im=Kb.din("bT_im", [32, 16, 64], F32),
        cT_re=Kb.din("cT_re", [32, 64, 16], F32), cT_im=Kb.din("cT_im", [32, 64, 16], F32),
        d_col=Kb.din("d_col", [128, 4], F32), ysgT=Kb.dout("ysgT", [512, L], F32),
    )
    if dbg:
        io.update(dbg_cw=Kb.dout("dbg_cw", [128, 24, 16], F32), dbg_Ec=Kb.dout("dbg_Ec", [128, 16, TC], F32), dbg_Es=Kb.dout("dbg_Es", [128, 16, TC], F32),
                  dbg_P=Kb.dout("dbg_P", [128, 4, 16], F32), dbg_Bb=Kb.dout("dbg_Bb", [128, 2, 16, 128], BF16),
                  dbg_Cb=Kb.dout("dbg_Cb", [128, 2, 16, 128], BF16))
    g = s5_mixer(Kb, io, [Kb.ps[0], Kb.ps[1]], Kb.ps[2], Kb.ps[3])
    assert next(g) == "persist"
    m = Kb.mark()
    assert next(g) == "setup"
    Kb.S.barrier()
    Kb.release(m)
    for _ in g:
        pass
    Kb.S.emit()
    return Kb


def lru_mixer(Kb, io, psl):
    S = Kb.S
    sb = Kb.sb
    lc = sb("lr_col", [128, 4, 8], F32)
    cc = sb("lr_cc", [128, 6, 4], F32)
    Wbd = sb("lr_W", [128, 2, 4, 128], BF16)
    Dg = sb("lr_Dg", [128, 4, 4, 128], F32)
    carry = sb("lr_carry", [128, 4], F32)
    yield "persist"
    Wz = sb("lr_Wz", [128, 2, 4, 128], F32)
    ident = sb("lr_id", [128, 128], F32)
    onec = Kb.one_col()
    S.op("pool", lambda e: e.memset(ident[:], 1.0), writes=["lr_id"])
    S.op("pool", lambda e: e.affine_select(out=ident[:], in_=ident[:], pattern=[[-1, 128]], compare_op=ALU.is_equal, fill=0.0,
                                           base=0, channel_multiplier=1), reads=["lr_id"], writes=["lr_id"])
    S.op("sp", lambda e: e.dma_start(out=lc[:], in_=io["lru_col"]), writes=["lr_col"], stream="ld")
    S.op("pool", lambda e: e.memset(Wz[:], 0.0), writes=["lr_Wz"])

    def ldW(e):
        r = []
        for ri, nm in enumerate(("w_r", "w_i")):
            for q in range(4):
                for bl in range(2):
                    r.append(e.dma_start(out=Wz[bl * 64:(bl + 1) * 64, ri, q, bl * 64:(bl + 1) * 64], in_=io[nm][2 * q + bl]))
        return r
    S.op("sp", ldW, reads=[], writes=["lr_Wz"], stream="ld", ndma=16)
    S.op("pool", lambda e: e.tensor_copy(out=Wbd[:], in_=Wz[:]), reads=["lr_Wz"], writes=["lr_W"])
    for q in range(4):
        for kk in range(4):
            _ts(S, "pool", Dg[:, q, kk, :], ident[:], lc[:, q, kk:kk + 1], None, ALU.mult, None, ["lr_id", "lr_col"], ["lr_Dg"])
    kc = "lr_cc"
    lam = lc[:, :, 7]
    _act(S, cc[:, 4, :], lam, AF.Exp, ["lr_col"], [kc], scale=-1.0)
    _act(S, cc[:, 4, :], cc[:, 4, :], AF.Ln, [kc, "c_onecol"], [kc], bias=onec)
    _ts(S, "pool", cc[:, 0, :], cc[:, 4, :], -8.0, None, ALU.mult, None, [kc], [kc])
    _ts(S, "pool", cc[:, 1, :], cc[:, 4, :], -16.0, None, ALU.mult, None, [kc], [kc])
    _ts(S, "pool", cc[:, 2, :], lc[:, :, 5], -1.0, None, ALU.mult, None, ["lr_col"], [kc])
    _ts(S, "pool", cc[:, 3, :], lc[:, :, 6], -1.0, None, ALU.mult, None, ["lr_col"], [kc])
    S.op("pool", lambda e: e.memset(carry[:], 0.0), writes=["lr_carry"])
    yield "setup"
    HALF = 2048
    ubuf = Rot([(sb("lr_u%d" % i, [128, 8 + TC], F32), "lr_u%d" % i) for i in range(2)])
    xcf = Rot([(sb("lr_xc%d" % i, [128, TC], F32), "lr_xc%d" % i) for i in range(2)])
    xcb = Rot([(sb("lr_xb%d" % i, [128, TC], BF16), "lr_xb%d" % i) for i in range(2)])
    tmp = Rot([(sb("lr_t%d" % i, [128, 4, TC], F32), "lr_t%d" % i) for i in range(2)])
    abuf = Rot([(sb("lr_a%d" % i, [128, HALF], F32), "lr_a%d" % i) for i in range(1)])
    gbuf = Rot([(sb("lr_g%d" % i, [128, HALF], F32), "lr_g%d" % i) for i in range(1)])
    hbf = Rot([(sb("lr_hb%d" % i, [128, HALF], BF16), "lr_hb%d" % i) for i in range(1)])
    uT = io["uLT"]
    pl, plkey = psl
    for q in range(4):
        for half in range(2):
            af, akey = abuf.next()
            gf, gkey = gbuf.next()
            for cs in range(HALF // TC):
                c = half * (HALF // TC) + cs
                t0 = c * TC
                ub, ukey = ubuf.next()
                if c == 0:
                    S.op("pool", lambda e, ub=ub: e.memset(ub[:, 0:8], 0.0), writes=[ukey])
                    S.op("sp", lambda e, ub=ub, q=q: e.dma_start(out=ub[:, 8:8 + TC], in_=uT[q * 128:(q + 1) * 128, 0:TC]),
                         reads=[ukey], writes=[ukey], stream="ld")
                else:
                    S.op("sp", lambda e, ub=ub, q=q, t0=t0: e.dma_start(out=ub[:, 4:8 + TC], in_=uT[q * 128:(q + 1) * 128, t0 - 4:t0 + TC]),
                         writes=[ukey], stream="ld")
                for kk in range(4):
                    S.op("pe", lambda e, ub=ub, q=q, kk=kk: e.matmul(pl[:, 0:TC], lhsT=Dg[:, q, kk, :], rhs=ub[:, 5 + kk:5 + kk + TC],
                                                                   start=(kk == 0), stop=(kk == 3)),
                         reads=["lr_Dg", ukey], writes=[plkey])
                xc, xkey = xcf.next()
                xb, xbkey = xcb.next()
                _act(S, xc[:], pl[:, 0:TC], AF.Identity, [plkey, "lr_col"], [xkey], bias=lc[:, q, 4:5])
                S.op("pool", lambda e, xb=xb, xc=xc: e.tensor_copy(out=xb[:], in_=xc[:]), reads=[xkey], writes=[xbkey])
                S.op("pe", lambda e, xb=xb, q=q: e.matmul(pl[:, 0:TC], lhsT=Wbd[:, 0, q, :], rhs=xb[:], start=True, stop=True),
                     reads=["lr_W", xbkey], writes=[plkey])
                S.op("pe", lambda e, xb=xb, q=q: e.matmul(pl[:, TC:2 * TC], lhsT=Wbd[:, 1, q, :], rhs=xb[:], start=True, stop=True),
                     reads=["lr_W", xbkey], writes=[plkey])
                t, tkey = tmp.next()
                _act(S, t[:, 0, :], pl[:, 0:TC], AF.Exp, [plkey, kc], [tkey], scale=-1.0, bias=cc[:, 2, q:q + 1])
                _act(S, t[:, 1, :], pl[:, TC:2 * TC], AF.Exp, [plkey, kc], [tkey], scale=-1.0, bias=cc[:, 3, q:q + 1])
                _act(S, t[:, 0, :], t[:, 0, :], AF.Ln, [tkey, "c_onecol"], [tkey], bias=onec)
                _act(S, t[:, 1, :], t[:, 1, :], AF.Ln, [tkey, "c_onecol"], [tkey], bias=onec)
                _act(S, t[:, 0, :], t[:, 0, :], AF.Exp, [tkey], [tkey], scale=-1.0)
                _act(S, t[:, 1, :], t[:, 1, :], AF.Exp, [tkey], [tkey], scale=-1.0)
                asl = af[:, cs * TC:(cs + 1) * TC]
                gsl = gf[:, cs * TC:(cs + 1) * TC]
                _act(S, asl, t[:, 0, :], AF.Exp, [tkey, kc], [akey], scale=cc[:, 0, q:q + 1])
                _act(S, t[:, 2, :], t[:, 0, :], AF.Exp, [tkey, kc], [tkey], scale=cc[:, 1, q:q + 1])
                _act(S, t[:, 2, :], t[:, 2, :], AF.Ln, [tkey, "c_onecol"], [tkey], scale=-1.0, bias=onec)
                _act(S, t[:, 2, :], t[:, 2, :], AF.Exp, [tkey], [tkey], scale=0.5)
                _tt(S, "pool", t[:, 3, :], t[:, 1, :], xc[:], ALU.mult, [tkey, xkey], [tkey])
                _tt(S, "pool", gsl, t[:, 3, :], t[:, 2, :], ALU.mult, [tkey], [gkey])
                if "dbg_t" in io and q == 0 and c == 1:
                    S.op("sp", lambda e, t=t: e.dma_start(out=io["dbg_t"], in_=t[:]), reads=[tkey], writes=["dbgl1"], stream="st")
                    S.op("sp", lambda e, xc=xc: e.dma_start(out=io["dbg_xc"], in_=xc[:]), reads=[xkey], writes=["dbgl2"], stream="st")
                    S.op("sp", lambda e, af=af: e.dma_start(out=io["dbg_a"], in_=af[:, TC:2 * TC]), reads=[akey], writes=["dbgl3"], stream="st")
                    S.op("sp", lambda e, gf=gf: e.dma_start(out=io["dbg_g"], in_=gf[:, TC:2 * TC]), reads=[gkey], writes=["dbgl4"], stream="st")
                    S.op("sp", lambda e: e.dma_start(out=io["dbg_cc"], in_=cc[:]), reads=[kc], writes=["dbgl5"], stream="st")
                yield
            hb, hkey = hbf.next()
            S.op("dve", lambda e, af=af, gf=gf, q=q: e.tensor_tensor_scan(out=gf[:], data0=af[:], data1=gf[:], initial=carry[:, q:q + 1],
                                                                    op0=ALU.mult, op1=ALU.add),
                 reads=[akey, gkey, "lr_carry"], writes=[gkey])
            S.op("pool", lambda e, gf=gf, q=q: e.tensor_copy(out=carry[:, q:q + 1], in_=gf[:, HALF - 1:HALF]), reads=[gkey], writes=["lr_carry"])
            S.op("pool", lambda e, gf=gf, hb=hb: e.tensor_copy(out=hb[:], in_=gf[:]), reads=[gkey], writes=[hkey])
            S.op("sp", lambda e, hb=hb, q=q, half=half: e.dma_start(out=io["yLruT"][q * 128:(q + 1) * 128, half * HALF:(half + 1) * HALF], in_=hb[:]),
                 reads=[hkey], writes=[("d_ylru", q, half)], stream="st")
            yield


def lru_host_params(inp, l, s):
    ch = slice(512 * s, 512 * (s + 1))
    cols = [inp["lru_conv_w"][l][k, ch] for k in range(4)] + [inp["lru_conv_b"][l][ch], inp["lru_b_r"][l].reshape(-1)[ch],
                                                               inp["lru_b_i"][l].reshape(-1)[ch], inp["lru_lambda"][l][ch]]
    a = np.stack(cols, axis=-1).reshape(4, 128, 8).transpose(1, 0, 2)
    return dict(lru_col=np.ascontiguousarray(a, dtype=np.float32),
                w_r=np.ascontiguousarray(inp["lru_w_r"][l][8 * s:8 * (s + 1)]), w_i=np.ascontiguousarray(inp["lru_w_i"][l][8 * s:8 * (s + 1)]))


def attn_mixer(Kb, io, banksA, banksO):
    S = Kb.S
    sb = Kb.sb
    tri = sb("at_tri", [128, 128], BF16)
    onesn = sb("at_on", [128, 128], F32)
    dmask = sb("at_dm", [128, 128], BF16)
    yield "persist"
    tmpf = sb("at_tmpf", [128, 128], F32)
    S.op("pool", lambda e: e.memset(tmpf[:], -1.0), writes=["at_tmpf"])
    S.op("pool", lambda e: e.memset(onesn[:], -1.0), writes=["at_on"])
    S.op("pool", lambda e: e.affine_select(out=tri[:], in_=tmpf[:], pattern=[[-1, 128]], compare_op=ALU.is_ge, fill=0.0,
                                           base=0, channel_multiplier=1), reads=["at_tmpf"], writes=["at_tri"])
    S.op("pool", lambda e: e.memset(tmpf[:], 1.0), reads=["at_tri"], writes=["at_tmpf"])
    S.op("pool", lambda e: e.affine_select(out=dmask[:], in_=tmpf[:], pattern=[[1, 128]], compare_op=ALU.is_gt, fill=0.0,
                                           base=0, channel_multiplier=-1), reads=["at_tmpf"], writes=["at_dm"])
    yield "setup"
    onec = Kb.one_col()
    st = []
    for s_ in range(2):
        st.append(dict(
            q=sb("at_q%d" % s_, [128, L], BF16), k=sb("at_k%d" % s_, [128, L], BF16), v=sb("at_v%d" % s_, [128, 32, 128], BF16),
            e=Rot([(sb("at_e%d_%d" % (s_, i), [128, 512], F32), "at_e%d_%d" % (s_, i)) for i in range(1)]),
            sp=Rot([(sb("at_sp%d_%d" % (s_, i), [128, 512], BF16), "at_sp%d_%d" % (s_, i)) for i in range(2)]),
            w=Rot([(sb("at_w%d_%d" % (s_, i), [128, 512], BF16), "at_w%d_%d" % (s_, i)) for i in range(2)]),
            acc=sb("at_acc%d" % s_, [128, 512], F32), o=Rot([(sb("at_o%d_%d" % (s_, i), [128, 512], BF16), "at_o%d_%d" % (s_, i)) for i in range(1)]),
            A=banksA[s_], O=banksO[s_], key="at%d" % s_))

    def stream(s_):
        x = st[s_]
        kq, kk_, kv, kacc = x["key"] + "q", x["key"] + "k", x["key"] + "v", x["key"] + "acc"
        A, Akey = x["A"]
        O, Okey = x["O"]
        for h in (2 * s_, 2 * s_ + 1):
            S.op("sp", lambda e, h=h: e.dma_start(out=x["q"][:], in_=io["qT"][h * 128:(h + 1) * 128, :]), writes=[kq], stream="ld")
            S.op("sp", lambda e, h=h: e.dma_start(out=x["k"][:], in_=io["kT"][h * 128:(h + 1) * 128, :]), writes=[kk_], stream="ld")

            def ldv(e, h=h):
                r = []
                for b0 in range(0, 32, 8):
                    r.append(e.dma_start(out=x["v"][:, b0:b0 + 8, :],
                                         in_=io["v"][b0 * 128:(b0 + 8) * 128, h * 128:(h + 1) * 128].rearrange("(kb p) d -> p kb d", p=128)))
                return r
            S.op("sp", ldv, writes=[kv], stream="ld", ndma=4)
            for qg in range(8):
                S.op("pool", lambda e: e.memset(x["acc"][:], 0.0), writes=[kacc])
                nk = 4 * qg + 4
                for ui, kb in enumerate(range(nk - 1, -1, -1)):
                    j = kb - 4 * qg
                    c0 = 128 * j if j >= 0 else 0
                    cs = slice(c0, 512)
                    qs = slice(qg * 512 + c0, (qg + 1) * 512)
                    ks = slice(kb * 128, (kb + 1) * 128)
                    e_, ekey = x["e"].next()
                    sp, spkey = x["sp"].next()
                    w, wkey = x["w"].next()
                    S.op("pe", lambda e, ks=ks, qs=qs, cs=cs: e.matmul(A[:, cs], lhsT=x["k"][:, ks], rhs=x["q"][:, qs], start=True, stop=True),
                         reads=[kk_, kq], writes=[Akey])
                    _act(S, e_[:, cs], A[:, cs], AF.Exp, [Akey], [ekey])
                    _act(S, sp[:, cs], e_[:, cs], AF.Ln, [ekey, "c_onecol"], [spkey], bias=onec)
                    if j >= 0:
                        _tt(S, "pool", sp[:, c0:c0 + 128], sp[:, c0:c0 + 128], dmask[:], ALU.mult, [spkey, "at_dm"], [spkey])
                    first = (ui == 0)
                    S.op("pe", lambda e, ks=ks, qs=qs, cs=cs: e.matmul(A[:, cs], lhsT=x["k"][:, ks], rhs=x["q"][:, qs], start=True, stop=False),
                         reads=[kk_, kq, ekey], writes=[Akey])
                    S.op("pe", lambda e, sp=sp, cs=cs, first=first: e.matmul(A[:, cs], lhsT=tri[:], rhs=sp[:, cs], start=False, stop=first),
                         reads=["at_tri", spkey], writes=[Akey])
                    if not first:
                        S.op("pe", lambda e, cs=cs: e.matmul(A[:, cs], lhsT=onesn[:], rhs=x["acc"][:, cs], start=False, stop=True),
                             reads=["at_on", kacc], writes=[Akey])
                    _act(S, w[:, cs], A[:, cs], AF.Exp, [Akey], [wkey])
                    if j >= 0:
                        _tt(S, "pool", w[:, c0:c0 + 128], w[:, c0:c0 + 128], dmask[:], ALU.mult, [wkey, "at_dm"], [wkey])
                    if kb > 0:
                        _tt(S, "pool", x["acc"][:, cs], x["acc"][:, cs], sp[:, cs], ALU.add, [kacc, spkey], [kacc])
                    S.op("pe", lambda e, w=w, cs=cs, kb=kb, first=first: e.matmul(O[:, cs], lhsT=x["v"][:, kb, :], rhs=w[:, cs], start=first, stop=(kb == 0)),
                         reads=[kv, wkey], writes=[Okey])
                    yield
                o, okey = x["o"].next()
                S.op("dve", lambda e, o=o: e.tensor_copy(out=o[:], in_=O[:]), reads=[Okey], writes=[okey])
                S.op("sp", lambda e, o=o, h=h, qg=qg: e.dma_start(out=io["yAttT"][h * 128:(h + 1) * 128, qg * 512:(qg + 1) * 512], in_=o[:]),
                     reads=[okey], writes=[("d_yatt", h, qg)], stream="st")

    g0, g1 = stream(0), stream(1)
    alive = [g0, g1]
    while alive:
        for g in list(alive):
            try:
                next(g)
            except StopIteration:
                alive.remove(g)
        yield


def stage_B(Kb, io, which=("s5", "lru", "attn")):
    gens = []
    ps = Kb.ps
    if "s5" in which:
        gens.append(("s5", s5_mixer(Kb, io, [ps[0], ps[1]], ps[2], ps[3]), 128))
    if "lru" in which:
        gens.append(("lru", lru_mixer(Kb, io, ps[3]), 72))
    if "attn" in which:
        gens.append(("attn", attn_mixer(Kb, io, [ps[4], ps[5]], [ps[6], ps[7]]), 288))
    for _, g, _n in gens:
        assert next(g) == "persist"
    m = Kb.mark()
    for _, g, _n in gens:
        assert next(g) == "setup"
    Kb.S.barrier()
    Kb.release(m)
    prog = {n: 0 for n, _, _ in gens}
    alive = {n: (g, tot) for n, g, tot in gens}
    while alive:
        n = min(alive, key=lambda n_: prog[n_] / alive[n_][1])
        try:
            next(alive[n][0])
            prog[n] += 1
        except StopIteration:
            del alive[n]


def build_B(which=("s5", "lru", "attn"), dbg=False):
    Kb = KB()
    io = {}
    if "s5" in which:
        io.update(uT=Kb.din("uT", [512, L], F32), s5col=Kb.din("s5col", [128, 3, 16], F32),
                  bT_re=Kb.din("bT_re", [32, 16, 64], F32), bT_im=Kb.din("bT_im", [32, 16, 64], F32),
                  cT_re=Kb.din("cT_re", [32, 64, 16], F32), cT_im=Kb.din("cT_im", [32, 64, 16], F32),
                  d_col=Kb.din("d_col", [128, 4], F32), ysgT=Kb.dout("ysgT", [512, L], F32))
    if "lru" in which:
        io.update(uLT=Kb.din("uLT", [512, L], F32), lru_col=Kb.din("lru_col", [128, 4, 8], F32),
                  w_r=Kb.din("w_r", [8, 64, 64], F32), w_i=Kb.din("w_i", [8, 64, 64], F32), yLruT=Kb.dout("yLruT", [512, L], BF16))
        if dbg:
            io.update(dbg_t=Kb.dout("dbg_t", [128, 4, TC], F32), dbg_xc=Kb.dout("dbg_xc", [128, TC], F32), dbg_a=Kb.dout("dbg_a", [128, TC], F32),
                      dbg_g=Kb.dout("dbg_g", [128, TC], F32), dbg_cc=Kb.dout("dbg_cc", [128, 6, 4], F32))
    if "attn" in which:
        io.update(qT=Kb.din("qT", [512, L], BF16), kT=Kb.din("kT", [512, L], BF16), v=Kb.din("v", [L, 512], BF16),
                  yAttT=Kb.dout("yAttT", [512, L], BF16))
    stage_B(Kb, io, which)
    Kb.S.emit()
    return Kb


def _evac_alt(S, cnt, out_ap, ps, pkey, okey, scale=1.0):
    if cnt % 2 == 0:
        S.op("act", lambda e: e.activation(out=out_ap, in_=ps[:], func=AF.Copy, scale=scale), reads=[pkey], writes=[okey])
    else:
        S.op("dve", lambda e: e.tensor_scalar(out=out_ap, in0=ps[:], scalar1=scale, scalar2=None, op0=ALU.mult), reads=[pkey], writes=[okey])


def stage_C(Kb, io, final_norm=False):
    S = Kb.S
    sb = Kb.sb
    HT = 1024
    bgl = sb("C_bgl", [128, 8], F32)
    bgt = sb("C_bgt", [128, 48], F32)
    g2 = sb("C_g2", [128, 16], F32)
    S.op("sp", lambda e: e.dma_start(out=bgl[:], in_=io["b_glu"]), writes=["C_bgl"], stream="ld")
    S.op("sp", lambda e: e.dma_start(out=bgt[:], in_=io["b_gate"]), writes=["C_bgt"], stream="ld")
    S.op("sp", lambda e: e.dma_start(out=g2[:], in_=io["g2"]), writes=["C_g2"], stream="ld")
    if final_norm:
        gf = sb("C_gf", [128, 16], F32)
        S.op("sp", lambda e: e.dma_start(out=gf[:], in_=io["gf"]), writes=["C_gf"], stream="ld")
    mark0 = Kb.mark()
    ev = [0]
    ysb = sb("C_ysb", [128, 8, HT], BF16)
    ylr = sb("C_ylr", [128, 8, HT], BF16)
    yat = sb("C_yat", [128, 8, HT], BF16)
    xn = sb("C_xn", [128, 16, HT], BF16)
    mg = sb("C_mg", [128, 16, HT], BF16)
    wga = WStream(Kb, "wCa", 5, 16, 128)
    wgb_ = WStream(Kb, "wCb", 3, 8, 384)
    ysf = Rot([(sb("C_ysf%d" % i, [128, 512], F32), "C_ysf%d" % i) for i in range(2)])
    zt = Rot([(sb("C_z%d" % i, [128, 512], F32), "C_z%d" % i) for i in range(2)])
    gt = Rot([(sb("C_g%d" % i, [128, 512], F32), "C_g%d" % i) for i in range(3)])
    ma = Rot([(sb("C_ma%d" % i, [128, 512], F32), "C_ma%d" % i) for i in range(2)])
    hld = Rot([(sb("C_hl%d" % i, [128, 512], F32), "C_hl%d" % i) for i in range(2)])
    hst = Rot([(sb("C_hs%d" % i, [128, 512], F32), "C_hs%d" % i) for i in range(2)])
    for th in range(2):
        tsl = slice(th * HT, (th + 1) * HT)
        S.op("pool", lambda e, tsl=tsl: [e.dma_start(out=ysb[:, k0:k0 + 4, :], in_=io["ysgT"][k0 * 128:(k0 + 4) * 128, tsl].rearrange("(kt p) t -> p kt t", p=128))
                                         for k0 in (0, 4)], writes=["C_ysb"], stream="ldc", ndma=2)
        S.op("sp", lambda e, tsl=tsl: [e.dma_start(out=ylr[:, k0:k0 + 4, :], in_=io["yLruT"][k0 * 128:(k0 + 4) * 128, tsl].rearrange("(kt p) t -> p kt t", p=128))
                                       for k0 in (0, 4)], writes=["C_ylr"], stream="ld", ndma=2)
        S.op("sp", lambda e, tsl=tsl: [e.dma_start(out=yat[:, k0:k0 + 4, :], in_=io["yAttT"][k0 * 128:(k0 + 4) * 128, tsl].rearrange("(kt p) t -> p kt t", p=128))
                                       for k0 in (0, 4)], writes=["C_yat"], stream="ld", ndma=2)
        S.op("sp", lambda e, tsl=tsl: [e.dma_start(out=xn[:, k0:k0 + 4, :], in_=io["xnT"][k0 * 128:(k0 + 4) * 128, tsl].rearrange("(kt p) t -> p kt t", p=128))
                                       for k0 in (0, 4, 8, 12)], writes=["C_xn"], stream="ld", ndma=4)
        ys5 = sb("C_ys5", [128, 8, HT], BF16) if th == 0 else ys5
        for s in range(3):
            c0, c1 = s * 384, min(1024, (s + 1) * 384)
            wb, wkey = wgb_.load(io["w_glu"][:, c0:c1], 8, c1 - c0)
            for cbl in range((c1 - c0) // 128):
                cb = c0 // 128 + cbl
                for tt in range(HT // 512):
                    ps, pkey = Kb.psrot.next()
                    mm_group(S, ps[:], pkey, [(wb[:, kt, cbl * 128:(cbl + 1) * 128], ysb[:, kt, tt * 512:(tt + 1) * 512], [wkey, "C_ysb"])
                                              for kt in range(8)])
                    z, zkey = zt.next()
                    yf, yfkey = ysf.next()
                    S.op("sp", lambda e, yf=yf, cb=cb, tt=tt, th=th: e.dma_start(out=yf[:], in_=io["ysgT"][cb * 128:(cb + 1) * 128, th * HT + tt * 512:th * HT + (tt + 1) * 512]),
                         writes=[yfkey], stream="ld")
                    _act(S, z[:], ps[:], AF.Sigmoid, [pkey, "C_bgl"], [zkey], bias=bgl[:, cb:cb + 1])
                    _tt(S, "dve", ys5[:, cb, tt * 512:(tt + 1) * 512], z[:], yf[:], ALU.mult, [zkey, yfkey], ["C_ys5"])
        ybr = [(ys5, "C_ys5"), (ylr, "C_ylr"), (yat, "C_yat")]
        for cb in range(16):
            wgs = []
            for br in range(3):
                wgs.append(wga.load(io["w_gate"][:, br * 2048 + cb * 128:br * 2048 + (cb + 1) * 128], 16, 128))
            wbs = []
            for br in range(3):
                wbs.append(None)
            wbb, wbkey = wgb_.rot.next()

            def ldbr(e, wbb=wbb, cb=cb):
                r = []
                for br in range(3):
                    for k0 in (0, 4):
                        r.append(e.dma_start(out=wbb[:, k0:k0 + 4, br * 128:(br + 1) * 128],
                                             in_=io["w_br"][br, k0 * 128:(k0 + 4) * 128, cb * 128:(cb + 1) * 128].rearrange("(kt p) c -> p kt c", p=128)))
                return r
            S.op("pool", ldbr, writes=[wbkey], stream="wCb", ndma=6)
            for tt in range(HT // 512):
                tsl2 = slice(tt * 512, (tt + 1) * 512)
                m, mkey = ma.next()
                for br in range(3):
                    wgb, wgkey = wgs[br]
                    psg, pgkey = Kb.psrot.next()
                    mm_group(S, psg[:], pgkey, [(wgb[:, kt, 0:128], xn[:, kt, tsl2], [wgkey, "C_xn"]) for kt in range(16)])
                    psp, ppkey = Kb.psrot.next()
                    yb_, ybkey = ybr[br]
                    mm_group(S, psp[:], ppkey, [(wbb[:, kt, br * 128:(br + 1) * 128], yb_[:, kt, tsl2], [wbkey, ybkey]) for kt in range(8)])
                    g, gkey = gt.next()
                    _act(S, g[:], psg[:], AF.Sigmoid, [pgkey, "C_bgt"], [gkey], bias=bgt[:, br * 16 + cb:br * 16 + cb + 1])
                    if br == 0:
                        _tt(S, "dve", m[:], psp[:], g[:], ALU.mult, [ppkey, gkey], [mkey])
                    else:
                        _tt(S, "dve", g[:], psp[:], g[:], ALU.mult, [ppkey, gkey], [gkey])
                        if br == 1:
                            _tt(S, "pool", m[:], m[:], g[:], ALU.add, [mkey, gkey], [mkey])
                        else:
                            _tt(S, "pool", mg[:, cb, tsl2], m[:], g[:], ALU.add, [mkey, gkey], [("C_mg", cb)])
        mgkeys = [("C_mg", cb) for cb in range(16)]
        for s in range(16):
            c0, c1 = s * 128, (s + 1) * 128
            wb, wkey = wga.load(io["w_out"][:, c0:c1], 16, 128)
            for cbl in range(1):
                cb = s
                for tt in range(HT // 512):
                    gsl = slice(th * HT + tt * 512, th * HT + (tt + 1) * 512)
                    ps, pkey = Kb.psrot.next()
                    mm_group(S, ps[:], pkey, [(wb[:, kt, cbl * 128:(cbl + 1) * 128], mg[:, kt, tt * 512:(tt + 1) * 512], [wkey] + mgkeys)
                                              for kt in range(16)])
                    hl, hlkey = hld.next()
                    hs, hskey = hst.next()
                    S.op("sp", lambda e, hl=hl, cb=cb, gsl=gsl: e.dma_start(out=hl[:], in_=io["hT"][cb * 128:(cb + 1) * 128, gsl]), writes=[hlkey], stream="ld")
                    _tt(S, "dve", hs[:], ps[:], hl[:], ALU.add, [pkey, hlkey], [hskey])
                    S.op("sp", lambda e, hs=hs, cb=cb, gsl=gsl: e.dma_start(out=io["h1T"][cb * 128:(cb + 1) * 128, gsl], in_=hs[:]),
                         reads=[hskey], writes=[("d_h1", cb, th, tt)], stream="st")
    h1keys = [("d_h1", cb, th, tt) for cb in range(16) for th in range(2) for tt in range(2)]
    S.barrier()
    Kb.release(mark0)
    hn = sb("C_hn", [128, 16, TOK], BF16)
    hb = sb("C_hb", [128, 16, 512], F32)
    nrm4 = make_nrm(Kb, "C4")
    for tt in range(4):
        def ld(e, tt=tt):
            r = []
            for f0 in range(0, 16, 4):
                r.append(e.dma_start(out=hb[:, f0:f0 + 4, :],
                                     in_=io["h1T"][f0 * 128:(f0 + 4) * 128, tt * 512:(tt + 1) * 512].rearrange("(ft p) t -> p ft t", p=128)))
            return r
        S.op("sp", ld, reads=[], writes=["C_hb"], stream="ld", ndma=4)
        rmsnorm_tile(Kb, hb, "C_hb", g2, "C_g2", lambda ft, tt=tt: hn[:, ft, tt * 512:(tt + 1) * 512], lambda ft, tt=tt: [("C_hn", tt)], nrm4)
    wu = WStream(Kb, "wU", 3, 16, 512)
    rl = Rot([(sb("C_rl%d" % i, [128, 512], F32), "C_rl%d" % i) for i in range(3)])
    hd = Rot([(sb("C_hd%d" % i, [128, TOK], BF16), "C_hd%d" % i) for i in range(2)])
    for s in range(16):
        wb, wkey = wu.load(io["w_up"][:, s * 512:(s + 1) * 512], 16, 512)
        for cbl in range(4):
            cb = s * 4 + cbl
            hdt, hdkey = hd.next()
            for tt in range(4):
                ps, pkey = Kb.psrot.next()
                mm_group(S, ps[:], pkey, [(wb[:, kt, cbl * 128:(cbl + 1) * 128], hn[:, kt, tt * 512:(tt + 1) * 512], [wkey, ("C_hn", tt)])
                                          for kt in range(16)])
                r_, rkey = rl.next()
                _act(S, r_[:], ps[:], AF.Relu, [pkey], [rkey])
                _tt(S, "pool" if tt % 2 else "dve", hdt[:, tt * 512:(tt + 1) * 512], r_[:], r_[:], ALU.mult, [rkey], [hdkey])
            S.op("sp", lambda e, hdt=hdt, cb=cb: e.dma_start(out=io["hidT"][cb * 128:(cb + 1) * 128, :], in_=hdt[:]),
                 reads=[hdkey], writes=[("d_hid", cb)], stream="st")
    S.barrier()
    Kb.release(mark0)
    hid = sb("C_hid", [128, 32, HT], BF16)
    acc = sb("C_acc", [128, 16, HT], F32)
    wd = WStream(Kb, "wD", 3, 32, 128)
    h1l = Rot([(sb("C_h1l%d" % i, [128, HT], F32), "C_h1l%d" % i) for i in range(2)])
    ost = Rot([(sb("C_ost%d" % i, [128, HT], F32), "C_ost%d" % i) for i in range(2)])
    nrm6 = make_nrm(Kb, "C6") if final_norm else None
    for th in range(2):
        for kh in range(2):
            def ldh(e, th=th, kh=kh):
                r = []
                for k0 in range(0, 32, 4):
                    r.append(e.dma_start(out=hid[:, k0:k0 + 4, :],
                                         in_=io["hidT"][(kh * 32 + k0) * 128:(kh * 32 + k0 + 4) * 128, th * HT:(th + 1) * HT].rearrange("(kt p) t -> p kt t", p=128)))
                return r
            S.op("sp", ldh, reads=[], writes=["C_hid"], stream="ld", ndma=8)
            for cb in range(16):
                wb, wkey = wd.load(io["w_down"][kh * 4096:(kh + 1) * 4096, cb * 128:(cb + 1) * 128], 32, 128)
                if kh == 0:
                    hl, hlkey = h1l.next()
                    S.op("sp", lambda e, hl=hl, cb=cb, th=th: e.dma_start(out=hl[:], in_=io["h1T"][cb * 128:(cb + 1) * 128, th * HT:(th + 1) * HT]),
                         writes=[hlkey], stream="ld")
                else:
                    o, okey = ost.next()
                for tt in range(HT // 512):
                    ps, pkey = Kb.psrot.next()
                    mm_group(S, ps[:], pkey, [(wb[:, kt, :], hid[:, kt, tt * 512:(tt + 1) * 512], [wkey, "C_hid"]) for kt in range(32)])
                    asl = acc[:, cb, tt * 512:(tt + 1) * 512]
                    if kh == 0:
                        _tt(S, "dve", asl, ps[:], hl[:, tt * 512:(tt + 1) * 512], ALU.add, [pkey, hlkey], [("C_acc", cb)])
                    elif not final_norm:
                        _tt(S, "dve", o[:, tt * 512:(tt + 1) * 512], ps[:], asl, ALU.add, [pkey, ("C_acc", cb)], [okey])
                    else:
                        _tt(S, "dve", asl, ps[:], asl, ALU.add, [pkey, ("C_acc", cb)], [("C_acc", cb)])
                if kh == 1 and not final_norm:
                    S.op("sp", lambda e, o=o, cb=cb, th=th: e.dma_start(out=io["hoT"][cb * 128:(cb + 1) * 128, th * HT:(th + 1) * HT], in_=o[:]),
                         reads=[okey], writes=[("d_ho", cb, th)], stream="st")
        if final_norm:
            for tt in range(HT // 512):
                ps, pkey = Kb.psrot.next()
                ones = Kb.ones_f32()
                c = nrm6
                for ft in range(16):
                    s_, skey = c["sq"].next()
                    S.op("act", lambda e, s_=s_, ft=ft, tt=tt: e.activation(out=s_[:], in_=acc[:, ft, tt * 512:(tt + 1) * 512], func=AF.Square),
                         reads=[("C_acc", ft)], writes=[skey])
                    S.op("pe", lambda e, s_=s_, ft=ft, ps=ps: e.matmul(ps[:], lhsT=ones[:], rhs=s_[:], start=(ft == 0), stop=(ft == 15)),
                         reads=[skey, "c_ones"], writes=[pkey])
                ln, lkey = c["ln"]
                rstd, rkey = c["rstd"]
                et = c["eps"][0]
                _act(S, ln[:], ps[:], AF.Ln, [pkey, c["eps"][1]], [lkey], scale=1.0 / D, bias=et[:, 0:1])
                _act(S, rstd[:], ln[:], AF.Exp, [lkey], [rkey], scale=-0.5)
                for ft in range(16):
                    o, okey = ost.next()
                    _stt(S, "dve", o[:, 0:512], acc[:, ft, tt * 512:(tt + 1) * 512], gf[:, ft:ft + 1], rstd[:], ALU.mult, ALU.mult,
                         [("C_acc", ft), rkey, "C_gf"], [okey])
                    S.op("sp", lambda e, o=o, ft=ft, th=th, tt=tt: e.dma_start(out=io["hoT"][ft * 128:(ft + 1) * 128, th * HT + tt * 512:th * HT + (tt + 1) * 512], in_=o[:, 0:512]),
                         reads=[okey], writes=[("d_ho", ft, th, tt)], stream="st")


def build_C(final_norm=False, dbg=False):
    Kb = KB()
    io = dict(
        hT=Kb.din("hT", [D, TOK], F32), xnT=Kb.din("xnT", [D, TOK], BF16), ysgT=Kb.din("ysgT", [1024, TOK], F32),
        yLruT=Kb.din("yLruT", [1024, TOK], BF16), yAttT=Kb.din("yAttT", [1024, TOK], BF16),
        w_glu=Kb.din("w_glu", [1024, 1024], F32), b_glu=Kb.din("b_glu", [128, 8], F32),
        w_gate=Kb.din("w_gate", [D, 6144], F32), b_gate=Kb.din("b_gate", [128, 48], F32),
        w_br=Kb.din("w_br", [3, 1024, D], F32), w_out=Kb.din("w_out", [D, D], F32), g2=Kb.din("g2", [128, 16], F32),
        w_up=Kb.din("w_up", [D, DFF], F32), w_down=Kb.din("w_down", [DFF, D], F32),
        mT=None, h1T=(Kb.dout if dbg else Kb.dint)("h1T", [D, TOK], F32), hidT=(Kb.dout if dbg else Kb.dint)("hidT", [DFF, TOK], BF16),
        hoT=Kb.dout("hoT", [D, TOK], F32),
    )
    if final_norm:
        io["gf"] = Kb.din("gf", [128, 16], F32)
    stage_C(Kb, io, final_norm)
    Kb.S.emit()
    return Kb


def _io_A(Kb, hT=None):
    return dict(
        hT=hT if hT is not None else Kb.din("hT", [D, TOK], F32), g1=Kb.din("g1", [128, 16], F32), w_in5=Kb.din("w_in5", [D, 5120], F32),
        xnT=Kb.dout("xnT", [D, TOK], BF16), uS5T=Kb.dout("uS5T", [1024, TOK], F32), uLruT=Kb.dout("uLruT", [1024, TOK], F32),
        qT=Kb.dout("qT", [1024, TOK], BF16), kT=Kb.dout("kT", [1024, TOK], BF16), v=Kb.dout("v", [TOK, 1024], BF16))


def _io_C(Kb, final_norm):
    io = dict(
        hT=Kb.din("hT", [D, TOK], F32), xnT=Kb.din("xnT_in", [D, TOK], BF16), ysgT=Kb.din("ysgT", [1024, TOK], F32),
        yLruT=Kb.din("yLruT", [1024, TOK], BF16), yAttT=Kb.din("yAttT", [1024, TOK], BF16),
        w_glu=Kb.din("w_glu", [1024, 1024], F32), b_glu=Kb.din("b_glu", [128, 8], F32),
        w_gate=Kb.din("w_gate", [D, 6144], F32), b_gate=Kb.din("b_gate", [128, 48], F32),
        w_br=Kb.din("w_br", [3, 1024, D], F32), w_out=Kb.din("w_out", [D, D], F32), g2=Kb.din("g2", [128, 16], F32),
        w_up=Kb.din("w_up", [D, DFF], F32), w_down=Kb.din("w_down", [DFF, D], F32),
        h1T=Kb.dint("h1T", [D, TOK], F32), hidT=Kb.dint("hidT", [DFF, TOK], BF16),
        hoT=Kb.dout("hoT", [D, TOK], F32))
    if final_norm:
        io["gf"] = Kb.din("gf", [128, 16], F32)
    return io


def build_CA():
    Kb = KB()
    m0 = Kb.mark()
    ioC = _io_C(Kb, False)
    stage_C(Kb, ioC, False)
    Kb.S.barrier()
    Kb.release(m0)
    ioA = _io_A(Kb, hT=ioC["hoT"])
    keys = [("d_ho", cb, th) for cb in range(16) for th in range(2)]
    stage_A(Kb, ioA, h_dep_keys=keys)
    Kb.S.emit()
    return Kb


def build_Cfinal():
    Kb = KB()
    io = _io_C(Kb, True)
    stage_C(Kb, io, True)
    Kb.S.emit()
    return Kb


_PROGS = {}


def _prog(name):
    if name not in _PROGS:
        _PROGS[name] = dict(A=build_A, B=build_B, CA=build_CA, CF=build_Cfinal)[name]()
    return _PROGS[name]


def _col(a, n):
    return np.ascontiguousarray(np.asarray(a, np.float32).reshape(n, 128).T)


def _run(name, in_maps):
    Kb = _prog(name)
    res = run_bass_kernel_spmd(Kb.nc, in_maps, core_ids=list(range(NCORES)))
    return res.results


def _A_inputs(inp, l, hT_list):
    w5 = np.ascontiguousarray(inp["w_in"][l][:, :5120])
    g1 = _col(inp["norm_mix_g"][l], 16)
    return [dict(hT=hT_list[c], g1=g1, w_in5=w5) for c in range(NCORES)]


def _B_inputs(inp, l, rA):
    maps = []
    for c in range(NCORES):
        b, s = c // 2, c % 2
        ch = slice(512 * s, 512 * (s + 1))
        m = {}
        m.update(s5_host_params(inp, l, s))
        m.update(lru_host_params(inp, l, s))
        m["uT"] = np.ascontiguousarray(np.concatenate([rA[2 * b]["uS5T"][ch], rA[2 * b + 1]["uS5T"][ch]], axis=1))
        m["uLT"] = np.ascontiguousarray(np.concatenate([rA[2 * b]["uLruT"][ch], rA[2 * b + 1]["uLruT"][ch]], axis=1))
        m["qT"] = np.ascontiguousarray(np.concatenate([rA[2 * b]["qT"][ch], rA[2 * b + 1]["qT"][ch]], axis=1))
        m["kT"] = np.ascontiguousarray(np.concatenate([rA[2 * b]["kT"][ch], rA[2 * b + 1]["kT"][ch]], axis=1))
        m["v"] = np.ascontiguousarray(np.concatenate([rA[2 * b]["v"][:, ch], rA[2 * b + 1]["v"][:, ch]], axis=0))
        maps.append(m)
    return maps


def _C_inputs(inp, l, hT_list, rA, rB, final):
    wts = dict(
        w_glu=np.ascontiguousarray(inp["s5_w_glu"][l]), b_glu=_col(inp["s5_b_glu"][l], 8),
        w_gate=np.ascontiguousarray(inp["w_in"][l][:, 5120:]), b_gate=_col(inp["b_gate"][l], 48),
        w_br=np.ascontiguousarray(np.stack([inp["w_br_s5"][l], inp["w_br_lru"][l], inp["w_br_attn"][l]])),
        w_out=np.ascontiguousarray(inp["w_out"][l]), g2=_col(inp["norm_mlp_g"][l], 16),
        w_up=np.ascontiguousarray(inp["w_up"][l]), w_down=np.ascontiguousarray(inp["w_down"][l]))
    if final:
        wts["gf"] = _col(inp["final_norm_g"], 16)
    maps = []
    for c in range(NCORES):
        b, s = c // 2, c % 2
        tk = slice(2048 * s, 2048 * (s + 1))
        m = dict(wts)
        m["hT"] = hT_list[c]
        m["xnT_in"] = rA[c]["xnT"]
        for nm in ("ysgT", "yLruT", "yAttT"):
            m[nm] = np.ascontiguousarray(np.concatenate([rB[2 * b][nm][:, tk], rB[2 * b + 1][nm][:, tk]], axis=0))
        maps.append(m)
    return maps


def kernel_unfused(**inp):
    inp = {k: np.asarray(v) for k, v in inp.items()}
    x = inp["x"]
    hT = [np.ascontiguousarray(x[c // 2, 2048 * (c % 2):2048 * (c % 2 + 1), :].T) for c in range(NCORES)]
    rA = _run("A", _A_inputs(inp, 0, hT))
    rB = _run("B", _B_inputs(inp, 0, rA))
    mC = _C_inputs(inp, 0, hT, rA, rB, False)
    aw = _A_inputs(inp, 1, hT)
    for c in range(NCORES):
        mC[c]["g1"] = aw[c]["g1"]
        mC[c]["w_in5"] = aw[c]["w_in5"]
    rCA = _run("CA", mC)
    hT1 = [r["hoT"] for r in rCA]
    rB1 = _run("B", _B_inputs(inp, 1, rCA))
    rF = _run("CF", _C_inputs(inp, 1, hT1, rCA, rB1, True))
    out = np.empty((NBATCH, L, D), np.float32)
    for c in range(NCORES):
        out[c // 2, 2048 * (c % 2):2048 * (c % 2 + 1), :] = np.asarray(rF[c]["hoT"], np.float32).T
    return out


S5_NAMES = ("s5col", "bT_re", "bT_im", "cT_re", "cT_im", "d_col")
S5_SHAPES = dict(s5col=[128, 3, 16], bT_re=[32, 16, 64], bT_im=[32, 16, 64], cT_re=[32, 64, 16], cT_im=[32, 64, 16], d_col=[128, 4])
LRU_SHAPES = dict(lru_col=[128, 4, 8], w_r=[8, 64, 64], w_i=[8, 64, 64])


def build_fused():
    Kb = KB()
    S = Kb.S
    xT = Kb.din("xT", [D, L], F32)
    outT = Kb.dout("outT", [D, L], F32)
    W = {}
    for l in range(2):
        W[l] = dict(
            g1=Kb.din("g1_%d" % l, [128, 16], F32), w_in=Kb.din("w_in_%d" % l, [D, 11264], F32),
            w_glu=Kb.din("w_glu_%d" % l, [1024, 1024], F32), b_glu=Kb.din("b_glu_%d" % l, [128, 8], F32),
            b_gate=Kb.din("b_gate_%d" % l, [128, 48], F32), w_br=Kb.din("w_br_%d" % l, [3, 1024, D], F32),
            w_out=Kb.din("w_out_%d" % l, [D, D], F32), g2=Kb.din("g2_%d" % l, [128, 16], F32),
            w_up=Kb.din("w_up_%d" % l, [D, DFF], F32), w_down=Kb.din("w_down_%d" % l, [DFF, D], F32))
        for s in range(2):
            for nm, shp in S5_SHAPES.items():
                W[l]["%s_%d" % (nm, s)] = Kb.din("%s_%d_%d" % (nm, l, s), shp, F32)
            for nm, shp in LRU_SHAPES.items():
                W[l]["%s_%d" % (nm, s)] = Kb.din("%s_%d_%d" % (nm, l, s), shp, F32)
    gf = Kb.din("gf", [128, 16], F32)
    sc = dict(
        xnT=Kb.dint("sc_xnT", [D, L], BF16), uS5T=Kb.dint("sc_uS5T", [1024, L], F32), uLruT=Kb.dint("sc_uLruT", [1024, L], F32),
        qT=Kb.dint("sc_qT", [1024, L], BF16), kT=Kb.dint("sc_kT", [1024, L], BF16), v=Kb.dint("sc_v", [L, 1024], BF16),
        ysgT=Kb.dint("sc_ysgT", [1024, L], F32), yLruT=Kb.dint("sc_yLruT", [1024, L], BF16), yAttT=Kb.dint("sc_yAttT", [1024, L], BF16),
        h1T=Kb.dint("sc_h1T", [D, TOK], F32), hidT=Kb.dint("sc_hidT", [DFF, TOK], BF16), hmid=Kb.dint("sc_hmid", [D, L], F32))
    m0 = Kb.mark()

    def sep():
        S.barrier()
        Kb.release(m0)

    h_in = xT
    for l in range(2):
        w = W[l]
        h_out = sc["hmid"] if l == 0 else outT
        for th in range(2):
            ts_ = slice(th * TOK, (th + 1) * TOK)
            stage_A(Kb, dict(hT=h_in[:, ts_], g1=w["g1"], w_in5=w["w_in"][:, 0:5120], xnT=sc["xnT"][:, ts_], uS5T=sc["uS5T"][:, ts_],
                             uLruT=sc["uLruT"][:, ts_], qT=sc["qT"][:, ts_], kT=sc["kT"][:, ts_], v=sc["v"][ts_, :]))
            sep()
        for s in range(2):
            ch = slice(512 * s, 512 * (s + 1))
            io = dict(uT=sc["uS5T"][ch, :], uLT=sc["uLruT"][ch, :], qT=sc["qT"][ch, :], kT=sc["kT"][ch, :], v=sc["v"][:, ch],
                      ysgT=sc["ysgT"][ch, :], yLruT=sc["yLruT"][ch, :], yAttT=sc["yAttT"][ch, :])
            for nm in S5_SHAPES:
                io[nm] = w["%s_%d" % (nm, s)]
            for nm in LRU_SHAPES:
                io[nm] = w["%s_%d" % (nm, s)]
            stage_B(Kb, io)
            sep()
        for th in range(2):
            ts_ = slice(th * TOK, (th + 1) * TOK)
            io = dict(hT=h_in[:, ts_], xnT=sc["xnT"][:, ts_], ysgT=sc["ysgT"][:, ts_], yLruT=sc["yLruT"][:, ts_], yAttT=sc["yAttT"][:, ts_],
                      w_glu=w["w_glu"], b_glu=w["b_glu"], w_gate=w["w_in"][:, 5120:11264], b_gate=w["b_gate"], w_br=w["w_br"], w_out=w["w_out"],
                      g2=w["g2"], w_up=w["w_up"], w_down=w["w_down"], h1T=sc["h1T"], hidT=sc["hidT"], hoT=h_out[:, ts_], gf=gf)
            stage_C(Kb, io, final_norm=(l == 1))
            sep()
        h_in = h_out
    S.emit()
    return Kb


def _fused_inputs(inp, b):
    m = dict(xT=np.ascontiguousarray(inp["x"][b].T), gf=_col(inp["final_norm_g"], 16))
    for l in range(2):
        m["g1_%d" % l] = _col(inp["norm_mix_g"][l], 16)
        m["w_in_%d" % l] = np.ascontiguousarray(inp["w_in"][l])
        m["w_glu_%d" % l] = np.ascontiguousarray(inp["s5_w_glu"][l])
        m["b_glu_%d" % l] = _col(inp["s5_b_glu"][l], 8)
        m["b_gate_%d" % l] = _col(inp["b_gate"][l], 48)
        m["w_br_%d" % l] = np.ascontiguousarray(np.stack([inp["w_br_s5"][l], inp["w_br_lru"][l], inp["w_br_attn"][l]]))
        m["w_out_%d" % l] = np.ascontiguousarray(inp["w_out"][l])
        m["g2_%d" % l] = _col(inp["norm_mlp_g"][l], 16)
        m["w_up_%d" % l] = np.ascontiguousarray(inp["w_up"][l])
        m["w_down_%d" % l] = np.ascontiguousarray(inp["w_down"][l])
        for s in range(2):
            for k_, v_ in s5_host_params(inp, l, s).items():
                m["%s_%d_%d" % (k_, l, s)] = v_
            for k_, v_ in lru_host_params(inp, l, s).items():
                m["%s_%d_%d" % (k_, l, s)] = v_
    return m


def kernel_fused(**inp):
    inp = {k: np.asarray(v) for k, v in inp.items()}
    if "F" not in _PROGS:
        _PROGS["F"] = build_fused()
    Kb = _PROGS["F"]
    per_b = [_fused_inputs(inp, b) for b in range(NBATCH)]
    in_maps = [per_b[c // 2] for c in range(NCORES)]
    res = run_bass_kernel_spmd(Kb.nc, in_maps, core_ids=list(range(NCORES)))
    out = np.empty((NBATCH, L, D), np.float32)
    for c in range(NCORES):
        b, s = c // 2, c % 2
        out[b, 2048 * s:2048 * (s + 1), :] = np.asarray(res.results[c]["outT"], np.float32)[:, 2048 * s:2048 * (s + 1)].T
    return out


FUSED = True


def kernel(**inp):
    return kernel_fused(**inp) if FUSED else kernel_unfused(**inp)
```
